# Optimizing a Trainium2 kernel written in Bass

```python
import jax, jax.numpy as jnp
from jax import lax
import numpy as np

D_MODEL = 1024
BATCH = 8
SEQ = 8192
DEPTH = 2
DEC_BATCH = 16
DEC_SEQ = 64
PAST_LEN = 2048

CHUNK = 64
Q_BLOCK = 128
D_MIX = D_MODEL
FOX_WIDTH = D_MIX // 2
FOX_HEAD_DIM = 64
FOX_HEADS = FOX_WIDTH // FOX_HEAD_DIM
HG_WIDTH = D_MIX - FOX_WIDTH
HG_HEAD_DIM = 128
HG_HEADS = HG_WIDTH // HG_HEAD_DIM
NORM_EPS = 1e-6
_SIZES = (FOX_WIDTH, FOX_WIDTH, FOX_WIDTH, FOX_HEADS, FOX_WIDTH, HG_WIDTH, HG_WIDTH, HG_WIDTH, HG_WIDTH)
D_IN = 4 * FOX_WIDTH + FOX_HEADS + 4 * HG_WIDTH
SPLIT_POINTS = tuple(int(v) for v in np.cumsum(_SIZES)[:-1])

kernel_name = "fox_hgrn2_parallel_stream_step"

F32 = jnp.float32


def rmsnorm(x, g):
    xf = x.astype(F32)
    y = xf * lax.rsqrt(jnp.mean(xf * xf, axis=-1, keepdims=True) + NORM_EPS)
    return (y * g.astype(F32)).astype(x.dtype)


def project(h, w_in, b_f, lb):
    B, S = h.shape[0], h.shape[1]
    z = jnp.einsum('bsd,de->bse', h, w_in)
    fq, fk, fv, ff, fg, hq, hf, hi, hg = jnp.split(z, SPLIT_POINTS, axis=-1)
    fq = fq.reshape(B, S, FOX_HEADS, FOX_HEAD_DIM)
    fk = fk.reshape(B, S, FOX_HEADS, FOX_HEAD_DIM)
    fv = fv.reshape(B, S, FOX_HEADS, FOX_HEAD_DIM)
    fox_logf = jax.nn.log_sigmoid(ff.astype(F32) + b_f.astype(F32))
    zf = hf.astype(F32).reshape(B, S, HG_HEADS, HG_HEAD_DIM)
    lbr = lb.reshape(HG_HEADS, HG_HEAD_DIM)
    hg_logf = jnp.logaddexp(jnp.log(lbr), jnp.log1p(-lbr) + jax.nn.log_sigmoid(zf))
    hg_k = (1.0 - lbr) * jax.nn.sigmoid(-zf)
    hq = jax.nn.silu(hq).reshape(B, S, HG_HEADS, HG_HEAD_DIM)
    hi = hi.reshape(B, S, HG_HEADS, HG_HEAD_DIM)
    return fq, fk, fv, fox_logf, fg, hq, hg_logf, hg_k, hi, hg


def fox_attend(q, k, v, c_q, c_k, q_pos, k_pos):
    s = jnp.einsum('bqhd,bkhd->bhqk', q.astype(F32), k.astype(F32)) * (FOX_HEAD_DIM ** -0.5)
    s = s + (jnp.swapaxes(c_q, 1, 2)[..., :, None] - jnp.swapaxes(c_k, 1, 2)[..., None, :])
    mask = k_pos[None, :] <= q_pos[:, None]
    s = jnp.where(mask, s, -jnp.inf)
    p = jax.nn.softmax(s, axis=-1)
    return jnp.einsum('bhqk,bkhd->bqhd', p.astype(v.dtype), v)


def fox_prompt(q, k, v, logf):
    B, S = q.shape[0], q.shape[1]
    c = jnp.cumsum(logf, axis=1)
    pos = jnp.arange(S)

    def block(i):
        st = i * Q_BLOCK
        qb = lax.dynamic_slice_in_dim(q, st, Q_BLOCK, axis=1)
        cb = lax.dynamic_slice_in_dim(c, st, Q_BLOCK, axis=1)
        return fox_attend(qb, k, v, cb, c, st + jnp.arange(Q_BLOCK), pos)

    out = lax.map(block, jnp.arange(S // Q_BLOCK))
    return jnp.swapaxes(out, 0, 1).reshape(B, S, FOX_WIDTH)


def fox_sample(q, k, v, logf, ck, cv, clogf):
    B, T = q.shape[0], q.shape[1]
    P = ck.shape[1]
    kk = jnp.concatenate([ck.astype(k.dtype), k], axis=1)
    vv = jnp.concatenate([cv.astype(v.dtype), v], axis=1)
    c = jnp.cumsum(jnp.concatenate([clogf.astype(F32), logf], axis=1), axis=1)
    out = fox_attend(q, kk, vv, c[:, P:], c, P + jnp.arange(T), jnp.arange(P + T))
    return out.reshape(B, T, FOX_WIDTH)


def hgrn_chunk(S0, q, logf, k, v):
    q, k, v = q.astype(F32), k.astype(F32), v.astype(F32)
    L = q.shape[1]
    b = jnp.cumsum(logf, axis=1)
    o_inter = jnp.einsum('blhk,bhkv->blhv', q * jnp.exp(b), S0)
    causal = jnp.tril(jnp.ones((L, L), dtype=bool))
    diff = b[:, :, None] - b[:, None, :]
    decay = jnp.exp(jnp.where(causal[None, :, :, None, None], diff, -jnp.inf))
    A = jnp.einsum('bthk,btshk,bshk->bhts', q, decay, k)
    o = o_inter + jnp.einsum('bhts,bshv->bthv', A, v)
    b_last = b[:, -1]
    S_new = jnp.exp(b_last)[..., None] * S0 + jnp.einsum(
        'bshk,bshv->bhkv', jnp.exp(b_last[:, None] - b) * k, v)
    return o, S_new


def hgrn_prompt(q, logf, k, v):
    B, S = q.shape[0], q.shape[1]
    n_chunks = S // CHUNK

    def to_chunks(a):
        return jnp.swapaxes(a.reshape(B, n_chunks, CHUNK, *a.shape[2:]), 0, 1)

    def step(state, inp):
        o, s_new = hgrn_chunk(state, *inp)
        return s_new, o

    S0 = jnp.zeros((B, HG_HEADS, HG_HEAD_DIM, HG_HEAD_DIM), F32)
    S_fin, o = lax.scan(step, S0, (to_chunks(q), to_chunks(logf), to_chunks(k), to_chunks(v)))
    return jnp.swapaxes(o, 0, 1).reshape(B, S, HG_HEADS, HG_HEAD_DIM), S_fin


def merge(fox_o, fox_gate, hg_o, hg_gate, g_norm, w_out):
    B, S = fox_gate.shape[0], fox_gate.shape[1]
    dt = fox_gate.dtype
    fox_y = fox_o.reshape(B, S, FOX_WIDTH) * jax.nn.silu(fox_gate)
    hn = hg_o * lax.rsqrt(jnp.mean(hg_o * hg_o, axis=-1, keepdims=True) + NORM_EPS)
    hn = hn * g_norm.astype(F32).reshape(HG_HEADS, HG_HEAD_DIM)
    hg_y = hn.reshape(B, S, HG_WIDTH).astype(dt) * jax.nn.silu(hg_gate)
    return jnp.einsum('bse,ed->bsd', jnp.concatenate([fox_y, hg_y], axis=-1), w_out)


def setup_inputs(seed: int = 0) -> dict:
    key = jax.random.key(seed)
    ks = jax.random.split(key, 14)
    nrm = jax.random.normal
    return {
        "x_prompt": nrm(ks[0], (BATCH, SEQ, D_MODEL), F32),
        "x_sample": nrm(ks[1], (DEC_BATCH, DEC_SEQ, D_MODEL), F32),
        "cache_k": nrm(ks[2], (DEPTH, DEC_BATCH, PAST_LEN, FOX_HEADS, FOX_HEAD_DIM), F32),
        "cache_v": nrm(ks[3], (DEPTH, DEC_BATCH, PAST_LEN, FOX_HEADS, FOX_HEAD_DIM), F32),
        "cache_logf": jax.nn.log_sigmoid(2.0 + nrm(ks[4], (DEPTH, DEC_BATCH, PAST_LEN, FOX_HEADS), F32)),
        "state_hgrn": 0.5 * nrm(ks[5], (DEPTH, DEC_BATCH, HG_HEADS, HG_HEAD_DIM, HG_HEAD_DIM), F32),
        "norm_g": 1.0 + 0.01 * nrm(ks[6], (DEPTH, D_MODEL), F32),
        "w_in": nrm(ks[7], (DEPTH, D_MODEL, D_IN), F32) * (D_MODEL ** -0.5),
        "fox_b_f": 0.1 * nrm(ks[8], (DEPTH, FOX_HEADS), F32),
        "hg_lower": nrm(ks[9], (DEPTH, HG_WIDTH), F32),
        "hg_norm_g": 1.0 + 0.01 * nrm(ks[10], (DEPTH, HG_WIDTH), F32),
        "w_out": nrm(ks[11], (DEPTH, D_MIX, D_MODEL), F32) * (D_MIX ** -0.5),
        "final_g": 1.0 + 0.01 * nrm(ks[12], (D_MODEL,), F32),
    }


def reference(x_prompt, x_sample, cache_k, cache_v, cache_logf, state_hgrn,
              norm_g, w_in, fox_b_f, hg_lower, hg_norm_g, w_out, final_g):
    lb_all = jnp.cumsum(jax.nn.softmax(hg_lower.astype(F32), axis=0), axis=0)

    xp = x_prompt
    kp, vp, lfp, sp = [], [], [], []
    for l in range(DEPTH):
        lb = lb_all[l] - lb_all[0]
        h = rmsnorm(xp, norm_g[l])
        fq, fk, fv, flogf, fg, hq, hlogf, hk, hi, hg = project(h, w_in[l], fox_b_f[l], lb)
        fox_o = fox_prompt(fq, fk, fv, flogf)
        hg_o, s_fin = hgrn_prompt(hq, hlogf, hk, hi)
        xp = xp + merge(fox_o, fg, hg_o, hg, hg_norm_g[l], w_out[l])
        kp.append(fk); vp.append(fv); lfp.append(flogf); sp.append(s_fin)
    y_prompt = rmsnorm(xp, final_g)

    xs = x_sample
    ksm, vsm, lfs, ss = [], [], [], []
    for l in range(DEPTH):
        lb = lb_all[l] - lb_all[0]
        h = rmsnorm(xs, norm_g[l])
        fq, fk, fv, flogf, fg, hq, hlogf, hk, hi, hg = project(h, w_in[l], fox_b_f[l], lb)
        fox_o = fox_sample(fq, fk, fv, flogf, cache_k[l], cache_v[l], cache_logf[l])
        hg_o, s_new = hgrn_chunk(state_hgrn[l].astype(F32), hq, hlogf, hk, hi)
        xs = xs + merge(fox_o, fg, hg_o, hg, hg_norm_g[l], w_out[l])
        ksm.append(fk); vsm.append(fv); lfs.append(flogf); ss.append(s_new)
    y_sample = rmsnorm(xs, final_g)

    return (y_prompt, y_sample,
            jnp.stack(kp), jnp.stack(vp), jnp.stack(lfp), jnp.stack(sp),
            jnp.stack(ksm), jnp.stack(vsm), jnp.stack(lfs), jnp.stack(ss))
```

```python
import numpy as np
from contextlib import ExitStack
import concourse.bass as bass
import concourse.mybir as mybir
from concourse.bass_utils import run_bass_kernel_spmd

F32 = mybir.dt.float32
BF16 = mybir.dt.bfloat16
AF = mybir.ActivationFunctionType
ALU = mybir.AluOpType

D = 1024
DIN = 4104
H = 8
HD = 64
G = 4
GD = 128
EPS = 1e-6
NEG = -30000.0


class Sched:
    ENGS = ("pe", "act", "dve", "pool", "sp")
    EPOCH = 12000
    NDMA = 14

    def __init__(self, nc, es):
        self.nc = nc
        self.es = es
        self.streams = {e: [] for e in self.ENGS}
        self.n = {e: 0 for e in self.ENGS}
        self.sems = {}
        self.known = {}
        self.lastw = {}
        self.readers = {}
        self.pending = {e: [] for e in self.ENGS}
        self.last_tok = {}
        self.dma_last = {}
        self.ndma = {}
        self.lazy = set()

    def sem(self, key):
        if key not in self.sems:
            nm = "s_" + "_".join(str(k) for k in key)
            self.sems[key] = self.es.enter_context(self.nc.semaphore(nm))
        return self.sems[key]

    def op(self, eng, fn, r=(), w=(), dma=False, lazy=False):
        waits = {}
        me = eng + "_dma" if dma else eng

        def need(tok):
            if tok is None:
                return
            teng, key, val = tok
            if teng == eng == "pe":
                return
            if self.known.get((eng, key), 0) >= val:
                return
            if waits.get(key, 0) < val:
                waits[key] = val

        for tok in self.pending[eng]:
            need(tok)
        self.pending[eng] = []
        for k in r:
            need(self.lastw.get(k))
            if isinstance(k, tuple) and k[0] in ("ps", "ps2"):
                for t in self.readers.get(k, ()):
                    if t[0] != me:
                        need(t)
        for k in w:
            need(self.lastw.get(k))
            for t in self.readers.get(k, ()):
                if t[0] != me or dma:
                    need(t)
        if dma:
            i = self.ndma.get(eng, 0)
            self.ndma[eng] = i + 1
            slot = i % self.NDMA
            key = (me, slot)
            val = 16 * (i // self.NDMA + 1)
            if val > 16 and self.known.get((eng, key), 0) < val - 16:
                waits[key] = max(waits.get(key, 0), val - 16)
            inc = 16
            self.dma_last[key] = val
            if lazy:
                self.lazy.add((key, val))
        else:
            i = self.n[eng]
            self.n[eng] += 1
            key = (eng, i // self.EPOCH)
            val = i % self.EPOCH + 1
            inc = 1
            self.last_tok[eng] = (eng, key, val)
        self.sem(key)
        for k2 in waits:
            self.sem(k2)
        tok = (me, key, val)
        for k2, v2 in waits.items():
            self.known[(eng, k2)] = v2
        self.streams[eng].append((list(waits.items()), fn, key, inc))
        for k in r:
            lst = self.readers.setdefault(k, [])
            if not dma:
                lst[:] = [t for t in lst if t[0] != me]
            lst.append(tok)
        for k in w:
            self.lastw[k] = tok
            self.readers[k] = []
        return tok

    def barrier(self):
        toks = []
        for e in ("pe", "act", "dve", "pool"):
            if e in self.last_tok:
                toks.append(self.last_tok[e])
        for key, val in self.dma_last.items():
            if (key, val) in self.lazy:
                continue
            toks.append((key[0], key, val))
        for e in self.ENGS:
            self.pending[e] = [t for t in toks if t[0] != e]

    def emit(self):
        nc = self.nc
        streams = self.streams
        sems = self.sems
        dma_last = dict(self.dma_last)

        def mk(name):
            def body(eng):
                for waits, fn, key, inc in streams[name]:
                    for k, v in waits:
                        eng.wait_ge(sems[k], v)
                    ins = fn(eng)
                    ins.then_inc(sems[key], inc)
                if name == "sp":
                    for key, val in dma_last.items():
                        eng.wait_ge(sems[key], val)
            return body

        with nc.Block() as block:
            block.tensor(mk("pe"))
            block.scalar(mk("act"))
            block.vector(mk("dve"))
            block.gpsimd(mk("pool"))
            block.sync(mk("sp"))


class Arena:
    def __init__(self, nc, es, words):
        self.t = es.enter_context(nc.sbuf_tensor("arena", [128, words], F32))
        self.words = words
        self.off = 0
        self.marks = []

    def alloc(self, free_shape, dt):
        n = int(np.prod(free_shape))
        bpe = 4 if dt == F32 else 2
        w = (n * bpe + 3) // 4
        w = (w + 7) // 8 * 8
        assert self.off + w <= self.words, ("arena overflow", self.off, w, self.words)
        ap = self.t[:, self.off:self.off + w]
        self.off += w
        if dt != F32:
            ap = ap.bitcast(dt)
        ap = ap[:, 0:n]
        if len(free_shape) == 2:
            ap = ap.rearrange("p (a b) -> p a b", a=free_shape[0])
        elif len(free_shape) == 3:
            ap = ap.rearrange("p (a b c) -> p a b c", a=free_shape[0], b=free_shape[1])
        return ap

    def mark(self):
        self.marks.append(self.off)

    def release(self):
        self.off = self.marks.pop()


def build(cfg):
    SEQ = cfg["SEQ"]
    PAST = cfg["PAST"]
    TS = 64
    NS = 2
    TSAMP = NS * TS
    nc = bass.Bass("TRN2", target_bir_lowering=False)

    def din(name, shape):
        return nc.dram_tensor(name, list(shape), F32, kind="ExternalInput").ap()

    def dout(name, shape):
        return nc.dram_tensor(name, list(shape), F32, kind="ExternalOutput").ap()

    def dscr(name, shape, dt):
        if cfg.get("debug"):
            return nc.dram_tensor(name, list(shape), dt, kind="ExternalOutput").ap()
        return nc.dram_tensor(name, list(shape), dt).ap()

    xp = din("xp", [SEQ, D])
    xs = din("xs", [TSAMP, D])
    ck = din("ck", [2, NS, PAST, H * HD])
    cv = din("cv", [2, NS, PAST, H * HD])
    clf = din("clf", [2, NS, PAST, H])
    stin = din("st", [2, NS, G, GD, GD])
    norm_g = din("norm_g", [2, D])
    w_in = din("w_in", [2, D, DIN])
    fox_b_f = din("fox_b_f", [2, H])
    hg_lower = din("hg_lower", [2, G * GD])
    hg_norm_g = din("hg_norm_g", [2, G * GD])
    w_out = din("w_out", [2, D, D])
    final_g = din("final_g", [D])
    c_ident = din("c_ident", [128, 128])
    c_maskneg = din("c_maskneg", [128, 128])
    c_hmask = din("c_hmask", [128, 128])

    yp = dout("yp", [SEQ, D])
    ys = dout("ys", [TSAMP, D])
    kp = dout("kp", [2, SEQ, H * HD])
    vp = dout("vp", [2, SEQ, H * HD])
    lfp = dout("lfp", [2, SEQ, H])
    spo = dout("sp", [2, G, GD, GD])
    ks = dout("ks", [2, TSAMP, H * HD])
    vs = dout("vs", [2, TSAMP, H * HD])
    lfs = dout("lfs", [2, TSAMP, H])
    sso = dout("ss", [2, NS, G, GD, GD])

    def mkstream(nm, T, NB):
        return dict(
            name=nm, T=T, NB=NB, nblk=T // NB,
            x1=dscr(nm + "_x1", [T, D], F32),
            QT=dscr(nm + "_QT", [H, HD + 2, T], BF16),
            KT=dscr(nm + "_KT", [H, HD, T], BF16),
            Vb=dscr(nm + "_Vb", [T, H * HD], BF16),
            GT=dscr(nm + "_GT", [H * HD, T], BF16),
            YT=dscr(nm + "_YT", [D, T], BF16),
            HT=dscr(nm + "_HT", [D, T], BF16),
        )

    P = mkstream("p", SEQ, 512)
    Sm = mkstream("s", TSAMP, 128)
    P.update(x0=xp, y=yp, kout=kp, vout=vp, lfout=lfp, past=0)
    Sm.update(x0=xs, y=ys, kout=ks, vout=vs, lfout=lfs, past=PAST)
    STREAMS = (P, Sm)

    es = ExitStack()
    with es:
        S = Sched(nc, es)
        PB2 = [es.enter_context(nc.psum_tensor("pb%d" % i, [128, 1024], F32)) for i in range(4)]
        PB = []
        for i in range(4):
            PB.append(PB2[i][:, 0:512])
            PB.append(PB2[i][:, 512:1024])
        PBh = [pb.bitcast(BF16) for pb in PB]
        cst = es.enter_context(nc.sbuf_tensor("cst", [128, 2048], F32))
        A = Arena(nc, es, 47000)

        LQ = cfg.get("loadq", "pool")

        def DMA(out, in_, r=(), w=(), slow=False, q="sp", lazy=False):
            if slow:
                S.op(q, lambda e: e.dma_start(out=out, in_=in_, allow_slow_non_contiguous=True), r, w, dma=True,
                     lazy=lazy)
            else:
                S.op(q, lambda e: e.dma_start(out=out, in_=in_), r, w, dma=True, lazy=lazy)

        def MM(out, lhsT, rhs, start, stop, r=(), w=(), skip=False):
            if skip:
                S.op("pe", lambda e: e.matmul(out, lhsT, rhs, start=start, stop=stop, skip_group_check=True), r, w)
            else:
                S.op("pe", lambda e: e.matmul(out, lhsT, rhs, start=start, stop=stop), r, w)

        def TR(out, in_, ident, r=(), w=()):
            S.op("pe", lambda e: e.transpose(out, in_, ident), r, w)

        def ACT(out, in_, func, bias=None, scale=None, accum=None, r=(), w=()):
            kw = {}
            if bias is not None:
                kw["bias"] = bias
            if scale is not None:
                kw["scale"] = scale
            if accum is not None:
                kw["accum_out"] = accum
            S.op("act", lambda e: e.activation(out=out, in_=in_, func=func, **kw), r, w)

        def TS_(eng, out, in0, s1, s2, op0, op1=None, r=(), w=()):
            if op1 is None:
                S.op(eng, lambda e: e.tensor_scalar(out=out, in0=in0, scalar1=s1, scalar2=None, op0=op0), r, w)
            else:
                S.op(eng, lambda e: e.tensor_scalar(out=out, in0=in0, scalar1=s1, scalar2=s2, op0=op0, op1=op1), r, w)

        def TT(eng, out, in0, in1, op, r=(), w=()):
            S.op(eng, lambda e: e.tensor_tensor(out=out, in0=in0, in1=in1, op=op), r, w)

        def STT(out, in0, scalar, in1, op0, op1, r=(), w=()):
            S.op("dve", lambda e: e.scalar_tensor_tensor(out=out, in0=in0, scalar=scalar, in1=in1, op0=op0, op1=op1), r, w)

        def CP(eng, out, in_, r=(), w=()):
            if eng == "act":
                S.op("act", lambda e: e.copy(out=out, in_=in_), r, w)
            else:
                S.op(eng, lambda e: e.tensor_copy(out=out, in_=in_), r, w)

        def MSET(eng, ap, val, w=()):
            S.op(eng, lambda e: e.memset(ap, val), (), w)

        def RECIP(out, in_, r=(), w=()):
            S.op("dve", lambda e: e.reciprocal(out=out, in_=in_), r, w)

        def SCAN(out, d0, d1, init, op0, op1, r=(), w=()):
            S.op("dve", lambda e: e.tensor_tensor_scan(out=out, data0=d0, data1=d1, initial=init, op0=op0, op1=op1), r, w)

        ident_f = cst[:, 0:128]
        cbf = cst[:, 128:128 + 256].bitcast(BF16)
        ident_b = cbf[:, 0:128]
        maskneg = cbf[:, 128:256]
        hmask = cbf[:, 256:384]
        ones_b = cbf[:, 384:512]
        ctmp = cst[:, 384:768]
        vec = cst[:, 768:1024]
        fgb = cst[:, 1024:2048]
        zero_c = vec[:, 200:201]
        one_c = vec[:, 201:202]

        DMA(ident_f, c_ident, w=["ident_f"])
        DMA(ctmp[:, 0:128], c_maskneg, w=["ctmp0"])
        DMA(ctmp[:, 128:256], c_hmask, w=["ctmp1"])
        CP("dve", ident_b, ident_f, r=["ident_f"], w=["ident_b"])
        CP("dve", maskneg, ctmp[:, 0:128], r=["ctmp0"], w=["maskneg"])
        CP("dve", hmask, ctmp[:, 128:256], r=["ctmp1"], w=["hmask"])
        MSET("dve", ones_b, 1.0, w=["ones_b"])
        ones_f = es.enter_context(nc.sbuf_tensor("ones_f", [128, 64], F32))
        MSET("dve", ones_f[:, :], 1.0, w=["ones_f"])
        MSET("dve", zero_c, 0.0, w=["zc"])
        MSET("dve", one_c, 1.0, w=["oc"])
        eps_c = vec[:, 202:203]
        MSET("dve", eps_c, EPS, w=["epsc"])
        DMA(fgb, final_g.partition_broadcast(128), w=["fgb"])
        CK = ["ident_f", "ident_b", "maskneg", "hmask", "ones_b", "zc", "oc", "fgb"]

        def vslot(i, n):
            return vec[:, i:i + n]
        gcol = [vslot(0, 8), vslot(8, 8)]
        hl = [vslot(16, 4), vslot(20, 4)]
        lbv = [vslot(24, 4), vslot(28, 4)]
        oml = [vslot(32, 4), vslot(36, 4)]
        noml = [vslot(40, 4), vslot(44, 4)]
        gnv = [vslot(48, 4), vslot(52, 4)]
        bfc = [vslot(56, 1), vslot(57, 1)]
        nbfc = [vslot(58, 1), vslot(59, 1)]
        for l in range(2):
            DMA(gcol[l], norm_g[l].rearrange("(a p) -> p a", p=128), w=["gcol%d" % l], slow=True)
            DMA(hl[l], hg_lower[l].rearrange("(g p) -> p g", p=128), w=["hl%d" % l], slow=True)
            DMA(gnv[l], hg_norm_g[l].rearrange("(g p) -> p g", p=128), w=["gn%d" % l], slow=True)
            DMA(bfc[l][0:8, :], fox_b_f[l].rearrange("(h o) -> h o", o=1), w=["bf%d" % l], slow=True)
        for l in range(2):
            TS_("dve", nbfc[l][0:8, :], bfc[l][0:8, :], -1.0, None, ALU.mult, r=["bf%d" % l], w=["nbf%d" % l])
        MSET("dve", lbv[0], 0.0, w=["lb0"])
        TT("dve", lbv[1], hl[1], hl[0], ALU.subtract, r=["hl0", "hl1"], w=["lb1"])
        ACT(lbv[1], lbv[1], AF.Sigmoid, r=["lb1"], w=["lb1"])
        for l in range(2):
            TS_("dve", oml[l], lbv[l], -1.0, 1.0, ALU.mult, ALU.add, r=["lb%d" % l], w=["oml%d" % l])
            TS_("dve", noml[l], oml[l], -1.0, None, ALU.mult, r=["oml%d" % l], w=["noml%d" % l])
        VK = lambda l: ["gcol%d" % l, "lb%d" % l, "oml%d" % l, "noml%d" % l, "gn%d" % l, "bf%d" % l, "nbf%d" % l]

        psn = [0]

        def bank(lo, hi):
            b = lo + psn[0] % (hi - lo)
            psn[0] += 1
            return b

        def load_w(l, wsrc, c0, c1, Wb, key):
            ncol = c1 - c0
            stg = [A.alloc([ncol], F32) for _ in range(2)]
            src = wsrc.rearrange("(a p) e -> p a e", p=128)
            for dt_ in range(8):
                sb = stg[dt_ % 2]
                sk = "wstg%d" % (dt_ % 2)
                DMA(sb, src[:, dt_, c0:c1], w=[sk], q=("sp" if dt_ % 2 else LQ))
                if key == "wout":
                    CP("act" if dt_ % 2 else "dve", Wb[:, dt_, :], sb, r=[sk], w=[key])
                elif dt_ % 2:
                    gc = gcol[l][:, dt_:dt_ + 1]
                    S.op("act", lambda e, o=Wb[:, dt_, :], i_=sb, g_=gc: e.mul(out=o, in_=i_, mul=g_),
                         [sk, "gcol%d" % l], [key])
                else:
                    TS_("dve", Wb[:, dt_, :], sb, gcol[l][:, dt_:dt_ + 1], None,
                        ALU.mult, r=[sk, "gcol%d" % l], w=[key])

        def phase_A1(l):
            A.mark()
            lfT = {st["name"]: A.alloc([st["T"]], F32) for st in STREAMS}
            A.mark()
            NC1 = 2056
            Wb = A.alloc([8, NC1], BF16)
            load_w(l, w_in[l], 0, NC1, Wb, "W1")
            NXB = 6
            xbuf = [A.alloc([D], F32) for _ in range(NXB)]
            junk = A.alloc([D], F32)
            ssb = A.alloc([8], F32)
            hb = [A.alloc([D], BF16) for _ in range(2)]
            hT = [A.alloc([8, 512], BF16) for _ in range(2)]
            evb = [A.alloc([512], BF16) for _ in range(4)]
            evf = [A.alloc([512], F32) for _ in range(4)]
            sgt = [A.alloc([512], F32) for _ in range(2)]
            sgz = [A.alloc([512], F32) for _ in range(2)]
            zcb = [A.alloc([512], F32) for _ in range(2)]
            cn = {"x": 0, "e": 0, "f": 0, "s": 0}
            jobs = [(st, b) for st in STREAMS for b in range(st["nblk"])]

            hb4 = [A.alloc([D], BF16) for _ in range(4)]

            def norm_a(ji):
                st, b = jobs[ji]
                NB = st["NB"]
                xsrc = st["x0"] if l == 0 else st["x1"]
                for tt in range(NB // 128):
                    t0 = b * NB + tt * 128
                    xi = cn["x"]
                    cn["x"] += 1
                    xb = xbuf[xi % NXB]
                    xk = "xbuf%d" % (xi % NXB)
                    hbb = hb4[tt]
                    hbk = "hb4_%d" % tt
                    sc = ssb[:, (xi % 2) * 4:(xi % 2) * 4 + 1]
                    sck = "ss%d" % (xi % 2)
                    DMA(xb, xsrc[t0:t0 + 128, :], w=[xk], q=LQ)
                    ACT(junk, xb, AF.Square, accum=sc, r=[xk], w=["junk", sck])
                    ACT(sc, sc, AF.Ln, bias=eps_c, scale=1.0 / D, r=[sck, "epsc"], w=[sck])
                    ACT(sc, sc, AF.Exp, scale=-0.5, r=[sck], w=[sck])
                    TS_("dve", hbb, xb, sc, None, ALU.mult, r=[xk, sck], w=[hbk])

            def TRs(ji):
                st, b = jobs[ji]
                NB = st["NB"]
                hk_ = "hT%d" % (ji % 2)
                hTb = hT[ji % 2]
                for tt in range(NB // 128):
                    hbb = hb4[tt]
                    hbk = "hb4_%d" % tt
                    tb = bank(0, 2)
                    for dt_ in range(8):
                        TR(PBh[tb][:, dt_ * 128:(dt_ + 1) * 128], hbb[:, dt_ * 128:(dt_ + 1) * 128], ident_b,
                           r=[hbk, "ident_b"], w=[("ps", tb)])
                    CP("act" if tt % 2 else "dve", hTb[:, :, tt * 128:(tt + 1) * 128],
                       PBh[tb][:, 0:1024].rearrange("p (a b) -> p a b", a=8),
                       r=[("ps", tb)], w=[hk_])
                bsl = slice(b * NB, (b + 1) * NB)
                DMA(st["HT"].rearrange("(a p) t -> p a t", p=128)[:, :, bsl], hTb[:, :, 0:NB], r=[hk_])

            def projs(ji, part):
                st, b = jobs[ji]
                NB, nm = st["NB"], st["name"]
                ntt = NB // 128
                hk_ = "hT%d" % (ji % 2)
                hTb = hT[ji % 2]
                bsl = slice(b * NB, (b + 1) * NB)
                for grp, c0, M in (([("g", 1544 + j * 128, 128) for j in range(4)] +
                                    [("f", 1536, 8)] +
                                    [("q", j * 128, 128) for j in range(4)] +
                                    [("k", 512 + j * 128, 128) for j in range(4)]) if part == "fm" else []):
                    pb = bank(2, 5)
                    for dt_ in range(8):
                        MM(PB[pb][0:M, 0:NB], Wb[:, dt_, c0:c0 + M], hTb[:, dt_, 0:NB], dt_ == 0, dt_ == 7,
                           r=["W1", hk_], w=[("ps", pb)])
                    if grp == "f":
                        sg = sgt[0]
                        ACT(sg[0:8, 0:NB], PB[pb][0:8, 0:NB], AF.Exp, bias=nbfc[l][0:8, :], scale=-1.0,
                            r=[("ps", pb), "nbf%d" % l], w=["sgt0"])
                        ACT(sg[0:8, 0:NB], sg[0:8, 0:NB], AF.Ln, bias=one_c[0:8, :], r=["sgt0", "oc"], w=["sgt0"])
                        TS_("dve", lfT[nm][0:8, bsl], sg[0:8, 0:NB], -1.0, None, ALU.mult, r=["sgt0"], w=["lfT" + nm])
                        continue
                    ev = evb[cn["e"] % 4]
                    ek = "evb%d" % (cn["e"] % 4)
                    cn["e"] += 1
                    j = (c0 % 512) // 128 if grp != "g" else (c0 - 1544) // 128
                    if grp == "q":
                        TS_("dve", ev[:, 0:NB], PB[pb][:, 0:NB], 0.125, None, ALU.mult, r=[("ps", pb)], w=[ek])
                        DMA(st["QT"][2 * j, 0:64, bsl], ev[0:64, 0:NB], r=[ek])
                        DMA(st["QT"][2 * j + 1, 0:64, bsl], ev[64:128, 0:NB], r=[ek])
                    elif grp == "k":
                        CP("act", ev[:, 0:NB], PB[pb][:, 0:NB], r=[("ps", pb)], w=[ek])
                        DMA(st["KT"][2 * j, :, bsl], ev[0:64, 0:NB], r=[ek])
                        DMA(st["KT"][2 * j + 1, :, bsl], ev[64:128, 0:NB], r=[ek])
                    else:
                        gi = cn["s"] % 2
                        cn["s"] += 1
                        sg = sgz[gi]
                        zc = zcb[gi]
                        sgk, zck = "sgz%d" % gi, "zcb%d" % gi
                        CP("dve", zc[:, 0:NB], PB[pb][:, 0:NB], r=[("ps", pb)], w=[zck])
                        ACT(sg[:, 0:NB], zc[:, 0:NB], AF.Exp, scale=-1.0, r=[zck], w=[sgk])
                        ACT(sg[:, 0:NB], sg[:, 0:NB], AF.Ln, bias=one_c, r=[sgk, "oc"], w=[sgk])
                        ACT(sg[:, 0:NB], sg[:, 0:NB], AF.Exp, scale=-1.0, r=[sgk], w=[sgk])
                        TT("dve", ev[:, 0:NB], zc[:, 0:NB], sg[:, 0:NB], ALU.mult, r=[zck, sgk], w=[ek])
                        DMA(st["GT"][j * 128:(j + 1) * 128, bsl], ev[:, 0:NB], r=[ek])
                for tt in (range(ntt) if part == "tm" else []):
                    t0 = b * NB + tt * 128
                    for which, c0 in (("k", 512), ("v", 1024)):
                        pb = bank(5, 8)
                        for dt_ in range(8):
                            MM(PB[pb][:, :], hTb[:, dt_, tt * 128:(tt + 1) * 128], Wb[:, dt_, c0:c0 + 512],
                               dt_ == 0, dt_ == 7, r=["W1", hk_], w=[("ps", pb)])
                        ef = evf[cn["f"] % 4]
                        efk = "evf%d" % (cn["f"] % 4)
                        cn["f"] += 1
                        if which == "k":
                            CP("dve", ef, PB[pb][:, :], r=[("ps", pb)], w=[efk])
                            DMA(st["kout"][l, t0:t0 + 128, :], ef, r=[efk])
                        else:
                            CP("act", ef, PB[pb][:, :], r=[("ps", pb)], w=[efk])
                            DMA(st["vout"][l, t0:t0 + 128, :], ef, r=[efk])
                            ev = evb[cn["e"] % 4]
                            ek = "evb%d" % (cn["e"] % 4)
                            cn["e"] += 1
                            CP("pool", ev, ef, r=[efk], w=[ek])
                            DMA(st["Vb"][t0:t0 + 128, :], ev, r=[ek])

            norm_a(0)
            TRs(0)
            for ji in range(len(jobs)):
                if ji + 1 < len(jobs):
                    norm_a(ji + 1)
                projs(ji, "fm")
                if ji + 1 < len(jobs):
                    TRs(ji + 1)
                projs(ji, "tm")
            A.release()
            return lfT

        def phase_B(l, lfT, negc):
            A.mark()
            lfulls = [A.alloc([PAST + TS], F32) for _ in range(NS)]
            for s_ in range(NS):
                DMA(lfulls[s_][0:8, 0:PAST], clf[l, s_].rearrange("t h -> h t"), w=["lfull%d" % s_], slow=True)
            for st in STREAMS:
                nm, T, past = st["name"], st["T"], st["past"]
                nseq = 1 if past == 0 else NS
                Tq = T // nseq
                L = past + Tq
                lfull = None
                cT = A.alloc([L], F32)
                CH = min(2048, Tq)
                hi = A.alloc([CH], BF16)
                hif = A.alloc([CH], F32)
                lo = A.alloc([CH], BF16)
                ntile = (L + 127) // 128
                tok = tokp if past == 0 else A.alloc([ntile * 8], F32)
                for s in range(nseq):
                    if past:
                        lfull = lfulls[s]
                        CP("act", lfull[0:8, past:L], lfT[nm][0:8, s * Tq:(s + 1) * Tq], r=["lfT" + nm],
                           w=["lfull%d" % s])
                        src, srck = lfull, "lfull%d" % s
                    else:
                        src, srck = lfT[nm], "lfT" + nm
                    SCAN(cT[0:8, :], src[0:8, 0:L], zero_c[0:8, :].to_broadcast([8, L]), 0.0, ALU.add, ALU.add,
                         r=[srck, "zc"], w=["cT"])
                    for c0 in range(0, Tq, CH):
                        sl = slice(past + c0, past + c0 + CH)
                        CP("dve", hi[0:8, :], cT[0:8, sl], r=["cT"], w=["chi"])
                        CP("dve", hif[0:8, :], hi[0:8, :], r=["chi"], w=["chif"])
                        TT("dve", lo[0:8, :], cT[0:8, sl], hif[0:8, :], ALU.subtract, r=["cT", "chif"], w=["clo"])
                        dsl = slice(s * Tq + c0, s * Tq + c0 + CH)
                        DMA(st["QT"][:, 64, dsl], hi[0:8, :], r=["chi"])
                        DMA(st["QT"][:, 65, dsl], lo[0:8, :], r=["clo"])
                    pb = bank(0, 2)
                    for tI in range(ntile):
                        n = min(128, L - tI * 128)
                        TR(PB[pb][0:n, tI * 8:(tI + 1) * 8], cT[0:8, tI * 128:tI * 128 + n], ident_f[0:8, 0:8],
                           r=["cT", "ident_f"], w=[("ps", pb)])
                    ng = negc[nm][s]
                    TS_("dve", ng[:, 0:ntile * 8], PB[pb][:, 0:ntile * 8], -1.0, None, ALU.mult,
                        r=[("ps", pb)], w=["negc%s%d" % (nm, s)])
                    pb = bank(0, 2)
                    ntq = (Tq + 127) // 128
                    for tI in range(ntq):
                        n = min(128, Tq - tI * 128)
                        TR(PB[pb][0:n, tI * 8:(tI + 1) * 8], src[0:8, past + tI * 128:past + tI * 128 + n],
                           ident_f[0:8, 0:8], r=[srck, "ident_f"], w=[("ps", pb)])
                    CP("act", tok[:, 0:ntq * 8], PB[pb][:, 0:ntq * 8], r=[("ps", pb)], w=["tok"])
                    if past == 0:
                        DMA(st["lfout"][l].rearrange("(a p) h -> p a h", p=128),
                            tok[:, 0:ntq * 8].rearrange("p (a h) -> p a h", h=8), r=["tok"], slow=True, lazy=True)
                    else:
                        DMA(st["lfout"][l, s * Tq:(s + 1) * Tq, :], tok[0:Tq, 0:8], r=["tok"], slow=True)
            A.release()

        def phase_A2(l):
            A.mark()
            C0 = 2056
            NC2 = 2048
            Wb = A.alloc([8, NC2], BF16)
            load_w(l, w_in[l], C0, C0 + NC2, Wb, "W2")
            hT = [A.alloc([8, 512], BF16) for _ in range(2)]
            hib = [A.alloc([4, 512], BF16) for _ in range(2)]
            Sst2 = [A.alloc([G, 8, GD], F32) for _ in range(2)]
            Sbf = [A.alloc([8, GD], BF16) for _ in range(2)]
            stl = A.alloc([G, GD], F32)
            ez = [A.alloc([512], F32) for _ in range(3)]
            zq = A.alloc([512], F32)
            zg = A.alloc([512], F32)
            sig = A.alloc([512], F32)
            lf = A.alloc([512], F32)
            bcs = A.alloc([512], F32)
            hkk = A.alloc([512], F32)
            eb = A.alloc([512], F32)
            enb = A.alloc([512], F32)
            t1 = A.alloc([512], F32)
            qt = [A.alloc([512], BF16) for _ in range(2)]
            kt = [A.alloc([512], BF16) for _ in range(2)]
            kh = [A.alloc([512], BF16) for _ in range(2)]
            dch = [A.alloc([8], F32) for _ in range(2)]
            gs = [A.alloc([512], F32) for _ in range(2)]
            khT = A.alloc([4, 128], BF16)
            Am = A.alloc([4, 128], BF16)
            sq = A.alloc([512], BF16)
            rst = A.alloc([512], F32)
            t2 = A.alloc([512], F32)
            yb = [A.alloc([512], BF16) for _ in range(2)]
            cmask = A.alloc([512], F32)
            MSET("pool", cmask, 1.0, w=["cmask"])
            MSET("pool", cmask.rearrange("p (c t) -> p c t", t=64)[:, :, 0:1], 0.0, w=["cmask"])
            cn = {"y": 0}
            lbk = VK(l)
            for st in STREAMS:
                T, NB, nm, past = st["T"], st["NB"], st["name"], st["past"]
                ntt = NB // 128
                nch = NB // 64
                nblk = st["nblk"]
                if past == 0:
                    MSET("pool", Sst2[0][:, :, 0, :], 0.0, w=["S%d_0_0" % g for g in range(G)])
                HTv = st["HT"].rearrange("(a p) t -> p a t", p=128)
                DMA(hT[0][:, :, 0:NB], HTv[:, :, 0:NB], w=["hT0"])

                def pre_block(b):
                    hk_ = "hT%d" % (b % 2)
                    hTb = hT[b % 2]
                    hibb = hib[b % 2]
                    hibk = "hib%d" % (b % 2)
                    if b + 1 < nblk:
                        DMA(hT[(b + 1) % 2][:, :, 0:NB], HTv[:, :, (b + 1) * NB:(b + 2) * NB], w=["hT%d" % ((b + 1) % 2)])
                    for tt in range(ntt):
                        pb = bank(0, 2)
                        for dt_ in range(8):
                            MM(PB[pb][:, :], hTb[:, dt_, tt * 128:(tt + 1) * 128], Wb[:, dt_, 1024:1536],
                               dt_ == 0, dt_ == 7, r=["W2", hk_], w=[("ps", pb)])
                        CP("act", hibb[:, tt, :], PB[pb][:, :], r=[("ps", pb)], w=[hibk])

                def sigm(pb_, e, ek):
                    ACT(e[:, 0:NB], PB[pb_][:, 0:NB], AF.Exp, scale=-1.0, r=[("ps", pb_)], w=[ek])
                    ACT(e[:, 0:NB], e[:, 0:NB], AF.Ln, bias=one_c, r=[ek, "oc"], w=[ek])
                    ACT(e[:, 0:NB], e[:, 0:NB], AF.Exp, scale=-1.0, r=[ek], w=[ek])

                def sigm_sb(z, zk, e, ek):
                    ACT(e[:, 0:NB], z[:, 0:NB], AF.Exp, scale=-1.0, r=[zk], w=[ek])
                    ACT(e[:, 0:NB], e[:, 0:NB], AF.Ln, bias=one_c, r=[ek, "oc"], w=[ek])
                    ACT(e[:, 0:NB], e[:, 0:NB], AF.Exp, scale=-1.0, r=[ek], w=[ek])

                def stageA(b, g):
                    i = g % 2
                    hk_ = "hT%d" % (b % 2)
                    hTb = hT[b % 2]
                    pbank = {"n": 0}

                    def proj(c0):
                        pb_ = (2, 3, 5)[(g * 3 + pbank["n"]) % 3]
                        pbank["n"] += 1
                        for dt_ in range(8):
                            MM(PB[pb_][:, 0:NB], Wb[:, dt_, c0:c0 + 128], hTb[:, dt_, 0:NB], dt_ == 0, dt_ == 7,
                               r=["W2", hk_], w=[("ps", pb_)])
                        return pb_
                    pf = proj(512 + g * 128)
                    pq = proj(0 + g * 128)
                    pg = proj(1536 + g * 128)
                    CP("dve", zq[:, 0:NB], PB[pq][:, 0:NB], r=[("ps", pq)], w=["zq"])
                    CP("dve", zg[:, 0:NB], PB[pg][:, 0:NB], r=[("ps", pg)], w=["zg"])
                    ACT(ez[0][:, 0:NB], PB[pf][:, 0:NB], AF.Exp, scale=-1.0, r=[("ps", pf)], w=["ez0"])
                    ACT(ez[1][:, 0:NB], zq[:, 0:NB], AF.Exp, scale=-1.0, r=["zq"], w=["ez1"])
                    ACT(ez[2][:, 0:NB], zg[:, 0:NB], AF.Exp, scale=-1.0, r=["zg"], w=["ez2"])
                    for j_ in range(3):
                        ACT(ez[j_][:, 0:NB], ez[j_][:, 0:NB], AF.Ln, bias=one_c, r=["ez%d" % j_, "oc"], w=["ez%d" % j_])
                    for j_ in range(3):
                        ACT(ez[j_][:, 0:NB], ez[j_][:, 0:NB], AF.Exp, scale=-1.0, r=["ez%d" % j_], w=["ez%d" % j_])
                    ACT(lf[:, 0:NB], ez[0][:, 0:NB], AF.Ln, bias=lbv[l][:, g:g + 1], scale=oml[l][:, g:g + 1],
                        r=["ez0"] + lbk, w=["lf"])
                    TT("dve", gs[i][:, 0:NB], zg[:, 0:NB], ez[2][:, 0:NB], ALU.mult, r=["zg", "ez2"],
                       w=["gs%d" % i])
                    TS_("dve", hkk[:, 0:NB], ez[0][:, 0:NB], noml[l][:, g:g + 1], oml[l][:, g:g + 1],
                        ALU.mult, ALU.add, r=["ez0"] + lbk, w=["hkk"])
                    SCAN(bcs[:, 0:NB], cmask[:, 0:NB], lf[:, 0:NB], 0.0, ALU.mult, ALU.add,
                         r=["lf", "cmask"], w=["bcs"])
                    ACT(eb[:, 0:NB], bcs[:, 0:NB], AF.Exp, r=["bcs"], w=["eb"])
                    ACT(enb[:, 0:NB], bcs[:, 0:NB], AF.Exp, scale=-1.0, r=["bcs"], w=["enb"])
                    blast = bcs[:, 0:NB].rearrange("p (c t) -> p c t", t=64)[:, :, 63]
                    ACT(dch[i][:, 0:nch], blast, AF.Exp, r=["bcs"], w=["dch%d" % i])
                    TT("dve", ez[1][:, 0:NB], ez[1][:, 0:NB], eb[:, 0:NB], ALU.mult, r=["ez1", "eb"], w=["ez1"])
                    TT("dve", qt[i][:, 0:NB], zq[:, 0:NB], ez[1][:, 0:NB], ALU.mult, r=["zq", "ez1"],
                       w=["qt%d" % i])
                    TT("dve", t1[:, 0:NB], hkk[:, 0:NB], enb[:, 0:NB], ALU.mult, r=["hkk", "enb"], w=["t1"])
                    CP("act", kt[i][:, 0:NB], t1[:, 0:NB], r=["t1"], w=["kt%d" % i])
                    TT("dve", kh[i][:, 0:NB].rearrange("p (c t) -> p c t", t=64),
                       t1[:, 0:NB].rearrange("p (c t) -> p c t", t=64),
                       dch[i][:, 0:nch].unsqueeze(2).to_broadcast([128, nch, 64]), ALU.mult,
                       r=["t1", "dch%d" % i], w=["kh%d" % i])

                def stageB(b, g):
                    i = g % 2
                    hibb = hib[b % 2]
                    hibk = "hib%d" % (b % 2)
                    bsl = slice(b * NB, (b + 1) * NB)
                    Sst = Sst2[b % 2]
                    Snx = Sst2[(b + 1) % 2]

                    def skey(par, slot):
                        return "S%d_%d_%d" % (g, par, slot)
                    qtk, ktk, khk, dk, gk_ = "qt%d" % i, "kt%d" % i, "kh%d" % i, "dch%d" % i, "gs%d" % i
                    pa = bank(0, 2)
                    for tt in range(ntt):
                        tsl = slice(tt * 128, (tt + 1) * 128)
                        MM(PB[pa][:, tsl], kt[i][:, tsl], qt[i][:, tsl], True, True, r=[ktk, qtk], w=[("ps", pa)],
                           skip=True)
                    TT("dve", Am[:, 0:ntt, :], PB[pa][:, 0:NB].rearrange("p (a b) -> p a b", b=128),
                       hmask.unsqueeze(1).to_broadcast([128, ntt, 128]), ALU.mult,
                       r=[("ps", pa), "hmask"], w=["Am"])
                    pt = bank(0, 2)
                    for tt in range(ntt):
                        tsl = slice(tt * 128, (tt + 1) * 128)
                        TR(PBh[pt][:, tsl], kh[i][:, tsl], ident_b, r=[khk, "ident_b"], w=[("ps", pt)])
                    CP("act", khT[:, 0:ntt, :], PBh[pt][:, 0:NB].rearrange("p (a b) -> p a b", b=128),
                       r=[("ps", pt)], w=["khT"])
                    for c in range(nch):
                        tt, half = c // 2, c % 2
                        rows = slice(half * 64, half * 64 + 64)
                        pd = 6 + half
                        MM(PB[pd][:, tt * 128:(tt + 1) * 128], khT[rows, tt, :], hibb[rows, tt, g * 128:(g + 1) * 128],
                           True, True, r=["khT", hibk], w=[("ps", pd)], skip=True)
                    for c in range(nch):
                        kin = skey(b % 2, c)
                        if past:
                            DMA(stl[:, :, :], stin[l, c].rearrange("g k v -> k g v"), w=["stl"])
                            CP("pool", Sst[:, g, c, :], stl[:, g, :], r=["stl"], w=[kin])
                        tt, half = c // 2, c % 2
                        pd = 6 + half
                        if c < nch - 1:
                            Sout, kout = Sst[:, g, c + 1, :], skey(b % 2, c + 1)
                        else:
                            Sout, kout = Snx[:, g, 0, :], skey((b + 1) % 2, 0)
                        STT(Sout, Sst[:, g, c, :], dch[i][:, c:c + 1], PB[pd][:, tt * 128:(tt + 1) * 128],
                            ALU.mult, ALU.add, r=[kin, dk, ("ps", pd)], w=[kout])
                        if past:
                            DMA(sso[l, c, g], Sout, r=[kout])
                    CP("act", Sbf[i][:, 0:nch, :], Sst[:, g, 0:nch, :], r=[skey(b % 2, c_) for c_ in range(nch)],
                       w=["Sbf%d_%d" % (i, c_) for c_ in range(nch)])
                    if past == 0 and b == nblk - 1:
                        DMA(spo[l, g], Snx[:, g, 0, :], r=[skey((b + 1) % 2, 0)])
                    po = 4
                    for c in range(nch):
                        csl = slice(c * 64, (c + 1) * 64)
                        MM(PB[po][:, csl], Sbf[i][:, c, :], qt[i][:, csl], c == 0, False,
                           r=["Sbf%d_%d" % (i, c), qtk], w=[("ps", po)], skip=True)
                    for tt in range(ntt):
                        tsl = slice(tt * 128, (tt + 1) * 128)
                        MM(PB[po][:, tsl], hibb[:, tt, g * 128:(g + 1) * 128], Am[:, tt, :], False, tt == ntt - 1,
                           r=[hibk, "Am"], w=[("ps", po)], skip=True)
                    ACT(sq[:, 0:NB], PB[po][:, 0:NB], AF.Square, r=[("ps", po)], w=["sq"])
                    pn = bank(0, 2)
                    MM(PB[pn][:, 0:NB], ones_b, sq[:, 0:NB], True, True, r=["ones_b", "sq"], w=[("ps", pn)])
                    ACT(rst[:, 0:NB], PB[pn][:, 0:NB], AF.Ln, bias=eps_c, scale=1.0 / GD, r=[("ps", pn), "epsc"],
                        w=["rst"])
                    ACT(rst[:, 0:NB], rst[:, 0:NB], AF.Exp, scale=-0.5, r=["rst"], w=["rst"])
                    TT("dve", t2[:, 0:NB], PB[po][:, 0:NB], rst[:, 0:NB], ALU.mult, r=[("ps", po), "rst"], w=["t2"])
                    y = yb[cn["y"] % 2]
                    yk = "yb%d" % (cn["y"] % 2)
                    cn["y"] += 1
                    STT(y[:, 0:NB], t2[:, 0:NB], gnv[l][:, g:g + 1], gs[i][:, 0:NB], ALU.mult, ALU.mult,
                        r=["t2", gk_] + lbk, w=[yk])
                    DMA(st["YT"][512 + g * 128:512 + (g + 1) * 128, bsl], y[:, 0:NB], r=[yk])

                jobs = [(b, g) for b in range(nblk) for g in range(G)]

                def emitA(k):
                    b, g = jobs[k]
                    if g == 0:
                        pre_block(b)
                    stageA(b, g)
                emitA(0)
                emitA(1)
                for k in range(len(jobs)):
                    stageB(*jobs[k])
                    if k + 2 < len(jobs):
                        emitA(k + 2)
            A.release()

        def attend_epilogue(po, Nq, Gt, gk, ydst, bufs):
            rc, rch, rcf, rcl, tn, yo, yk = bufs
            RECIP(rc[64:65, 0:Nq], PB[po][64:65, 0:Nq], r=[("ps", po)], w=["rc"])
            CP("dve", rch[64:65, 0:Nq], rc[64:65, 0:Nq], r=["rc"], w=["rch"])
            CP("dve", rcf[64:65, 0:Nq], rch[64:65, 0:Nq], r=["rch"], w=["rcf"])
            TT("dve", rcl[64:65, 0:Nq], rc[64:65, 0:Nq], rcf[64:65, 0:Nq], ALU.subtract, r=["rc", "rcf"], w=["rcl"])
            pbc = 7
            MM(PB[pbc][0:64, 0:Nq], ones_b[64:65, 0:64], rch[64:65, 0:Nq], True, False, r=["ones_b", "rch"],
               w=[("ps", pbc)])
            MM(PB[pbc][0:64, 0:Nq], ones_b[64:65, 0:64], rcl[64:65, 0:Nq], False, True, r=["ones_b", "rcl"],
               w=[("ps", pbc)])
            TT("dve", tn[0:64, 0:Nq], PB[po][0:64, 0:Nq], Gt, ALU.mult, r=[("ps", po), gk], w=["tn"])
            TT("dve", yo[0:64, 0:Nq], tn[0:64, 0:Nq], PB[pbc][0:64, 0:Nq], ALU.mult, r=["tn", ("ps", pbc)], w=[yk])
            DMA(ydst, yo[0:64, 0:Nq], r=[yk], w=["YTdram"])

        def phase_C_prompt(l, negc):
            st = P
            T = st["T"]
            NT = T // 128
            SBQ = 1024
            NSB = T // SBQ
            NDUM = cfg.get("ndum", 0)
            A.mark()
            Ka = [A.alloc([T], BF16) for _ in range(2)]
            Qa = [A.alloc([T], BF16) for _ in range(2)]
            Ga = [A.alloc([T], BF16) for _ in range(2)]
            Va = [A.alloc([NT, 128], BF16) for _ in range(2)]
            pT = [A.alloc([SBQ], BF16) for _ in range(4)]
            rc = A.alloc([SBQ], F32)
            rch = A.alloc([SBQ], BF16)
            rcf = A.alloc([SBQ], F32)
            rcl = A.alloc([SBQ], BF16)
            tn = A.alloc([SBQ], F32)
            yo = [A.alloc([SBQ], BF16) for _ in range(2)]
            for i in range(2):
                MSET("pool", Ka[i][64:128, :], 0.0, w=["Ka%d" % i])
                MSET("pool", Ka[i][64:66, :], 1.0, w=["Ka%d" % i])
                MSET("dve", Qa[i][64:128, :], 0.0, w=["Qa%d" % i])
                MSET("pool", Va[i][:, :, 64:128], 0.0, w=["Va%d" % i])
                MSET("pool", Va[i][:, :, 64:65], 1.0, w=["Va%d" % i])
            ng = negc["p"][0]
            cnt = {"s": 0, "p": 0, "o": 0, "y": 0}
            pend = []

            def loads(h):
                i = h % 2
                DMA(Ka[i][0:64, :], st["KT"][h], w=["Ka%d" % i], q=LQ)
                DMA(Qa[i][0:66, :], st["QT"][h], w=["Qa%d" % i], q=LQ)
                DMA(Ga[i][0:64, :], st["GT"][h * 64:(h + 1) * 64, :], w=["Ga%d" % i], q=LQ)
                DMA(Va[i][:, :, 0:64], st["Vb"].rearrange("(a p) e -> p a e", p=128)[:, :, h * 64:(h + 1) * 64],
                    w=["Va%d" % i], q="sp")

            loads(0)
            for h in range(H):
                i = h % 2
                if h + 1 < H:
                    loads(h + 1)
                K_, Q_, G_, V_ = Ka[i], Qa[i], Ga[i], Va[i]
                kk, qk, gk, vk = "Ka%d" % i, "Qa%d" % i, "Ga%d" % i, "Va%d" % i
                for I2 in range(NSB):
                    oi = 2 + cnt["o"] % 2
                    cnt["o"] += 1
                    O = PB2[oi]
                    ok = ("ps2", oi)
                    nJ = 8 * I2 + 8
                    q0 = I2 * SBQ

                    def qk_step(J):
                        n0 = max(0, J - 8 * I2) * 128
                        diag = J >= 8 * I2
                        si = cnt["s"] % 2
                        cnt["s"] += 1
                        S_ = PB2[si]
                        sk = ("ps2", si)
                        Kt = K_[:, J * 128:(J + 1) * 128]
                        for d in range(NDUM):
                            MM(S_[:, 0:512], Kt, K_[:, 0:512], True, True, r=[kk], w=[sk])
                        if n0 < 512:
                            MM(S_[:, n0:512], Kt, Q_[:, q0 + n0:q0 + 512], True, not diag, r=[kk, qk], w=[sk])
                            if diag:
                                MM(S_[:, n0:n0 + 128], ident_b, maskneg, False, True, r=["ident_b", "maskneg"], w=[sk])
                            MM(S_[:, 512:1024], Kt, Q_[:, q0 + 512:q0 + 1024], True, True, r=[kk, qk], w=[sk])
                        else:
                            MM(S_[:, n0:1024], Kt, Q_[:, q0 + n0:q0 + 1024], True, False, r=[kk, qk], w=[sk])
                            MM(S_[:, n0:n0 + 128], ident_b, maskneg, False, True, r=["ident_b", "maskneg"], w=[sk])
                        pi = cnt["p"] % 4
                        cnt["p"] += 1
                        pt_ = pT[pi]
                        ptk = "pT%d" % pi
                        ACT(pt_[:, n0:1024], S_[:, n0:1024], AF.Exp, bias=ng[:, J * 8 + h:J * 8 + h + 1],
                            r=[sk, "negcp0"], w=[ptk])
                        return (J, n0, pt_, ptk)

                    def pv_step(item):
                        J, n0, pt_, ptk = item
                        last = (J == nJ - 1)
                        if n0 < 512:
                            MM(O[:, n0:512], V_[:, J, :], pt_[:, n0:512], J == 0, last, r=[vk, ptk], w=[ok], skip=True)
                            MM(O[:, 512:1024], V_[:, J, :], pt_[:, 512:1024], J == 0, last, r=[vk, ptk], w=[ok],
                               skip=True)
                        else:
                            MM(O[:, n0:1024], V_[:, J, :], pt_[:, n0:1024], False, last, r=[vk, ptk], w=[ok],
                               skip=True)

                    items = []
                    for J in range(nJ):
                        items.append(qk_step(J))
                        if J >= 2:
                            pv_step(items[J - 2])
                        if J == 5 and pend:
                            pend.pop(0)()
                    for it in items[max(0, nJ - 2):]:
                        pv_step(it)
                    RECIP(rc[64:65, :], O[64:65, :], r=[ok], w=["rc"])
                    TT("dve", tn[0:64, :], O[0:64, :], G_[0:64, q0:q0 + SBQ], ALU.mult, r=[ok, gk], w=["tn"])

                    def stage2(h=h, q0=q0):
                        si = cnt["s"] % 2
                        cnt["s"] += 1
                        bc = PB2[si]
                        bk = ("ps2", si)
                        for hf in range(2):
                            hs = slice(hf * 512, (hf + 1) * 512)
                            MM(bc[0:64, hs], ones_f[64:65, 0:64], rc[64:65, hs], True, True, r=["ones_f", "rc"], w=[bk])
                        yb_ = yo[cnt["y"] % 2]
                        yk = "yo%d" % (cnt["y"] % 2)
                        cnt["y"] += 1
                        TT("dve", yb_[0:64, :], tn[0:64, :], bc[0:64, :], ALU.mult, r=["tn", bk], w=[yk])
                        DMA(st["YT"][h * 64:(h + 1) * 64, q0:q0 + SBQ], yb_[0:64, :], r=[yk])
                    pend.append(stage2)
            while pend:
                pend.pop(0)()
            A.release()

        def phase_C_sample(l, negc):
            st = Sm
            NTc = PAST // 128
            KcT = A.alloc([H, PAST], BF16)
            Vc = A.alloc([NTc, H, 65], BF16)
            CHT = 4
            stg = [A.alloc([CHT, 512], F32) for _ in range(2)]
            kbf = [A.alloc([CHT, 512], BF16) for _ in range(2)]
            Kn = A.alloc([H, 128], BF16)
            Qn = A.alloc([H, 128], BF16)
            Gn = A.alloc([H, 128], BF16)
            Vn = A.alloc([H, 65], BF16)
            pT = [A.alloc([64], BF16) for _ in range(3)]
            rc = A.alloc([512], F32)
            rch = A.alloc([512], BF16)
            rcf = A.alloc([512], F32)
            rcl = A.alloc([512], BF16)
            tn = A.alloc([512], F32)
            yo = [A.alloc([512], BF16) for _ in range(2)]
            MSET("pool", KcT[64:66, :, :], 1.0, w=["KcT"])
            MSET("pool", Vc[:, :, :, 64:65], 1.0, w=["Vc"])
            MSET("pool", Kn[64:66, :, :], 1.0, w=["Kn"])
            MSET("pool", Vn[:, :, 64:65], 1.0, w=["Vn"])
            for h in range(H):
                DMA(Kn[0:64, h, :], st["KT"][h], w=["Kn"])
                DMA(Qn[0:66, h, :], st["QT"][h], w=["Qn"])
                DMA(Gn[0:64, h, :], st["GT"][h * 64:(h + 1) * 64, :], w=["Gn"])
            yield
            pn = 0
            yn = 0
            sn = 0
            for s in range(NS):
                ng = negc["s"][s]
                ngk = "negcs%d" % s
                DMA(Vn[0:64, :, 0:64], st["Vb"][s * 64:(s + 1) * 64, :].rearrange("t (h e) -> t h e", h=H), w=["Vn"])
                for c in range(NTc // CHT):
                    for src, kind in ((ck, "k"), (cv, "v")):
                        sb = stg[sn % 2]
                        sk = "cstg%d" % (sn % 2)
                        kb = kbf[sn % 2]
                        kbk = "kbf%d" % (sn % 2)
                        sn += 1
                        DMA(sb, src[l, s, c * CHT * 128:(c + 1) * CHT * 128, :].rearrange("(a p) e -> p a e", p=128),
                            w=[sk])
                        if kind == "v":
                            CP("act" if c % 2 else "dve", Vc[:, c * CHT:(c + 1) * CHT, :, 0:64],
                               sb.rearrange("p a (h e) -> p a h e", h=H), r=[sk], w=["Vc"])
                            continue
                        CP("dve", kb, sb, r=[sk], w=[kbk])
                        for h in range(H):
                            pb = bank(0, 2)
                            for a in range(CHT):
                                TR(PBh[pb][0:64, a * 128:(a + 1) * 128], kb[:, a, h * 64:(h + 1) * 64], ident_b,
                                   r=[kbk, "ident_b"], w=[("ps", pb)])
                            CP("act" if h % 2 else "dve", KcT[0:64, h, c * CHT * 128:(c + 1) * CHT * 128],
                               PBh[pb][0:64, 0:CHT * 128], r=[("ps", pb)], w=["KcT"])
                yield
                for h in range(H):
                    po = 3 + (h % 2)
                    nJ = NTc + 1
                    qsl = slice(s * 64, (s + 1) * 64)

                    def qk_step(J):
                        nonlocal pn
                        ps_ = pn % 3
                        pn += 1
                        pt_ = pT[ps_]
                        ptk = "pTs%d" % ps_
                        if J < NTc:
                            MM(PB[ps_][:, 0:64], KcT[0:66, h, J * 128:(J + 1) * 128], Qn[0:66, h, qsl], True, True,
                               r=["KcT", "Qn"], w=[("ps", ps_)])
                            ACT(pt_[:, 0:64], PB[ps_][:, 0:64], AF.Exp, bias=ng[:, J * 8 + h:J * 8 + h + 1],
                                r=[("ps", ps_), ngk], w=[ptk])
                        else:
                            MM(PB[ps_][0:64, 0:64], Kn[0:66, h, qsl], Qn[0:66, h, qsl], True, False,
                               r=["Kn", "Qn"], w=[("ps", ps_)])
                            MM(PB[ps_][0:64, 0:64], ident_b[0:64, 0:64], maskneg[0:64, 0:64], False, True,
                               r=["ident_b", "maskneg"], w=[("ps", ps_)])
                            ACT(pt_[0:64, 0:64], PB[ps_][0:64, 0:64], AF.Exp, bias=ng[0:64, J * 8 + h:J * 8 + h + 1],
                                r=[("ps", ps_), ngk], w=[ptk])
                        return (J, pt_, ptk)

                    def pv_step(item):
                        J, pt_, ptk = item
                        if J < NTc:
                            MM(PB[po][0:65, 0:64], Vc[:, J, h, :], pt_[:, 0:64], J == 0, False, r=["Vc", ptk],
                               w=[("ps", po)], skip=True)
                        else:
                            MM(PB[po][0:65, 0:64], Vn[0:64, h, :], pt_[0:64, 0:64], False, True, r=["Vn", ptk],
                               w=[("ps", po)], skip=True)

                    prev = None
                    for J in range(nJ):
                        cur = qk_step(J)
                        if prev is not None:
                            pv_step(prev)
                        prev = cur
                        yield
                    pv_step(prev)
                    attend_epilogue(po, 64, Gn[0:64, h, qsl], "Gn", st["YT"][h * 64:(h + 1) * 64, qsl],
                                    (rc, rch, rcf, rcl, tn, yo[yn % 2], "yos%d" % (yn % 2)))
                    yn += 1
                    yield

        def phase_D_setup(l):
            ctx = {}
            ctx["Wo"] = A.alloc([8, D], BF16)
            load_w(l, w_out[l], 0, D, ctx["Wo"], "wout")
            ctx["yT"] = [A.alloc([8, 512], BF16) for _ in range(2)]
            ctx["xbuf"] = [A.alloc([D], F32) for _ in range(3)]
            ctx["xo"] = [A.alloc([D], F32) for _ in range(2)]
            ctx["junk"] = A.alloc([D], F32)
            ctx["ssb"] = A.alloc([8], F32)
            ctx["xi"] = 0
            ctx["yi"] = 0
            return ctx

        def phase_D_run(l, ctx, st):
            Wo, yT, xbuf, xo, junk, ssb = ctx["Wo"], ctx["yT"], ctx["xbuf"], ctx["xo"], ctx["junk"], ctx["ssb"]
            last = (l == 1)
            T, NB = st["T"], st["NB"]
            ntt = NB // 128
            xsrc = st["x0"] if l == 0 else st["x1"]
            YTv = st["YT"].rearrange("(a p) t -> p a t", p=128)
            ydep = ["YTdram"] if st is Sm else []
            y0 = ctx["yi"]
            DMA(yT[y0 % 2][:, :, 0:NB], YTv[:, :, 0:NB], r=ydep, w=["yT%d" % (y0 % 2)])
            for b in range(st["nblk"]):
                yi = ctx["yi"]
                ctx["yi"] += 1
                yk = "yT%d" % (yi % 2)
                yTb = yT[yi % 2]
                if b + 1 < st["nblk"]:
                    DMA(yT[(yi + 1) % 2][:, :, 0:NB], YTv[:, :, (b + 1) * NB:(b + 2) * NB], r=ydep,
                        w=["yT%d" % ((yi + 1) % 2)])
                for tt in range(ntt):
                    t0 = b * NB + tt * 128
                    xi = ctx["xi"]
                    ctx["xi"] += 1
                    xb = xbuf[xi % 3]
                    xk = "xbuf%d" % (xi % 3)
                    xob = xo[xi % 2]
                    xok = "xo%d" % (xi % 2)
                    sc = ssb[:, (xi % 2) * 4:(xi % 2) * 4 + 1]
                    sck = "ss%d" % (xi % 2)
                    DMA(xb, xsrc[t0:t0 + 128, :], w=[xk], q=LQ)
                    for half in range(2):
                        pb = bank(5, 7)
                        for et in range(8):
                            MM(PB[pb][:, :], yTb[:, et, tt * 128:(tt + 1) * 128],
                               Wo[:, et, half * 512:(half + 1) * 512], et == 0, et == 7,
                               r=[yk, "wout"], w=[("ps", pb)])
                        TT("dve", xob[:, half * 512:(half + 1) * 512], PB[pb][:, :],
                           xb[:, half * 512:(half + 1) * 512], ALU.add, r=[("ps", pb), xk], w=[xok])
                        yield
                    if not last:
                        DMA(st["x1"][t0:t0 + 128, :], xob, r=[xok])
                    else:
                        ACT(junk, xob, AF.Square, accum=sc, r=[xok], w=["junk", sck])
                        ACT(sc, sc, AF.Ln, bias=eps_c, scale=1.0 / D, r=[sck, "epsc"], w=[sck])
                        ACT(sc, sc, AF.Exp, scale=-0.5, r=[sck], w=[sck])
                        STT(xb, xob, sc, fgb, ALU.mult, ALU.mult, r=[xok, sck, "fgb"], w=[xk])
                        DMA(st["y"][t0:t0 + 128, :], xb, r=[xk])

        tokp = A.alloc([(SEQ // 128) * 8], F32)
        negc = {
            "p": [A.alloc([(SEQ // 128) * 8], F32)],
            "s": [A.alloc([((PAST + TS + 127) // 128) * 8], F32) for _ in range(NS)],
        }
        for l in range(2):
            lfT = phase_A1(l)
            S.barrier()
            phase_B(l, lfT, negc)
            A.release()
            S.barrier()
            phase_A2(l)
            S.barrier()
            phase_C_prompt(l, negc)
            S.barrier()
            A.mark()
            gC = phase_C_sample(l, negc)
            next(gC)
            dctx = phase_D_setup(l)
            gD = phase_D_run(l, dctx, P)
            c_alive, d_alive = True, True
            while c_alive or d_alive:
                for _ in range(2):
                    if c_alive:
                        try:
                            next(gC)
                        except StopIteration:
                            c_alive = False
                if d_alive:
                    try:
                        next(gD)
                    except StopIteration:
                        d_alive = False
            for _ in phase_D_run(l, dctx, Sm):
                pass
            A.release()
            S.barrier()
        S.emit()
    return nc


CFG_FULL = dict(SEQ=8192, PAST=2048)
_CONSTS = None


def _consts():
    ident = np.eye(128, dtype=np.float32)
    k = np.arange(128)[:, None]
    q = np.arange(128)[None, :]
    maskneg = np.where(k <= q, 0.0, NEG).astype(np.float32)
    hmask = ((k <= q) & ((k // 64) == (q // 64))).astype(np.float32)
    return ident, maskneg, hmask


def run(cfg, x_prompt, x_sample, cache_k, cache_v, cache_logf, state_hgrn,
        norm_g, w_in, fox_b_f, hg_lower, hg_norm_g, w_out, final_g, n_cores=8):
    SEQ, PAST = cfg["SEQ"], cfg["PAST"]
    nc = build(cfg)
    ident, maskneg, hmask = _consts()
    f = lambda a: np.ascontiguousarray(np.asarray(a, dtype=np.float32))
    in_maps = []
    for c in range(n_cores):
        sl = slice(2 * c, 2 * c + 2)
        in_maps.append({
            "xp": f(x_prompt[c]),
            "xs": f(x_sample[sl]).reshape(128, D),
            "ck": f(cache_k[:, sl]).reshape(2, 2, PAST, H * HD),
            "cv": f(cache_v[:, sl]).reshape(2, 2, PAST, H * HD),
            "clf": f(cache_logf[:, sl]),
            "st": f(state_hgrn[:, sl]),
            "norm_g": f(norm_g), "w_in": f(w_in), "fox_b_f": f(fox_b_f), "hg_lower": f(hg_lower),
            "hg_norm_g": f(hg_norm_g), "w_out": f(w_out), "final_g": f(final_g),
            "c_ident": ident, "c_maskneg": maskneg, "c_hmask": hmask,
        })
    res = run_bass_kernel_spmd(nc, in_maps, core_ids=list(range(n_cores)))
    R = res.results
    B = n_cores
    if cfg.get("debug"):
        global DEBUG_R
        DEBUG_R = R
    y_prompt = np.stack([R[c]["yp"] for c in range(B)])
    y_sample = np.concatenate([R[c]["ys"].reshape(2, 64, D) for c in range(B)])
    k_prompt = np.stack([R[c]["kp"].reshape(2, SEQ, H, HD) for c in range(B)], axis=1)
    v_prompt = np.stack([R[c]["vp"].reshape(2, SEQ, H, HD) for c in range(B)], axis=1)
    logf_prompt = np.stack([R[c]["lfp"] for c in range(B)], axis=1)
    hgrn_prompt = np.stack([R[c]["sp"] for c in range(B)], axis=1)
    k_sample = np.concatenate([R[c]["ks"].reshape(2, 2, 64, H, HD) for c in range(B)], axis=1)
    v_sample = np.concatenate([R[c]["vs"].reshape(2, 2, 64, H, HD) for c in range(B)], axis=1)
    logf_sample = np.concatenate([R[c]["lfs"].reshape(2, 2, 64, H) for c in range(B)], axis=1)
    hgrn_sample = np.concatenate([R[c]["ss"] for c in range(B)], axis=1)
    outs = (y_prompt, y_sample, k_prompt, v_prompt, logf_prompt, hgrn_prompt,
            k_sample, v_sample, logf_sample, hgrn_sample)
    return tuple(np.ascontiguousarray(o, dtype=np.float32) for o in outs)


def kernel(x_prompt, x_sample, cache_k, cache_v, cache_logf, state_hgrn,
           norm_g, w_in, fox_b_f, hg_lower, hg_norm_g, w_out, final_g):
    return run(CFG_FULL, x_prompt, x_sample, cache_k, cache_v, cache_logf, state_hgrn,
               norm_g, w_in, fox_b_f, hg_lower, hg_norm_g, w_out, final_g)
```

```python
import numpy as np
from contextlib import ExitStack
import concourse.bass as bass
import concourse.mybir as mybir
from concourse.bass_utils import run_bass_kernel_spmd

F32 = mybir.dt.float32
BF16 = mybir.dt.bfloat16
AF = mybir.ActivationFunctionType
ALU = mybir.AluOpType

D = 1024
DIN = 4104
H = 8
HD = 64
G = 4
GD = 128
EPS = 1e-6
NEG = -30000.0


class Sched:
    ENGS = ("pe", "act", "dve", "pool", "sp")
    EPOCH = 12000
    NDMA = 14

    def __init__(self, nc, es):
        self.nc = nc
        self.es = es
        self.streams = {e: [] for e in self.ENGS}
        self.n = {e: 0 for e in self.ENGS}
        self.sems = {}
        self.known = {}
        self.lastw = {}
        self.readers = {}
        self.pending = {e: [] for e in self.ENGS}
        self.last_tok = {}
        self.dma_last = {}
        self.ndma = {}
        self.lazy = set()

    def sem(self, key):
        if key not in self.sems:
            nm = "s_" + "_".join(str(k) for k in key)
            self.sems[key] = self.es.enter_context(self.nc.semaphore(nm))
        return self.sems[key]

    def op(self, eng, fn, r=(), w=(), dma=False, lazy=False):
        waits = {}
        me = eng + "_dma" if dma else eng

        def need(tok):
            if tok is None:
                return
            teng, key, val = tok
            if teng == eng == "pe":
                return
            if self.known.get((eng, key), 0) >= val:
                return
            if waits.get(key, 0) < val:
                waits[key] = val

        for tok in self.pending[eng]:
            need(tok)
        self.pending[eng] = []
        for k in r:
            need(self.lastw.get(k))
            if isinstance(k, tuple) and k[0] in ("ps", "ps2"):
                for t in self.readers.get(k, ()):
                    if t[0] != me:
                        need(t)
        for k in w:
            need(self.lastw.get(k))
            for t in self.readers.get(k, ()):
                if t[0] != me or dma:
                    need(t)
        if dma:
            i = self.ndma.get(eng, 0)
            self.ndma[eng] = i + 1
            slot = i % self.NDMA
            key = (me, slot)
            val = 16 * (i // self.NDMA + 1)
            if val > 16 and self.known.get((eng, key), 0) < val - 16:
                waits[key] = max(waits.get(key, 0), val - 16)
            inc = 16
            self.dma_last[key] = val
            if lazy:
                self.lazy.add((key, val))
        else:
            i = self.n[eng]
            self.n[eng] += 1
            key = (eng, i // self.EPOCH)
            val = i % self.EPOCH + 1
            inc = 1
            self.last_tok[eng] = (eng, key, val)
        self.sem(key)
        for k2 in waits:
            self.sem(k2)
        tok = (me, key, val)
        for k2, v2 in waits.items():
            self.known[(eng, k2)] = v2
        self.streams[eng].append((list(waits.items()), fn, key, inc))
        for k in r:
            lst = self.readers.setdefault(k, [])
            if not dma:
                lst[:] = [t for t in lst if t[0] != me]
            lst.append(tok)
        for k in w:
            self.lastw[k] = tok
            self.readers[k] = []
        return tok

    def barrier(self):
        toks = []
        for e in ("pe", "act", "dve", "pool"):
            if e in self.last_tok:
                toks.append(self.last_tok[e])
        for key, val in self.dma_last.items():
            if (key, val) in self.lazy:
                continue
            toks.append((key[0], key, val))
        for e in self.ENGS:
            self.pending[e] = [t for t in toks if t[0] != e]

    def emit(self):
        nc = self.nc
        streams = self.streams
        sems = self.sems
        dma_last = dict(self.dma_last)

        def mk(name):
            def body(eng):
                for waits, fn, key, inc in streams[name]:
                    for k, v in waits:
                        eng.wait_ge(sems[k], v)
                    ins = fn(eng)
                    ins.then_inc(sems[key], inc)
                if name == "sp":
                    for key, val in dma_last.items():
                        eng.wait_ge(sems[key], val)
            return body

        with nc.Block() as block:
            block.tensor(mk("pe"))
            block.scalar(mk("act"))
            block.vector(mk("dve"))
            block.gpsimd(mk("pool"))
            block.sync(mk("sp"))


class Arena:
    def __init__(self, nc, es, words):
        self.t = es.enter_context(nc.sbuf_tensor("arena", [128, words], F32))
        self.words = words
        self.off = 0
        self.marks = []

    def alloc(self, free_shape, dt):
        n = int(np.prod(free_shape))
        bpe = 4 if dt == F32 else 2
        w = (n * bpe + 3) // 4
        w = (w + 7) // 8 * 8
        assert self.off + w <= self.words, ("arena overflow", self.off, w, self.words)
        ap = self.t[:, self.off:self.off + w]
        self.off += w
        if dt != F32:
            ap = ap.bitcast(dt)
        ap = ap[:, 0:n]
        if len(free_shape) == 2:
            ap = ap.rearrange("p (a b) -> p a b", a=free_shape[0])
        elif len(free_shape) == 3:
            ap = ap.rearrange("p (a b c) -> p a b c", a=free_shape[0], b=free_shape[1])
        return ap

    def mark(self):
        self.marks.append(self.off)

    def release(self):
        self.off = self.marks.pop()


def build(cfg):
    SEQ = cfg["SEQ"]
    PAST = cfg["PAST"]
    TS = 64
    NS = 2
    TSAMP = NS * TS
    nc = bass.Bass("TRN2", target_bir_lowering=False)

    def din(name, shape):
        return nc.dram_tensor(name, list(shape), F32, kind="ExternalInput").ap()

    def dout(name, shape):
        return nc.dram_tensor(name, list(shape), F32, kind="ExternalOutput").ap()

    def dscr(name, shape, dt):
        if cfg.get("debug"):
            return nc.dram_tensor(name, list(shape), dt, kind="ExternalOutput").ap()
        return nc.dram_tensor(name, list(shape), dt).ap()

    xp = din("xp", [SEQ, D])
    xs = din("xs", [TSAMP, D])
    ck = din("ck", [2, NS, PAST, H * HD])
    cv = din("cv", [2, NS, PAST, H * HD])
    clf = din("clf", [2, NS, PAST, H])
    stin = din("st", [2, NS, G, GD, GD])
    norm_g = din("norm_g", [2, D])
    w_in = din("w_in", [2, D, DIN])
    fox_b_f = din("fox_b_f", [2, H])
    hg_lower = din("hg_lower", [2, G * GD])
    hg_norm_g = din("hg_norm_g", [2, G * GD])
    w_out = din("w_out", [2, D, D])
    final_g = din("final_g", [D])
    c_ident = din("c_ident", [128, 128])
    c_maskneg = din("c_maskneg", [128, 128])
    c_hmask = din("c_hmask", [128, 128])

    yp = dout("yp", [SEQ, D])
    ys = dout("ys", [TSAMP, D])
    kp = dout("kp", [2, SEQ, H * HD])
    vp = dout("vp", [2, SEQ, H * HD])
    lfp = dout("lfp", [2, SEQ, H])
    spo = dout("sp", [2, G, GD, GD])
    ks = dout("ks", [2, TSAMP, H * HD])
    vs = dout("vs", [2, TSAMP, H * HD])
    lfs = dout("lfs", [2, TSAMP, H])
    sso = dout("ss", [2, NS, G, GD, GD])

    def mkstream(nm, T, NB):
        return dict(
            name=nm, T=T, NB=NB, nblk=T // NB,
            x1=dscr(nm + "_x1", [T, D], F32),
            QT=dscr(nm + "_QT", [H, HD + 2, T], BF16),
            KT=dscr(nm + "_KT", [H, HD, T], BF16),
            Vb=dscr(nm + "_Vb", [T, H * HD], BF16),
            GT=dscr(nm + "_GT", [H * HD, T], BF16),
            YT=dscr(nm + "_YT", [D, T], BF16),
            HT=dscr(nm + "_HT", [D, T], BF16),
        )

    P = mkstream("p", SEQ, 512)
    Sm = mkstream("s", TSAMP, 128)
    P.update(x0=xp, y=yp, kout=kp, vout=vp, lfout=lfp, past=0)
    Sm.update(x0=xs, y=ys, kout=ks, vout=vs, lfout=lfs, past=PAST)
    STREAMS = (P, Sm)

    es = ExitStack()
    with es:
        S = Sched(nc, es)
        PB2 = [es.enter_context(nc.psum_tensor("pb%d" % i, [128, 1024], F32)) for i in range(4)]
        PB = []
        for i in range(4):
            PB.append(PB2[i][:, 0:512])
            PB.append(PB2[i][:, 512:1024])
        PBh = [pb.bitcast(BF16) for pb in PB]
        cst = es.enter_context(nc.sbuf_tensor("cst", [128, 2048], F32))
        A = Arena(nc, es, 47000)

        LQ = cfg.get("loadq", "pool")

        def DMA(out, in_, r=(), w=(), slow=False, q="sp", lazy=False):
            if slow:
                S.op(q, lambda e: e.dma_start(out=out, in_=in_, allow_slow_non_contiguous=True), r, w, dma=True,
                     lazy=lazy)
            else:
                S.op(q, lambda e: e.dma_start(out=out, in_=in_), r, w, dma=True, lazy=lazy)

        def MM(out, lhsT, rhs, start, stop, r=(), w=(), skip=False):
            if skip:
                S.op("pe", lambda e: e.matmul(out, lhsT, rhs, start=start, stop=stop, skip_group_check=True), r, w)
            else:
                S.op("pe", lambda e: e.matmul(out, lhsT, rhs, start=start, stop=stop), r, w)

        def TR(out, in_, ident, r=(), w=()):
            S.op("pe", lambda e: e.transpose(out, in_, ident), r, w)

        def ACT(out, in_, func, bias=None, scale=None, accum=None, r=(), w=()):
            kw = {}
            if bias is not None:
                kw["bias"] = bias
            if scale is not None:
                kw["scale"] = scale
            if accum is not None:
                kw["accum_out"] = accum
            S.op("act", lambda e: e.activation(out=out, in_=in_, func=func, **kw), r, w)

        def TS_(eng, out, in0, s1, s2, op0, op1=None, r=(), w=()):
            if op1 is None:
                S.op(eng, lambda e: e.tensor_scalar(out=out, in0=in0, scalar1=s1, scalar2=None, op0=op0), r, w)
            else:
                S.op(eng, lambda e: e.tensor_scalar(out=out, in0=in0, scalar1=s1, scalar2=s2, op0=op0, op1=op1), r, w)

        def TT(eng, out, in0, in1, op, r=(), w=()):
            S.op(eng, lambda e: e.tensor_tensor(out=out, in0=in0, in1=in1, op=op), r, w)

        def STT(out, in0, scalar, in1, op0, op1, r=(), w=()):
            S.op("dve", lambda e: e.scalar_tensor_tensor(out=out, in0=in0, scalar=scalar, in1=in1, op0=op0, op1=op1), r, w)

        def CP(eng, out, in_, r=(), w=()):
            if eng == "act":
                S.op("act", lambda e: e.copy(out=out, in_=in_), r, w)
            else:
                S.op(eng, lambda e: e.tensor_copy(out=out, in_=in_), r, w)

        def MSET(eng, ap, val, w=()):
            S.op(eng, lambda e: e.memset(ap, val), (), w)

        def RECIP(out, in_, r=(), w=()):
            S.op("dve", lambda e: e.reciprocal(out=out, in_=in_), r, w)

        def SCAN(out, d0, d1, init, op0, op1, r=(), w=()):
            S.op("dve", lambda e: e.tensor_tensor_scan(out=out, data0=d0, data1=d1, initial=init, op0=op0, op1=op1), r, w)

        ident_f = cst[:, 0:128]
        cbf = cst[:, 128:128 + 256].bitcast(BF16)
        ident_b = cbf[:, 0:128]
        maskneg = cbf[:, 128:256]
        hmask = cbf[:, 256:384]
        ones_b = cbf[:, 384:512]
        ctmp = cst[:, 384:768]
        vec = cst[:, 768:1024]
        fgb = cst[:, 1024:2048]
        zero_c = vec[:, 200:201]
        one_c = vec[:, 201:202]

        DMA(ident_f, c_ident, w=["ident_f"])
        DMA(ctmp[:, 0:128], c_maskneg, w=["ctmp0"])
        DMA(ctmp[:, 128:256], c_hmask, w=["ctmp1"])
        CP("dve", ident_b, ident_f, r=["ident_f"], w=["ident_b"])
        CP("dve", maskneg, ctmp[:, 0:128], r=["ctmp0"], w=["maskneg"])
        CP("dve", hmask, ctmp[:, 128:256], r=["ctmp1"], w=["hmask"])
        MSET("dve", ones_b, 1.0, w=["ones_b"])
        ones_f = es.enter_context(nc.sbuf_tensor("ones_f", [128, 64], F32))
        MSET("dve", ones_f[:, :], 1.0, w=["ones_f"])
        MSET("dve", zero_c, 0.0, w=["zc"])
        MSET("dve", one_c, 1.0, w=["oc"])
        eps_c = vec[:, 202:203]
        MSET("dve", eps_c, EPS, w=["epsc"])
        DMA(fgb, final_g.partition_broadcast(128), w=["fgb"])
        CK = ["ident_f", "ident_b", "maskneg", "hmask", "ones_b", "zc", "oc", "fgb"]

        def vslot(i, n):
            return vec[:, i:i + n]
        gcol = [vslot(0, 8), vslot(8, 8)]
        hl = [vslot(16, 4), vslot(20, 4)]
        lbv = [vslot(24, 4), vslot(28, 4)]
        oml = [vslot(32, 4), vslot(36, 4)]
        noml = [vslot(40, 4), vslot(44, 4)]
        gnv = [vslot(48, 4), vslot(52, 4)]
        bfc = [vslot(56, 1), vslot(57, 1)]
        nbfc = [vslot(58, 1), vslot(59, 1)]
        for l in range(2):
            DMA(gcol[l], norm_g[l].rearrange("(a p) -> p a", p=128), w=["gcol%d" % l], slow=True)
            DMA(hl[l], hg_lower[l].rearrange("(g p) -> p g", p=128), w=["hl%d" % l], slow=True)
            DMA(gnv[l], hg_norm_g[l].rearrange("(g p) -> p g", p=128), w=["gn%d" % l], slow=True)
            DMA(bfc[l][0:8, :], fox_b_f[l].rearrange("(h o) -> h o", o=1), w=["bf%d" % l], slow=True)
        for l in range(2):
            TS_("dve", nbfc[l][0:8, :], bfc[l][0:8, :], -1.0, None, ALU.mult, r=["bf%d" % l], w=["nbf%d" % l])
        MSET("dve", lbv[0], 0.0, w=["lb0"])
        TT("dve", lbv[1], hl[1], hl[0], ALU.subtract, r=["hl0", "hl1"], w=["lb1"])
        ACT(lbv[1], lbv[1], AF.Sigmoid, r=["lb1"], w=["lb1"])
        for l in range(2):
            TS_("dve", oml[l], lbv[l], -1.0, 1.0, ALU.mult, ALU.add, r=["lb%d" % l], w=["oml%d" % l])
            TS_("dve", noml[l], oml[l], -1.0, None, ALU.mult, r=["oml%d" % l], w=["noml%d" % l])
        VK = lambda l: ["gcol%d" % l, "lb%d" % l, "oml%d" % l, "noml%d" % l, "gn%d" % l, "bf%d" % l, "nbf%d" % l]

        psn = [0]

        def bank(lo, hi):
            b = lo + psn[0] % (hi - lo)
            psn[0] += 1
            return b

        def load_w(l, wsrc, c0, c1, Wb, key):
            ncol = c1 - c0
            stg = [A.alloc([ncol], F32) for _ in range(2)]
            src = wsrc.rearrange("(a p) e -> p a e", p=128)
            for dt_ in range(8):
                sb = stg[dt_ % 2]
                sk = "wstg%d" % (dt_ % 2)
                DMA(sb, src[:, dt_, c0:c1], w=[sk], q=("sp" if dt_ % 2 else LQ))
                if key == "wout":
                    CP("act" if dt_ % 2 else "dve", Wb[:, dt_, :], sb, r=[sk], w=[key])
                elif dt_ % 2:
                    gc = gcol[l][:, dt_:dt_ + 1]
                    S.op("act", lambda e, o=Wb[:, dt_, :], i_=sb, g_=gc: e.mul(out=o, in_=i_, mul=g_),
                         [sk, "gcol%d" % l], [key])
                else:
                    TS_("dve", Wb[:, dt_, :], sb, gcol[l][:, dt_:dt_ + 1], None,
                        ALU.mult, r=[sk, "gcol%d" % l], w=[key])

        def phase_A1(l):
            A.mark()
            lfT = {st["name"]: A.alloc([st["T"]], F32) for st in STREAMS}
            A.mark()
            NC1 = 2056
            Wb = A.alloc([8, NC1], BF16)
            load_w(l, w_in[l], 0, NC1, Wb, "W1")
            NXB = 6
            xbuf = [A.alloc([D], F32) for _ in range(NXB)]
            junk = A.alloc([D], F32)
            ssb = A.alloc([8], F32)
            hb = [A.alloc([D], BF16) for _ in range(2)]
            hT = [A.alloc([8, 512], BF16) for _ in range(2)]
            evb = [A.alloc([512], BF16) for _ in range(4)]
            evf = [A.alloc([512], F32) for _ in range(4)]
            sgt = [A.alloc([512], F32) for _ in range(2)]
            sgz = [A.alloc([512], F32) for _ in range(2)]
            zcb = [A.alloc([512], F32) for _ in range(2)]
            cn = {"x": 0, "e": 0, "f": 0, "s": 0}
            jobs = [(st, b) for st in STREAMS for b in range(st["nblk"])]

            hb4 = [A.alloc([D], BF16) for _ in range(4)]

            def norm_a(ji):
                st, b = jobs[ji]
                NB = st["NB"]
                xsrc = st["x0"] if l == 0 else st["x1"]
                for tt in range(NB // 128):
                    t0 = b * NB + tt * 128
                    xi = cn["x"]
                    cn["x"] += 1
                    xb = xbuf[xi % NXB]
                    xk = "xbuf%d" % (xi % NXB)
                    hbb = hb4[tt]
                    hbk = "hb4_%d" % tt
                    sc = ssb[:, (xi % 2) * 4:(xi % 2) * 4 + 1]
                    sck = "ss%d" % (xi % 2)
                    DMA(xb, xsrc[t0:t0 + 128, :], w=[xk], q=LQ)
                    ACT(junk, xb, AF.Square, accum=sc, r=[xk], w=["junk", sck])
                    ACT(sc, sc, AF.Ln, bias=eps_c, scale=1.0 / D, r=[sck, "epsc"], w=[sck])
                    ACT(sc, sc, AF.Exp, scale=-0.5, r=[sck], w=[sck])
                    TS_("dve", hbb, xb, sc, None, ALU.mult, r=[xk, sck], w=[hbk])

            def TRs(ji):
                st, b = jobs[ji]
                NB = st["NB"]
                hk_ = "hT%d" % (ji % 2)
                hTb = hT[ji % 2]
                for tt in range(NB // 128):
                    hbb = hb4[tt]
                    hbk = "hb4_%d" % tt
                    tb = bank(0, 2)
                    for dt_ in range(8):
                        TR(PBh[tb][:, dt_ * 128:(dt_ + 1) * 128], hbb[:, dt_ * 128:(dt_ + 1) * 128], ident_b,
                           r=[hbk, "ident_b"], w=[("ps", tb)])
                    CP("act" if tt % 2 else "dve", hTb[:, :, tt * 128:(tt + 1) * 128],
                       PBh[tb][:, 0:1024].rearrange("p (a b) -> p a b", a=8),
                       r=[("ps", tb)], w=[hk_])
                bsl = slice(b * NB, (b + 1) * NB)
                DMA(st["HT"].rearrange("(a p) t -> p a t", p=128)[:, :, bsl], hTb[:, :, 0:NB], r=[hk_])

            def projs(ji, part):
                st, b = jobs[ji]
                NB, nm = st["NB"], st["name"]
                ntt = NB // 128
                hk_ = "hT%d" % (ji % 2)
                hTb = hT[ji % 2]
                bsl = slice(b * NB, (b + 1) * NB)
                for grp, c0, M in (([("g", 1544 + j * 128, 128) for j in range(4)] +
                                    [("f", 1536, 8)] +
                                    [("q", j * 128, 128) for j in range(4)] +
                                    [("k", 512 + j * 128, 128) for j in range(4)]) if part == "fm" else []):
                    pb = bank(2, 5)
                    for dt_ in range(8):
                        MM(PB[pb][0:M, 0:NB], Wb[:, dt_, c0:c0 + M], hTb[:, dt_, 0:NB], dt_ == 0, dt_ == 7,
                           r=["W1", hk_], w=[("ps", pb)])
                    if grp == "f":
                        sg = sgt[0]
                        ACT(sg[0:8, 0:NB], PB[pb][0:8, 0:NB], AF.Exp, bias=nbfc[l][0:8, :], scale=-1.0,
                            r=[("ps", pb), "nbf%d" % l], w=["sgt0"])
                        ACT(sg[0:8, 0:NB], sg[0:8, 0:NB], AF.Ln, bias=one_c[0:8, :], r=["sgt0", "oc"], w=["sgt0"])
                        TS_("dve", lfT[nm][0:8, bsl], sg[0:8, 0:NB], -1.0, None, ALU.mult, r=["sgt0"], w=["lfT" + nm])
                        continue
                    ev = evb[cn["e"] % 4]
                    ek = "evb%d" % (cn["e"] % 4)
                    cn["e"] += 1
                    j = (c0 % 512) // 128 if grp != "g" else (c0 - 1544) // 128
                    if grp == "q":
                        TS_("dve", ev[:, 0:NB], PB[pb][:, 0:NB], 0.125, None, ALU.mult, r=[("ps", pb)], w=[ek])
                        DMA(st["QT"][2 * j, 0:64, bsl], ev[0:64, 0:NB], r=[ek])
                        DMA(st["QT"][2 * j + 1, 0:64, bsl], ev[64:128, 0:NB], r=[ek])
                    elif grp == "k":
                        CP("act", ev[:, 0:NB], PB[pb][:, 0:NB], r=[("ps", pb)], w=[ek])
                        DMA(st["KT"][2 * j, :, bsl], ev[0:64, 0:NB], r=[ek])
                        DMA(st["KT"][2 * j + 1, :, bsl], ev[64:128, 0:NB], r=[ek])
                    else:
                        gi = cn["s"] % 2
                        cn["s"] += 1
                        sg = sgz[gi]
                        zc = zcb[gi]
                        sgk, zck = "sgz%d" % gi, "zcb%d" % gi
                        CP("dve", zc[:, 0:NB], PB[pb][:, 0:NB], r=[("ps", pb)], w=[zck])
                        ACT(sg[:, 0:NB], zc[:, 0:NB], AF.Exp, scale=-1.0, r=[zck], w=[sgk])
                        ACT(sg[:, 0:NB], sg[:, 0:NB], AF.Ln, bias=one_c, r=[sgk, "oc"], w=[sgk])
                        ACT(sg[:, 0:NB], sg[:, 0:NB], AF.Exp, scale=-1.0, r=[sgk], w=[sgk])
                        TT("dve", ev[:, 0:NB], zc[:, 0:NB], sg[:, 0:NB], ALU.mult, r=[zck, sgk], w=[ek])
                        DMA(st["GT"][j * 128:(j + 1) * 128, bsl], ev[:, 0:NB], r=[ek])
                for tt in (range(ntt) if part == "tm" else []):
                    t0 = b * NB + tt * 128
                    for which, c0 in (("k", 512), ("v", 1024)):
                        pb = bank(5, 8)
                        for dt_ in range(8):
                            MM(PB[pb][:, :], hTb[:, dt_, tt * 128:(tt + 1) * 128], Wb[:, dt_, c0:c0 + 512],
                               dt_ == 0, dt_ == 7, r=["W1", hk_], w=[("ps", pb)])
                        ef = evf[cn["f"] % 4]
                        efk = "evf%d" % (cn["f"] % 4)
                        cn["f"] += 1
                        if which == "k":
                            CP("dve", ef, PB[pb][:, :], r=[("ps", pb)], w=[efk])
                            DMA(st["kout"][l, t0:t0 + 128, :], ef, r=[efk])
                        else:
                            CP("act", ef, PB[pb][:, :], r=[("ps", pb)], w=[efk])
                            DMA(st["vout"][l, t0:t0 + 128, :], ef, r=[efk])
                            ev = evb[cn["e"] % 4]
                            ek = "evb%d" % (cn["e"] % 4)
                            cn["e"] += 1
                            CP("dve", ev, ef, r=[efk], w=[ek])
                            DMA(st["Vb"][t0:t0 + 128, :], ev, r=[ek])

            norm_a(0)
            TRs(0)
            for ji in range(len(jobs)):
                if ji + 1 < len(jobs):
                    norm_a(ji + 1)
                projs(ji, "fm")
                if ji + 1 < len(jobs):
                    TRs(ji + 1)
                projs(ji, "tm")
            A.release()
            return lfT

        def phase_B(l, lfT, negc):
            A.mark()
            lfulls = [A.alloc([PAST + TS], F32) for _ in range(NS)]
            for s_ in range(NS):
                DMA(lfulls[s_][0:8, 0:PAST], clf[l, s_].rearrange("t h -> h t"), w=["lfull%d" % s_], slow=True)
            for st in STREAMS:
                nm, T, past = st["name"], st["T"], st["past"]
                nseq = 1 if past == 0 else NS
                Tq = T // nseq
                L = past + Tq
                lfull = None
                cT = A.alloc([L], F32)
                CH = min(2048, Tq)
                hi = A.alloc([CH], BF16)
                hif = A.alloc([CH], F32)
                lo = A.alloc([CH], BF16)
                ntile = (L + 127) // 128
                tok = tokp if past == 0 else A.alloc([ntile * 8], F32)
                for s in range(nseq):
                    if past:
                        lfull = lfulls[s]
                        CP("act", lfull[0:8, past:L], lfT[nm][0:8, s * Tq:(s + 1) * Tq], r=["lfT" + nm],
                           w=["lfull%d" % s])
                        src, srck = lfull, "lfull%d" % s
                    else:
                        src, srck = lfT[nm], "lfT" + nm
                    SCAN(cT[0:8, :], src[0:8, 0:L], zero_c[0:8, :].to_broadcast([8, L]), 0.0, ALU.add, ALU.add,
                         r=[srck, "zc"], w=["cT"])
                    for c0 in range(0, Tq, CH):
                        sl = slice(past + c0, past + c0 + CH)
                        CP("dve", hi[0:8, :], cT[0:8, sl], r=["cT"], w=["chi"])
                        CP("dve", hif[0:8, :], hi[0:8, :], r=["chi"], w=["chif"])
                        TT("dve", lo[0:8, :], cT[0:8, sl], hif[0:8, :], ALU.subtract, r=["cT", "chif"], w=["clo"])
                        dsl = slice(s * Tq + c0, s * Tq + c0 + CH)
                        DMA(st["QT"][:, 64, dsl], hi[0:8, :], r=["chi"])
                        DMA(st["QT"][:, 65, dsl], lo[0:8, :], r=["clo"])
                    pb = bank(0, 2)
                    for tI in range(ntile):
                        n = min(128, L - tI * 128)
                        TR(PB[pb][0:n, tI * 8:(tI + 1) * 8], cT[0:8, tI * 128:tI * 128 + n], ident_f[0:8, 0:8],
                           r=["cT", "ident_f"], w=[("ps", pb)])
                    ng = negc[nm][s]
                    TS_("dve", ng[:, 0:ntile * 8], PB[pb][:, 0:ntile * 8], -1.0, None, ALU.mult,
                        r=[("ps", pb)], w=["negc%s%d" % (nm, s)])
                    pb = bank(0, 2)
                    ntq = (Tq + 127) // 128
                    for tI in range(ntq):
                        n = min(128, Tq - tI * 128)
                        TR(PB[pb][0:n, tI * 8:(tI + 1) * 8], src[0:8, past + tI * 128:past + tI * 128 + n],
                           ident_f[0:8, 0:8], r=[srck, "ident_f"], w=[("ps", pb)])
                    CP("act", tok[:, 0:ntq * 8], PB[pb][:, 0:ntq * 8], r=[("ps", pb)], w=["tok"])
                    if past == 0:
                        DMA(st["lfout"][l].rearrange("(a p) h -> p a h", p=128),
                            tok[:, 0:ntq * 8].rearrange("p (a h) -> p a h", h=8), r=["tok"], slow=True, lazy=True)
                    else:
                        DMA(st["lfout"][l, s * Tq:(s + 1) * Tq, :], tok[0:Tq, 0:8], r=["tok"], slow=True)
            A.release()

        def phase_A2(l):
            A.mark()
            C0 = 2056
            NC2 = 2048
            Wb = A.alloc([8, NC2], BF16)
            load_w(l, w_in[l], C0, C0 + NC2, Wb, "W2")
            hT = [A.alloc([8, 512], BF16) for _ in range(2)]
            hib = [A.alloc([4, 512], BF16) for _ in range(2)]
            Sst2 = [A.alloc([G, 8, GD], F32) for _ in range(2)]
            Sbf = [A.alloc([8, GD], BF16) for _ in range(2)]
            stl = A.alloc([G, GD], F32)
            ez = [A.alloc([512], F32) for _ in range(3)]
            zq = A.alloc([512], F32)
            zg = A.alloc([512], F32)
            sig = A.alloc([512], F32)
            lf = A.alloc([512], F32)
            bcs = A.alloc([512], F32)
            hkk = A.alloc([512], F32)
            eb = A.alloc([512], F32)
            enb = A.alloc([512], F32)
            t1 = A.alloc([512], F32)
            qt = [A.alloc([512], BF16) for _ in range(2)]
            kt = [A.alloc([512], BF16) for _ in range(2)]
            kh = [A.alloc([512], BF16) for _ in range(2)]
            dch = [A.alloc([8], F32) for _ in range(2)]
            gs = [A.alloc([512], F32) for _ in range(2)]
            khT = A.alloc([4, 128], BF16)
            Am = A.alloc([4, 128], BF16)
            sq = A.alloc([512], BF16)
            rst = A.alloc([512], F32)
            t2 = A.alloc([512], F32)
            yb = [A.alloc([512], BF16) for _ in range(2)]
            cmask = A.alloc([512], F32)
            MSET("pool", cmask, 1.0, w=["cmask"])
            MSET("pool", cmask.rearrange("p (c t) -> p c t", t=64)[:, :, 0:1], 0.0, w=["cmask"])
            cn = {"y": 0}
            lbk = VK(l)
            for st in STREAMS:
                T, NB, nm, past = st["T"], st["NB"], st["name"], st["past"]
                ntt = NB // 128
                nch = NB // 64
                nblk = st["nblk"]
                if past == 0:
                    MSET("pool", Sst2[0][:, :, 0, :], 0.0, w=["S%d_0_0" % g for g in range(G)])
                HTv = st["HT"].rearrange("(a p) t -> p a t", p=128)
                DMA(hT[0][:, :, 0:NB], HTv[:, :, 0:NB], w=["hT0"])

                def pre_block(b):
                    hk_ = "hT%d" % (b % 2)
                    hTb = hT[b % 2]
                    hibb = hib[b % 2]
                    hibk = "hib%d" % (b % 2)
                    if b + 1 < nblk:
                        DMA(hT[(b + 1) % 2][:, :, 0:NB], HTv[:, :, (b + 1) * NB:(b + 2) * NB], w=["hT%d" % ((b + 1) % 2)])
                    for tt in range(ntt):
                        pb = bank(0, 2)
                        for dt_ in range(8):
                            MM(PB[pb][:, :], hTb[:, dt_, tt * 128:(tt + 1) * 128], Wb[:, dt_, 1024:1536],
                               dt_ == 0, dt_ == 7, r=["W2", hk_], w=[("ps", pb)])
                        CP("act", hibb[:, tt, :], PB[pb][:, :], r=[("ps", pb)], w=[hibk])

                def sigm(pb_, e, ek):
                    ACT(e[:, 0:NB], PB[pb_][:, 0:NB], AF.Exp, scale=-1.0, r=[("ps", pb_)], w=[ek])
                    ACT(e[:, 0:NB], e[:, 0:NB], AF.Ln, bias=one_c, r=[ek, "oc"], w=[ek])
                    ACT(e[:, 0:NB], e[:, 0:NB], AF.Exp, scale=-1.0, r=[ek], w=[ek])

                def sigm_sb(z, zk, e, ek):
                    ACT(e[:, 0:NB], z[:, 0:NB], AF.Exp, scale=-1.0, r=[zk], w=[ek])
                    ACT(e[:, 0:NB], e[:, 0:NB], AF.Ln, bias=one_c, r=[ek, "oc"], w=[ek])
                    ACT(e[:, 0:NB], e[:, 0:NB], AF.Exp, scale=-1.0, r=[ek], w=[ek])

                def stageA(b, g):
                    i = g % 2
                    hk_ = "hT%d" % (b % 2)
                    hTb = hT[b % 2]
                    pbank = {"n": 0}

                    def proj(c0):
                        pb_ = (2, 3, 5)[(g * 3 + pbank["n"]) % 3]
                        pbank["n"] += 1
                        for dt_ in range(8):
                            MM(PB[pb_][:, 0:NB], Wb[:, dt_, c0:c0 + 128], hTb[:, dt_, 0:NB], dt_ == 0, dt_ == 7,
                               r=["W2", hk_], w=[("ps", pb_)])
                        return pb_
                    pf = proj(512 + g * 128)
                    pq = proj(0 + g * 128)
                    pg = proj(1536 + g * 128)
                    CP("dve", zq[:, 0:NB], PB[pq][:, 0:NB], r=[("ps", pq)], w=["zq"])
                    CP("dve", zg[:, 0:NB], PB[pg][:, 0:NB], r=[("ps", pg)], w=["zg"])
                    ACT(ez[0][:, 0:NB], PB[pf][:, 0:NB], AF.Exp, scale=-1.0, r=[("ps", pf)], w=["ez0"])
                    ACT(ez[1][:, 0:NB], zq[:, 0:NB], AF.Exp, scale=-1.0, r=["zq"], w=["ez1"])
                    ACT(ez[2][:, 0:NB], zg[:, 0:NB], AF.Exp, scale=-1.0, r=["zg"], w=["ez2"])
                    for j_ in range(3):
                        ACT(ez[j_][:, 0:NB], ez[j_][:, 0:NB], AF.Ln, bias=one_c, r=["ez%d" % j_, "oc"], w=["ez%d" % j_])
                    for j_ in range(3):
                        ACT(ez[j_][:, 0:NB], ez[j_][:, 0:NB], AF.Exp, scale=-1.0, r=["ez%d" % j_], w=["ez%d" % j_])
                    ACT(lf[:, 0:NB], ez[0][:, 0:NB], AF.Ln, bias=lbv[l][:, g:g + 1], scale=oml[l][:, g:g + 1],
                        r=["ez0"] + lbk, w=["lf"])
                    TT("dve", gs[i][:, 0:NB], zg[:, 0:NB], ez[2][:, 0:NB], ALU.mult, r=["zg", "ez2"],
                       w=["gs%d" % i])
                    TS_("dve", hkk[:, 0:NB], ez[0][:, 0:NB], noml[l][:, g:g + 1], oml[l][:, g:g + 1],
                        ALU.mult, ALU.add, r=["ez0"] + lbk, w=["hkk"])
                    SCAN(bcs[:, 0:NB], cmask[:, 0:NB], lf[:, 0:NB], 0.0, ALU.mult, ALU.add,
                         r=["lf", "cmask"], w=["bcs"])
                    ACT(eb[:, 0:NB], bcs[:, 0:NB], AF.Exp, r=["bcs"], w=["eb"])
                    ACT(enb[:, 0:NB], bcs[:, 0:NB], AF.Exp, scale=-1.0, r=["bcs"], w=["enb"])
                    blast = bcs[:, 0:NB].rearrange("p (c t) -> p c t", t=64)[:, :, 63]
                    ACT(dch[i][:, 0:nch], blast, AF.Exp, r=["bcs"], w=["dch%d" % i])
                    TT("dve", ez[1][:, 0:NB], ez[1][:, 0:NB], eb[:, 0:NB], ALU.mult, r=["ez1", "eb"], w=["ez1"])
                    TT("dve", qt[i][:, 0:NB], zq[:, 0:NB], ez[1][:, 0:NB], ALU.mult, r=["zq", "ez1"],
                       w=["qt%d" % i])
                    TT("dve", t1[:, 0:NB], hkk[:, 0:NB], enb[:, 0:NB], ALU.mult, r=["hkk", "enb"], w=["t1"])
                    CP("act", kt[i][:, 0:NB], t1[:, 0:NB], r=["t1"], w=["kt%d" % i])
                    TT("dve", kh[i][:, 0:NB].rearrange("p (c t) -> p c t", t=64),
                       t1[:, 0:NB].rearrange("p (c t) -> p c t", t=64),
                       dch[i][:, 0:nch].unsqueeze(2).to_broadcast([128, nch, 64]), ALU.mult,
                       r=["t1", "dch%d" % i], w=["kh%d" % i])

                def stageB(b, g):
                    i = g % 2
                    hibb = hib[b % 2]
                    hibk = "hib%d" % (b % 2)
                    bsl = slice(b * NB, (b + 1) * NB)
                    Sst = Sst2[b % 2]
                    Snx = Sst2[(b + 1) % 2]

                    def skey(par, slot):
                        return "S%d_%d_%d" % (g, par, slot)
                    qtk, ktk, khk, dk, gk_ = "qt%d" % i, "kt%d" % i, "kh%d" % i, "dch%d" % i, "gs%d" % i
                    pa = bank(0, 2)
                    for tt in range(ntt):
                        tsl = slice(tt * 128, (tt + 1) * 128)
                        MM(PB[pa][:, tsl], kt[i][:, tsl], qt[i][:, tsl], True, True, r=[ktk, qtk], w=[("ps", pa)],
                           skip=True)
                    TT("dve", Am[:, 0:ntt, :], PB[pa][:, 0:NB].rearrange("p (a b) -> p a b", b=128),
                       hmask.unsqueeze(1).to_broadcast([128, ntt, 128]), ALU.mult,
                       r=[("ps", pa), "hmask"], w=["Am"])
                    pt = bank(0, 2)
                    for tt in range(ntt):
                        tsl = slice(tt * 128, (tt + 1) * 128)
                        TR(PBh[pt][:, tsl], kh[i][:, tsl], ident_b, r=[khk, "ident_b"], w=[("ps", pt)])
                    CP("act", khT[:, 0:ntt, :], PBh[pt][:, 0:NB].rearrange("p (a b) -> p a b", b=128),
                       r=[("ps", pt)], w=["khT"])
                    for c in range(nch):
                        tt, half = c // 2, c % 2
                        rows = slice(half * 64, half * 64 + 64)
                        pd = 6 + half
                        MM(PB[pd][:, tt * 128:(tt + 1) * 128], khT[rows, tt, :], hibb[rows, tt, g * 128:(g + 1) * 128],
                           True, True, r=["khT", hibk], w=[("ps", pd)], skip=True)
                    for c in range(nch):
                        kin = skey(b % 2, c)
                        if past:
                            DMA(stl[:, :, :], stin[l, c].rearrange("g k v -> k g v"), w=["stl"])
                            CP("pool", Sst[:, g, c, :], stl[:, g, :], r=["stl"], w=[kin])
                        tt, half = c // 2, c % 2
                        pd = 6 + half
                        CP("act", Sbf[i][:, c, :], Sst[:, g, c, :], r=[kin], w=["Sbf%d_%d" % (i, c)])
                        if c < nch - 1:
                            Sout, kout = Sst[:, g, c + 1, :], skey(b % 2, c + 1)
                        else:
                            Sout, kout = Snx[:, g, 0, :], skey((b + 1) % 2, 0)
                        STT(Sout, Sst[:, g, c, :], dch[i][:, c:c + 1], PB[pd][:, tt * 128:(tt + 1) * 128],
                            ALU.mult, ALU.add, r=[kin, dk, ("ps", pd)], w=[kout])
                        if past:
                            DMA(sso[l, c, g], Sout, r=[kout])
                    if past == 0 and b == nblk - 1:
                        DMA(spo[l, g], Snx[:, g, 0, :], r=[skey((b + 1) % 2, 0)])
                    po = 4
                    for c in range(nch):
                        csl = slice(c * 64, (c + 1) * 64)
                        MM(PB[po][:, csl], Sbf[i][:, c, :], qt[i][:, csl], c == 0, False,
                           r=["Sbf%d_%d" % (i, c), qtk], w=[("ps", po)], skip=True)
                    for tt in range(ntt):
                        tsl = slice(tt * 128, (tt + 1) * 128)
                        MM(PB[po][:, tsl], hibb[:, tt, g * 128:(g + 1) * 128], Am[:, tt, :], False, tt == ntt - 1,
                           r=[hibk, "Am"], w=[("ps", po)], skip=True)
                    ACT(sq[:, 0:NB], PB[po][:, 0:NB], AF.Square, r=[("ps", po)], w=["sq"])
                    pn = bank(0, 2)
                    MM(PB[pn][:, 0:NB], ones_b, sq[:, 0:NB], True, True, r=["ones_b", "sq"], w=[("ps", pn)])
                    ACT(rst[:, 0:NB], PB[pn][:, 0:NB], AF.Ln, bias=eps_c, scale=1.0 / GD, r=[("ps", pn), "epsc"],
                        w=["rst"])
                    ACT(rst[:, 0:NB], rst[:, 0:NB], AF.Exp, scale=-0.5, r=["rst"], w=["rst"])
                    TT("dve", t2[:, 0:NB], PB[po][:, 0:NB], rst[:, 0:NB], ALU.mult, r=[("ps", po), "rst"], w=["t2"])
                    y = yb[cn["y"] % 2]
                    yk = "yb%d" % (cn["y"] % 2)
                    cn["y"] += 1
                    STT(y[:, 0:NB], t2[:, 0:NB], gnv[l][:, g:g + 1], gs[i][:, 0:NB], ALU.mult, ALU.mult,
                        r=["t2", gk_] + lbk, w=[yk])
                    DMA(st["YT"][512 + g * 128:512 + (g + 1) * 128, bsl], y[:, 0:NB], r=[yk])

                jobs = [(b, g) for b in range(nblk) for g in range(G)]

                def emitA(k):
                    b, g = jobs[k]
                    if g == 0:
                        pre_block(b)
                    stageA(b, g)
                emitA(0)
                emitA(1)
                for k in range(len(jobs)):
                    stageB(*jobs[k])
                    if k + 2 < len(jobs):
                        emitA(k + 2)
            A.release()

        def attend_epilogue(po, Nq, Gt, gk, ydst, bufs):
            rc, rch, rcf, rcl, tn, yo, yk = bufs
            RECIP(rc[64:65, 0:Nq], PB[po][64:65, 0:Nq], r=[("ps", po)], w=["rc"])
            CP("dve", rch[64:65, 0:Nq], rc[64:65, 0:Nq], r=["rc"], w=["rch"])
            CP("dve", rcf[64:65, 0:Nq], rch[64:65, 0:Nq], r=["rch"], w=["rcf"])
            TT("dve", rcl[64:65, 0:Nq], rc[64:65, 0:Nq], rcf[64:65, 0:Nq], ALU.subtract, r=["rc", "rcf"], w=["rcl"])
            pbc = 7
            MM(PB[pbc][0:64, 0:Nq], ones_b[64:65, 0:64], rch[64:65, 0:Nq], True, False, r=["ones_b", "rch"],
               w=[("ps", pbc)])
            MM(PB[pbc][0:64, 0:Nq], ones_b[64:65, 0:64], rcl[64:65, 0:Nq], False, True, r=["ones_b", "rcl"],
               w=[("ps", pbc)])
            TT("dve", tn[0:64, 0:Nq], PB[po][0:64, 0:Nq], Gt, ALU.mult, r=[("ps", po), gk], w=["tn"])
            TT("dve", yo[0:64, 0:Nq], tn[0:64, 0:Nq], PB[pbc][0:64, 0:Nq], ALU.mult, r=["tn", ("ps", pbc)], w=[yk])
            DMA(ydst, yo[0:64, 0:Nq], r=[yk], w=["YTdram"])

        def phase_C_prompt(l, negc):
            st = P
            T = st["T"]
            NT = T // 128
            SBQ = 1024
            NSB = T // SBQ
            NDUM = cfg.get("ndum", 0)
            A.mark()
            Ka = [A.alloc([T], BF16) for _ in range(2)]
            Qa = [A.alloc([T], BF16) for _ in range(2)]
            Ga = [A.alloc([T], BF16) for _ in range(2)]
            Va = [A.alloc([NT, 128], BF16) for _ in range(2)]
            pT = [A.alloc([SBQ], BF16) for _ in range(4)]
            rc = A.alloc([SBQ], F32)
            rch = A.alloc([SBQ], BF16)
            rcf = A.alloc([SBQ], F32)
            rcl = A.alloc([SBQ], BF16)
            tn = A.alloc([SBQ], F32)
            yo = [A.alloc([SBQ], BF16) for _ in range(2)]
            for i in range(2):
                MSET("pool", Ka[i][64:128, :], 0.0, w=["Ka%d" % i])
                MSET("pool", Ka[i][64:66, :], 1.0, w=["Ka%d" % i])
                MSET("dve", Qa[i][64:128, :], 0.0, w=["Qa%d" % i])
                MSET("pool", Va[i][:, :, 64:128], 0.0, w=["Va%d" % i])
                MSET("pool", Va[i][:, :, 64:65], 1.0, w=["Va%d" % i])
            ng = negc["p"][0]
            cnt = {"s": 0, "p": 0, "o": 0, "y": 0}
            pend = []

            def loads(h):
                i = h % 2
                DMA(Ka[i][0:64, :], st["KT"][h], w=["Ka%d" % i], q=LQ)
                DMA(Qa[i][0:66, :], st["QT"][h], w=["Qa%d" % i], q=LQ)
                DMA(Ga[i][0:64, :], st["GT"][h * 64:(h + 1) * 64, :], w=["Ga%d" % i], q=LQ)
                DMA(Va[i][:, :, 0:64], st["Vb"].rearrange("(a p) e -> p a e", p=128)[:, :, h * 64:(h + 1) * 64],
                    w=["Va%d" % i], q="sp")

            loads(0)
            for h in range(H):
                i = h % 2
                if h + 1 < H:
                    loads(h + 1)
                K_, Q_, G_, V_ = Ka[i], Qa[i], Ga[i], Va[i]
                kk, qk, gk, vk = "Ka%d" % i, "Qa%d" % i, "Ga%d" % i, "Va%d" % i
                for I2 in range(NSB):
                    oi = 2 + cnt["o"] % 2
                    cnt["o"] += 1
                    O = PB2[oi]
                    ok = ("ps2", oi)
                    nJ = 8 * I2 + 8
                    q0 = I2 * SBQ

                    def qk_step(J):
                        n0 = max(0, J - 8 * I2) * 128
                        diag = J >= 8 * I2
                        si = cnt["s"] % 2
                        cnt["s"] += 1
                        S_ = PB2[si]
                        sk = ("ps2", si)
                        Kt = K_[:, J * 128:(J + 1) * 128]
                        for d in range(NDUM):
                            MM(S_[:, 0:512], Kt, K_[:, 0:512], True, True, r=[kk], w=[sk])
                        if n0 < 512:
                            MM(S_[:, n0:512], Kt, Q_[:, q0 + n0:q0 + 512], True, not diag, r=[kk, qk], w=[sk])
                            if diag:
                                MM(S_[:, n0:n0 + 128], ident_b, maskneg, False, True, r=["ident_b", "maskneg"], w=[sk])
                            MM(S_[:, 512:1024], Kt, Q_[:, q0 + 512:q0 + 1024], True, True, r=[kk, qk], w=[sk])
                        else:
                            MM(S_[:, n0:1024], Kt, Q_[:, q0 + n0:q0 + 1024], True, False, r=[kk, qk], w=[sk])
                            MM(S_[:, n0:n0 + 128], ident_b, maskneg, False, True, r=["ident_b", "maskneg"], w=[sk])
                        pi = cnt["p"] % 4
                        cnt["p"] += 1
                        pt_ = pT[pi]
                        ptk = "pT%d" % pi
                        ACT(pt_[:, n0:1024], S_[:, n0:1024], AF.Exp, bias=ng[:, J * 8 + h:J * 8 + h + 1],
                            r=[sk, "negcp0"], w=[ptk])
                        return (J, n0, pt_, ptk)

                    def pv_step(item):
                        J, n0, pt_, ptk = item
                        last = (J == nJ - 1)
                        if n0 < 512:
                            MM(O[:, n0:512], V_[:, J, :], pt_[:, n0:512], J == 0, last, r=[vk, ptk], w=[ok], skip=True)
                            MM(O[:, 512:1024], V_[:, J, :], pt_[:, 512:1024], J == 0, last, r=[vk, ptk], w=[ok],
                               skip=True)
                        else:
                            MM(O[:, n0:1024], V_[:, J, :], pt_[:, n0:1024], False, last, r=[vk, ptk], w=[ok],
                               skip=True)

                    items = []
                    for J in range(nJ):
                        items.append(qk_step(J))
                        if J >= 2:
                            pv_step(items[J - 2])
                        if J == 5 and pend:
                            pend.pop(0)()
                    for it in items[max(0, nJ - 2):]:
                        pv_step(it)
                    RECIP(rc[64:65, :], O[64:65, :], r=[ok], w=["rc"])
                    TT("dve", tn[0:64, :], O[0:64, :], G_[0:64, q0:q0 + SBQ], ALU.mult, r=[ok, gk], w=["tn"])

                    def stage2(h=h, q0=q0):
                        si = cnt["s"] % 2
                        cnt["s"] += 1
                        bc = PB2[si]
                        bk = ("ps2", si)
                        for hf in range(2):
                            hs = slice(hf * 512, (hf + 1) * 512)
                            MM(bc[0:64, hs], ones_f[64:65, 0:64], rc[64:65, hs], True, True, r=["ones_f", "rc"], w=[bk])
                        yb_ = yo[cnt["y"] % 2]
                        yk = "yo%d" % (cnt["y"] % 2)
                        cnt["y"] += 1
                        TT("dve", yb_[0:64, :], tn[0:64, :], bc[0:64, :], ALU.mult, r=["tn", bk], w=[yk])
                        DMA(st["YT"][h * 64:(h + 1) * 64, q0:q0 + SBQ], yb_[0:64, :], r=[yk])
                    pend.append(stage2)
            while pend:
                pend.pop(0)()
            A.release()

        def phase_C_sample(l, negc):
            st = Sm
            NTc = PAST // 128
            KcT = A.alloc([H, PAST], BF16)
            Vc = A.alloc([NTc, H, 65], BF16)
            CHT = 4
            stg = [A.alloc([CHT, 512], F32) for _ in range(2)]
            kbf = [A.alloc([CHT, 512], BF16) for _ in range(2)]
            Kn = A.alloc([H, 128], BF16)
            Qn = A.alloc([H, 128], BF16)
            Gn = A.alloc([H, 128], BF16)
            Vn = A.alloc([H, 65], BF16)
            pT = [A.alloc([64], BF16) for _ in range(3)]
            rc = A.alloc([512], F32)
            rch = A.alloc([512], BF16)
            rcf = A.alloc([512], F32)
            rcl = A.alloc([512], BF16)
            tn = A.alloc([512], F32)
            yo = [A.alloc([512], BF16) for _ in range(2)]
            MSET("pool", KcT[64:66, :, :], 1.0, w=["KcT"])
            MSET("pool", Vc[:, :, :, 64:65], 1.0, w=["Vc"])
            MSET("pool", Kn[64:66, :, :], 1.0, w=["Kn"])
            MSET("pool", Vn[:, :, 64:65], 1.0, w=["Vn"])
            for h in range(H):
                DMA(Kn[0:64, h, :], st["KT"][h], w=["Kn"])
                DMA(Qn[0:66, h, :], st["QT"][h], w=["Qn"])
                DMA(Gn[0:64, h, :], st["GT"][h * 64:(h + 1) * 64, :], w=["Gn"])
            yield
            pn = 0
            yn = 0
            sn = 0
            for s in range(NS):
                ng = negc["s"][s]
                ngk = "negcs%d" % s
                DMA(Vn[0:64, :, 0:64], st["Vb"][s * 64:(s + 1) * 64, :].rearrange("t (h e) -> t h e", h=H), w=["Vn"])
                for c in range(NTc // CHT):
                    for src, kind in ((ck, "k"), (cv, "v")):
                        sb = stg[sn % 2]
                        sk = "cstg%d" % (sn % 2)
                        kb = kbf[sn % 2]
                        kbk = "kbf%d" % (sn % 2)
                        sn += 1
                        DMA(sb, src[l, s, c * CHT * 128:(c + 1) * CHT * 128, :].rearrange("(a p) e -> p a e", p=128),
                            w=[sk])
                        if kind == "v":
                            CP("act" if c % 2 else "dve", Vc[:, c * CHT:(c + 1) * CHT, :, 0:64],
                               sb.rearrange("p a (h e) -> p a h e", h=H), r=[sk], w=["Vc"])
                            continue
                        CP("dve", kb, sb, r=[sk], w=[kbk])
                        for h in range(H):
                            pb = bank(0, 2)
                            for a in range(CHT):
                                TR(PBh[pb][0:64, a * 128:(a + 1) * 128], kb[:, a, h * 64:(h + 1) * 64], ident_b,
                                   r=[kbk, "ident_b"], w=[("ps", pb)])
                            CP("act" if h % 2 else "dve", KcT[0:64, h, c * CHT * 128:(c + 1) * CHT * 128],
                               PBh[pb][0:64, 0:CHT * 128], r=[("ps", pb)], w=["KcT"])
                yield
                for h in range(H):
                    po = 3 + (h % 2)
                    nJ = NTc + 1
                    qsl = slice(s * 64, (s + 1) * 64)

                    def qk_step(J):
                        nonlocal pn
                        ps_ = pn % 3
                        pn += 1
                        pt_ = pT[ps_]
                        ptk = "pTs%d" % ps_
                        if J < NTc:
                            MM(PB[ps_][:, 0:64], KcT[0:66, h, J * 128:(J + 1) * 128], Qn[0:66, h, qsl], True, True,
                               r=["KcT", "Qn"], w=[("ps", ps_)])
                            ACT(pt_[:, 0:64], PB[ps_][:, 0:64], AF.Exp, bias=ng[:, J * 8 + h:J * 8 + h + 1],
                                r=[("ps", ps_), ngk], w=[ptk])
                        else:
                            MM(PB[ps_][0:64, 0:64], Kn[0:66, h, qsl], Qn[0:66, h, qsl], True, False,
                               r=["Kn", "Qn"], w=[("ps", ps_)])
                            MM(PB[ps_][0:64, 0:64], ident_b[0:64, 0:64], maskneg[0:64, 0:64], False, True,
                               r=["ident_b", "maskneg"], w=[("ps", ps_)])
                            ACT(pt_[0:64, 0:64], PB[ps_][0:64, 0:64], AF.Exp, bias=ng[0:64, J * 8 + h:J * 8 + h + 1],
                                r=[("ps", ps_), ngk], w=[ptk])
                        return (J, pt_, ptk)

                    def pv_step(item):
                        J, pt_, ptk = item
                        if J < NTc:
                            MM(PB[po][0:65, 0:64], Vc[:, J, h, :], pt_[:, 0:64], J == 0, False, r=["Vc", ptk],
                               w=[("ps", po)], skip=True)
                        else:
                            MM(PB[po][0:65, 0:64], Vn[0:64, h, :], pt_[0:64, 0:64], False, True, r=["Vn", ptk],
                               w=[("ps", po)], skip=True)

                    prev = None
                    for J in range(nJ):
                        cur = qk_step(J)
                        if prev is not None:
                            pv_step(prev)
                        prev = cur
                        yield
                    pv_step(prev)
                    attend_epilogue(po, 64, Gn[0:64, h, qsl], "Gn", st["YT"][h * 64:(h + 1) * 64, qsl],
                                    (rc, rch, rcf, rcl, tn, yo[yn % 2], "yos%d" % (yn % 2)))
                    yn += 1
                    yield

        def phase_D_setup(l):
            ctx = {}
            ctx["Wo"] = A.alloc([8, D], BF16)
            load_w(l, w_out[l], 0, D, ctx["Wo"], "wout")
            ctx["yT"] = [A.alloc([8, 512], BF16) for _ in range(2)]
            ctx["xbuf"] = [A.alloc([D], F32) for _ in range(3)]
            ctx["xo"] = [A.alloc([D], F32) for _ in range(2)]
            ctx["junk"] = A.alloc([D], F32)
            ctx["ssb"] = A.alloc([8], F32)
            ctx["xi"] = 0
            ctx["yi"] = 0
            return ctx

        def phase_D_run(l, ctx, st):
            Wo, yT, xbuf, xo, junk, ssb = ctx["Wo"], ctx["yT"], ctx["xbuf"], ctx["xo"], ctx["junk"], ctx["ssb"]
            last = (l == 1)
            T, NB = st["T"], st["NB"]
            ntt = NB // 128
            xsrc = st["x0"] if l == 0 else st["x1"]
            YTv = st["YT"].rearrange("(a p) t -> p a t", p=128)
            ydep = ["YTdram"] if st is Sm else []
            y0 = ctx["yi"]
            DMA(yT[y0 % 2][:, :, 0:NB], YTv[:, :, 0:NB], r=ydep, w=["yT%d" % (y0 % 2)])
            for b in range(st["nblk"]):
                yi = ctx["yi"]
                ctx["yi"] += 1
                yk = "yT%d" % (yi % 2)
                yTb = yT[yi % 2]
                if b + 1 < st["nblk"]:
                    DMA(yT[(yi + 1) % 2][:, :, 0:NB], YTv[:, :, (b + 1) * NB:(b + 2) * NB], r=ydep,
                        w=["yT%d" % ((yi + 1) % 2)])
                for tt in range(ntt):
                    t0 = b * NB + tt * 128
                    xi = ctx["xi"]
                    ctx["xi"] += 1
                    xb = xbuf[xi % 3]
                    xk = "xbuf%d" % (xi % 3)
                    xob = xo[xi % 2]
                    xok = "xo%d" % (xi % 2)
                    sc = ssb[:, (xi % 2) * 4:(xi % 2) * 4 + 1]
                    sck = "ss%d" % (xi % 2)
                    DMA(xb, xsrc[t0:t0 + 128, :], w=[xk], q=LQ)
                    for half in range(2):
                        pb = bank(5, 7)
                        for et in range(8):
                            MM(PB[pb][:, :], yTb[:, et, tt * 128:(tt + 1) * 128],
                               Wo[:, et, half * 512:(half + 1) * 512], et == 0, et == 7,
                               r=[yk, "wout"], w=[("ps", pb)])
                        TT("dve", xob[:, half * 512:(half + 1) * 512], PB[pb][:, :],
                           xb[:, half * 512:(half + 1) * 512], ALU.add, r=[("ps", pb), xk], w=[xok])
                        yield
                    if not last:
                        DMA(st["x1"][t0:t0 + 128, :], xob, r=[xok])
                    else:
                        ACT(junk, xob, AF.Square, accum=sc, r=[xok], w=["junk", sck])
                        ACT(sc, sc, AF.Ln, bias=eps_c, scale=1.0 / D, r=[sck, "epsc"], w=[sck])
                        ACT(sc, sc, AF.Exp, scale=-0.5, r=[sck], w=[sck])
                        STT(xb, xob, sc, fgb, ALU.mult, ALU.mult, r=[xok, sck, "fgb"], w=[xk])
                        DMA(st["y"][t0:t0 + 128, :], xb, r=[xk])

        tokp = A.alloc([(SEQ // 128) * 8], F32)
        negc = {
            "p": [A.alloc([(SEQ // 128) * 8], F32)],
            "s": [A.alloc([((PAST + TS + 127) // 128) * 8], F32) for _ in range(NS)],
        }
        for l in range(2):
            lfT = phase_A1(l)
            S.barrier()
            phase_B(l, lfT, negc)
            A.release()
            S.barrier()
            phase_A2(l)
            S.barrier()
            phase_C_prompt(l, negc)
            S.barrier()
            A.mark()
            gC = phase_C_sample(l, negc)
            next(gC)
            dctx = phase_D_setup(l)
            gD = phase_D_run(l, dctx, P)
            c_alive, d_alive = True, True
            while c_alive or d_alive:
                for _ in range(2):
                    if c_alive:
                        try:
                            next(gC)
                        except StopIteration:
                            c_alive = False
                if d_alive:
                    try:
                        next(gD)
                    except StopIteration:
                        d_alive = False
            for _ in phase_D_run(l, dctx, Sm):
                pass
            A.release()
            S.barrier()
        S.emit()
    return nc


CFG_FULL = dict(SEQ=8192, PAST=2048)
_CONSTS = None


def _consts():
    ident = np.eye(128, dtype=np.float32)
    k = np.arange(128)[:, None]
    q = np.arange(128)[None, :]
    maskneg = np.where(k <= q, 0.0, NEG).astype(np.float32)
    hmask = ((k <= q) & ((k // 64) == (q // 64))).astype(np.float32)
    return ident, maskneg, hmask


def run(cfg, x_prompt, x_sample, cache_k, cache_v, cache_logf, state_hgrn,
        norm_g, w_in, fox_b_f, hg_lower, hg_norm_g, w_out, final_g, n_cores=8):
    SEQ, PAST = cfg["SEQ"], cfg["PAST"]
    nc = build(cfg)
    ident, maskneg, hmask = _consts()
    f = lambda a: np.ascontiguousarray(np.asarray(a, dtype=np.float32))
    in_maps = []
    for c in range(n_cores):
        sl = slice(2 * c, 2 * c + 2)
        in_maps.append({
            "xp": f(x_prompt[c]),
            "xs": f(x_sample[sl]).reshape(128, D),
            "ck": f(cache_k[:, sl]).reshape(2, 2, PAST, H * HD),
            "cv": f(cache_v[:, sl]).reshape(2, 2, PAST, H * HD),
            "clf": f(cache_logf[:, sl]),
            "st": f(state_hgrn[:, sl]),
            "norm_g": f(norm_g), "w_in": f(w_in), "fox_b_f": f(fox_b_f), "hg_lower": f(hg_lower),
            "hg_norm_g": f(hg_norm_g), "w_out": f(w_out), "final_g": f(final_g),
            "c_ident": ident, "c_maskneg": maskneg, "c_hmask": hmask,
        })
    res = run_bass_kernel_spmd(nc, in_maps, core_ids=list(range(n_cores)))
    R = res.results
    B = n_cores
    if cfg.get("debug"):
        global DEBUG_R
        DEBUG_R = R
    y_prompt = np.stack([R[c]["yp"] for c in range(B)])
    y_sample = np.concatenate([R[c]["ys"].reshape(2, 64, D) for c in range(B)])
    k_prompt = np.stack([R[c]["kp"].reshape(2, SEQ, H, HD) for c in range(B)], axis=1)
    v_prompt = np.stack([R[c]["vp"].reshape(2, SEQ, H, HD) for c in range(B)], axis=1)
    logf_prompt = np.stack([R[c]["lfp"] for c in range(B)], axis=1)
    hgrn_prompt = np.stack([R[c]["sp"] for c in range(B)], axis=1)
    k_sample = np.concatenate([R[c]["ks"].reshape(2, 2, 64, H, HD) for c in range(B)], axis=1)
    v_sample = np.concatenate([R[c]["vs"].reshape(2, 2, 64, H, HD) for c in range(B)], axis=1)
    logf_sample = np.concatenate([R[c]["lfs"].reshape(2, 2, 64, H) for c in range(B)], axis=1)
    hgrn_sample = np.concatenate([R[c]["ss"] for c in range(B)], axis=1)
    outs = (y_prompt, y_sample, k_prompt, v_prompt, logf_prompt, hgrn_prompt,
            k_sample, v_sample, logf_sample, hgrn_sample)
    return tuple(np.ascontiguousarray(o, dtype=np.float32) for o in outs)


def kernel(x_prompt, x_sample, cache_k, cache_v, cache_logf, state_hgrn,
           norm_g, w_in, fox_b_f, hg_lower, hg_norm_g, w_out, final_g):
    return run(CFG_FULL, x_prompt, x_sample, cache_k, cache_v, cache_logf, state_hgrn,
               norm_g, w_in, fox_b_f, hg_lower, hg_norm_g, w_out, final_g)
```

```python
import numpy as np
from contextlib import ExitStack
import concourse.bass as bass
import concourse.mybir as mybir
from concourse.bass_utils import run_bass_kernel_spmd

F32 = mybir.dt.float32
BF16 = mybir.dt.bfloat16
AF = mybir.ActivationFunctionType
ALU = mybir.AluOpType

D = 1024
DIN = 4104
H = 8
HD = 64
G = 4
GD = 128
EPS = 1e-6
NEG = -30000.0


class Sched:
    ENGS = ("pe", "act", "dve", "pool", "sp")
    EPOCH = 12000
    NDMA = 14

    def __init__(self, nc, es):
        self.nc = nc
        self.es = es
        self.streams = {e: [] for e in self.ENGS}
        self.n = {e: 0 for e in self.ENGS}
        self.sems = {}
        self.known = {}
        self.lastw = {}
        self.readers = {}
        self.pending = {e: [] for e in self.ENGS}
        self.last_tok = {}
        self.dma_last = {}
        self.ndma = {}
        self.lazy = set()

    def sem(self, key):
        if key not in self.sems:
            nm = "s_" + "_".join(str(k) for k in key)
            self.sems[key] = self.es.enter_context(self.nc.semaphore(nm))
        return self.sems[key]

    def op(self, eng, fn, r=(), w=(), dma=False, lazy=False):
        waits = {}
        me = eng + "_dma" if dma else eng

        def need(tok):
            if tok is None:
                return
            teng, key, val = tok
            if teng == eng == "pe":
                return
            if self.known.get((eng, key), 0) >= val:
                return
            if waits.get(key, 0) < val:
                waits[key] = val

        for tok in self.pending[eng]:
            need(tok)
        self.pending[eng] = []
        for k in r:
            need(self.lastw.get(k))
            if isinstance(k, tuple) and k[0] in ("ps", "ps2"):
                for t in self.readers.get(k, ()):
                    if t[0] != me:
                        need(t)
        for k in w:
            need(self.lastw.get(k))
            for t in self.readers.get(k, ()):
                if t[0] != me or dma:
                    need(t)
        if dma:
            i = self.ndma.get(eng, 0)
            self.ndma[eng] = i + 1
            slot = i % self.NDMA
            key = (me, slot)
            val = 16 * (i // self.NDMA + 1)
            if val > 16 and self.known.get((eng, key), 0) < val - 16:
                waits[key] = max(waits.get(key, 0), val - 16)
            inc = 16
            self.dma_last[key] = val
            if lazy:
                self.lazy.add((key, val))
        else:
            i = self.n[eng]
            self.n[eng] += 1
            key = (eng, i // self.EPOCH)
            val = i % self.EPOCH + 1
            inc = 1
            self.last_tok[eng] = (eng, key, val)
        self.sem(key)
        for k2 in waits:
            self.sem(k2)
        tok = (me, key, val)
        for k2, v2 in waits.items():
            self.known[(eng, k2)] = v2
        self.streams[eng].append((list(waits.items()), fn, key, inc))
        for k in r:
            lst = self.readers.setdefault(k, [])
            if not dma:
                lst[:] = [t for t in lst if t[0] != me]
            lst.append(tok)
        for k in w:
            self.lastw[k] = tok
            self.readers[k] = []
        return tok

    def barrier(self):
        toks = []
        for e in ("pe", "act", "dve", "pool"):
            if e in self.last_tok:
                toks.append(self.last_tok[e])
        for key, val in self.dma_last.items():
            if (key, val) in self.lazy:
                continue
            toks.append((key[0], key, val))
        for e in self.ENGS:
            self.pending[e] = [t for t in toks if t[0] != e]

    def emit(self):
        nc = self.nc
        streams = self.streams
        sems = self.sems
        dma_last = dict(self.dma_last)

        def mk(name):
            def body(eng):
                for waits, fn, key, inc in streams[name]:
                    for k, v in waits:
                        eng.wait_ge(sems[k], v)
                    ins = fn(eng)
                    ins.then_inc(sems[key], inc)
                if name == "sp":
                    for key, val in dma_last.items():
                        eng.wait_ge(sems[key], val)
            return body

        with nc.Block() as block:
            block.tensor(mk("pe"))
            block.scalar(mk("act"))
            block.vector(mk("dve"))
            block.gpsimd(mk("pool"))
            block.sync(mk("sp"))


class Arena:
    def __init__(self, nc, es, words):
        self.t = es.enter_context(nc.sbuf_tensor("arena", [128, words], F32))
        self.words = words
        self.off = 0
        self.marks = []

    def alloc(self, free_shape, dt):
        n = int(np.prod(free_shape))
        bpe = 4 if dt == F32 else 2
        w = (n * bpe + 3) // 4
        w = (w + 7) // 8 * 8
        assert self.off + w <= self.words, ("arena overflow", self.off, w, self.words)
        ap = self.t[:, self.off:self.off + w]
        self.off += w
        if dt != F32:
            ap = ap.bitcast(dt)
        ap = ap[:, 0:n]
        if len(free_shape) == 2:
            ap = ap.rearrange("p (a b) -> p a b", a=free_shape[0])
        elif len(free_shape) == 3:
            ap = ap.rearrange("p (a b c) -> p a b c", a=free_shape[0], b=free_shape[1])
        return ap

    def mark(self):
        self.marks.append(self.off)

    def release(self):
        self.off = self.marks.pop()


def build(cfg):
    SEQ = cfg["SEQ"]
    PAST = cfg["PAST"]
    TS = 64
    NS = 2
    TSAMP = NS * TS
    nc = bass.Bass("TRN2", target_bir_lowering=False)

    def din(name, shape):
        return nc.dram_tensor(name, list(shape), F32, kind="ExternalInput").ap()

    def dout(name, shape):
        return nc.dram_tensor(name, list(shape), F32, kind="ExternalOutput").ap()

    def dscr(name, shape, dt):
        if cfg.get("debug"):
            return nc.dram_tensor(name, list(shape), dt, kind="ExternalOutput").ap()
        return nc.dram_tensor(name, list(shape), dt).ap()

    xp = din("xp", [SEQ, D])
    xs = din("xs", [TSAMP, D])
    ck = din("ck", [2, NS, PAST, H * HD])
    cv = din("cv", [2, NS, PAST, H * HD])
    clf = din("clf", [2, NS, PAST, H])
    stin = din("st", [2, NS, G, GD, GD])
    norm_g = din("norm_g", [2, D])
    w_in = din("w_in", [2, D, DIN])
    fox_b_f = din("fox_b_f", [2, H])
    hg_lower = din("hg_lower", [2, G * GD])
    hg_norm_g = din("hg_norm_g", [2, G * GD])
    w_out = din("w_out", [2, D, D])
    final_g = din("final_g", [D])
    c_ident = din("c_ident", [128, 128])
    c_maskneg = din("c_maskneg", [128, 128])
    c_hmask = din("c_hmask", [128, 128])

    yp = dout("yp", [SEQ, D])
    ys = dout("ys", [TSAMP, D])
    kp = dout("kp", [2, SEQ, H * HD])
    vp = dout("vp", [2, SEQ, H * HD])
    lfp = dout("lfp", [2, SEQ, H])
    spo = dout("sp", [2, G, GD, GD])
    ks = dout("ks", [2, TSAMP, H * HD])
    vs = dout("vs", [2, TSAMP, H * HD])
    lfs = dout("lfs", [2, TSAMP, H])
    sso = dout("ss", [2, NS, G, GD, GD])

    def mkstream(nm, T, NB):
        return dict(
            name=nm, T=T, NB=NB, nblk=T // NB,
            x1=dscr(nm + "_x1", [T, D], F32),
            QT=dscr(nm + "_QT", [H, HD + 2, T], BF16),
            KT=dscr(nm + "_KT", [H, HD, T], BF16),
            Vb=dscr(nm + "_Vb", [T, H * HD], BF16),
            GT=dscr(nm + "_GT", [H * HD, T], BF16),
            YT=dscr(nm + "_YT", [D, T], BF16),
            HT=dscr(nm + "_HT", [D, T], BF16),
        )

    P = mkstream("p", SEQ, 512)
    Sm = mkstream("s", TSAMP, 128)
    P.update(x0=xp, y=yp, kout=kp, vout=vp, lfout=lfp, past=0)
    Sm.update(x0=xs, y=ys, kout=ks, vout=vs, lfout=lfs, past=PAST)
    STREAMS = (P, Sm)

    es = ExitStack()
    with es:
        S = Sched(nc, es)
        PB2 = [es.enter_context(nc.psum_tensor("pb%d" % i, [128, 1024], F32)) for i in range(4)]
        PB = []
        for i in range(4):
            PB.append(PB2[i][:, 0:512])
            PB.append(PB2[i][:, 512:1024])
        PBh = [pb.bitcast(BF16) for pb in PB]
        cst = es.enter_context(nc.sbuf_tensor("cst", [128, 2048], F32))
        A = Arena(nc, es, 47000)

        LQ = cfg.get("loadq", "pool")

        def DMA(out, in_, r=(), w=(), slow=False, q="sp", lazy=False):
            if slow:
                S.op(q, lambda e: e.dma_start(out=out, in_=in_, allow_slow_non_contiguous=True), r, w, dma=True,
                     lazy=lazy)
            else:
                S.op(q, lambda e: e.dma_start(out=out, in_=in_), r, w, dma=True, lazy=lazy)

        def MM(out, lhsT, rhs, start, stop, r=(), w=(), skip=False):
            if skip:
                S.op("pe", lambda e: e.matmul(out, lhsT, rhs, start=start, stop=stop, skip_group_check=True), r, w)
            else:
                S.op("pe", lambda e: e.matmul(out, lhsT, rhs, start=start, stop=stop), r, w)

        def TR(out, in_, ident, r=(), w=()):
            S.op("pe", lambda e: e.transpose(out, in_, ident), r, w)

        def ACT(out, in_, func, bias=None, scale=None, accum=None, r=(), w=()):
            kw = {}
            if bias is not None:
                kw["bias"] = bias
            if scale is not None:
                kw["scale"] = scale
            if accum is not None:
                kw["accum_out"] = accum
            S.op("act", lambda e: e.activation(out=out, in_=in_, func=func, **kw), r, w)

        def TS_(eng, out, in0, s1, s2, op0, op1=None, r=(), w=()):
            if op1 is None:
                S.op(eng, lambda e: e.tensor_scalar(out=out, in0=in0, scalar1=s1, scalar2=None, op0=op0), r, w)
            else:
                S.op(eng, lambda e: e.tensor_scalar(out=out, in0=in0, scalar1=s1, scalar2=s2, op0=op0, op1=op1), r, w)

        def TT(eng, out, in0, in1, op, r=(), w=()):
            S.op(eng, lambda e: e.tensor_tensor(out=out, in0=in0, in1=in1, op=op), r, w)

        def STT(out, in0, scalar, in1, op0, op1, r=(), w=()):
            S.op("dve", lambda e: e.scalar_tensor_tensor(out=out, in0=in0, scalar=scalar, in1=in1, op0=op0, op1=op1), r, w)

        def CP(eng, out, in_, r=(), w=()):
            if eng == "act":
                S.op("act", lambda e: e.copy(out=out, in_=in_), r, w)
            else:
                S.op(eng, lambda e: e.tensor_copy(out=out, in_=in_), r, w)

        def MSET(eng, ap, val, w=()):
            S.op(eng, lambda e: e.memset(ap, val), (), w)

        def RECIP(out, in_, r=(), w=()):
            S.op("dve", lambda e: e.reciprocal(out=out, in_=in_), r, w)

        def SCAN(out, d0, d1, init, op0, op1, r=(), w=()):
            S.op("dve", lambda e: e.tensor_tensor_scan(out=out, data0=d0, data1=d1, initial=init, op0=op0, op1=op1), r, w)

        ident_f = cst[:, 0:128]
        cbf = cst[:, 128:128 + 256].bitcast(BF16)
        ident_b = cbf[:, 0:128]
        maskneg = cbf[:, 128:256]
        hmask = cbf[:, 256:384]
        ones_b = cbf[:, 384:512]
        ctmp = cst[:, 384:768]
        vec = cst[:, 768:1024]
        fgb = cst[:, 1024:2048]
        zero_c = vec[:, 200:201]
        one_c = vec[:, 201:202]

        DMA(ident_f, c_ident, w=["ident_f"])
        DMA(ctmp[:, 0:128], c_maskneg, w=["ctmp0"])
        DMA(ctmp[:, 128:256], c_hmask, w=["ctmp1"])
        CP("dve", ident_b, ident_f, r=["ident_f"], w=["ident_b"])
        CP("dve", maskneg, ctmp[:, 0:128], r=["ctmp0"], w=["maskneg"])
        CP("dve", hmask, ctmp[:, 128:256], r=["ctmp1"], w=["hmask"])
        MSET("dve", ones_b, 1.0, w=["ones_b"])
        ones_f = es.enter_context(nc.sbuf_tensor("ones_f", [128, 64], F32))
        MSET("dve", ones_f[:, :], 1.0, w=["ones_f"])
        MSET("dve", zero_c, 0.0, w=["zc"])
        MSET("dve", one_c, 1.0, w=["oc"])
        eps_c = vec[:, 202:203]
        MSET("dve", eps_c, EPS, w=["epsc"])
        DMA(fgb, final_g.partition_broadcast(128), w=["fgb"])
        CK = ["ident_f", "ident_b", "maskneg", "hmask", "ones_b", "zc", "oc", "fgb"]

        def vslot(i, n):
            return vec[:, i:i + n]
        gcol = [vslot(0, 8), vslot(8, 8)]
        hl = [vslot(16, 4), vslot(20, 4)]
        lbv = [vslot(24, 4), vslot(28, 4)]
        oml = [vslot(32, 4), vslot(36, 4)]
        noml = [vslot(40, 4), vslot(44, 4)]
        gnv = [vslot(48, 4), vslot(52, 4)]
        bfc = [vslot(56, 1), vslot(57, 1)]
        nbfc = [vslot(58, 1), vslot(59, 1)]
        for l in range(2):
            DMA(gcol[l], norm_g[l].rearrange("(a p) -> p a", p=128), w=["gcol%d" % l], slow=True)
            DMA(hl[l], hg_lower[l].rearrange("(g p) -> p g", p=128), w=["hl%d" % l], slow=True)
            DMA(gnv[l], hg_norm_g[l].rearrange("(g p) -> p g", p=128), w=["gn%d" % l], slow=True)
            DMA(bfc[l][0:8, :], fox_b_f[l].rearrange("(h o) -> h o", o=1), w=["bf%d" % l], slow=True)
        for l in range(2):
            TS_("dve", nbfc[l][0:8, :], bfc[l][0:8, :], -1.0, None, ALU.mult, r=["bf%d" % l], w=["nbf%d" % l])
        MSET("dve", lbv[0], 0.0, w=["lb0"])
        TT("dve", lbv[1], hl[1], hl[0], ALU.subtract, r=["hl0", "hl1"], w=["lb1"])
        ACT(lbv[1], lbv[1], AF.Sigmoid, r=["lb1"], w=["lb1"])
        for l in range(2):
            TS_("dve", oml[l], lbv[l], -1.0, 1.0, ALU.mult, ALU.add, r=["lb%d" % l], w=["oml%d" % l])
            TS_("dve", noml[l], oml[l], -1.0, None, ALU.mult, r=["oml%d" % l], w=["noml%d" % l])
        VK = lambda l: ["gcol%d" % l, "lb%d" % l, "oml%d" % l, "noml%d" % l, "gn%d" % l, "bf%d" % l, "nbf%d" % l]

        psn = [0]

        def bank(lo, hi):
            b = lo + psn[0] % (hi - lo)
            psn[0] += 1
            return b

        def load_w(l, wsrc, c0, c1, Wb, key):
            ncol = c1 - c0
            stg = [A.alloc([ncol], F32) for _ in range(2)]
            src = wsrc.rearrange("(a p) e -> p a e", p=128)
            for dt_ in range(8):
                sb = stg[dt_ % 2]
                sk = "wstg%d" % (dt_ % 2)
                DMA(sb, src[:, dt_, c0:c1], w=[sk], q=("sp" if dt_ % 2 else LQ))
                if key == "wout":
                    CP("act" if dt_ % 2 else "dve", Wb[:, dt_, :], sb, r=[sk], w=[key])
                elif dt_ % 2:
                    gc = gcol[l][:, dt_:dt_ + 1]
                    S.op("act", lambda e, o=Wb[:, dt_, :], i_=sb, g_=gc: e.mul(out=o, in_=i_, mul=g_),
                         [sk, "gcol%d" % l], [key])
                else:
                    TS_("dve", Wb[:, dt_, :], sb, gcol[l][:, dt_:dt_ + 1], None,
                        ALU.mult, r=[sk, "gcol%d" % l], w=[key])

        def phase_A1(l):
            A.mark()
            lfT = {st["name"]: A.alloc([st["T"]], F32) for st in STREAMS}
            A.mark()
            NC1 = 2056
            Wb = A.alloc([8, NC1], BF16)
            load_w(l, w_in[l], 0, NC1, Wb, "W1")
            NXB = 6
            xbuf = [A.alloc([D], F32) for _ in range(NXB)]
            junk = A.alloc([D], F32)
            ssb = A.alloc([8], F32)
            hb = [A.alloc([D], BF16) for _ in range(2)]
            hT = [A.alloc([8, 512], BF16) for _ in range(2)]
            evb = [A.alloc([512], BF16) for _ in range(4)]
            evf = [A.alloc([512], F32) for _ in range(4)]
            sgt = [A.alloc([512], F32) for _ in range(2)]
            sgz = [A.alloc([512], F32) for _ in range(2)]
            zcb = [A.alloc([512], F32) for _ in range(2)]
            cn = {"x": 0, "e": 0, "f": 0, "s": 0}
            jobs = [(st, b) for st in STREAMS for b in range(st["nblk"])]

            hb4 = [A.alloc([D], BF16) for _ in range(4)]

            def norm_a(ji):
                st, b = jobs[ji]
                NB = st["NB"]
                xsrc = st["x0"] if l == 0 else st["x1"]
                for tt in range(NB // 128):
                    t0 = b * NB + tt * 128
                    xi = cn["x"]
                    cn["x"] += 1
                    xb = xbuf[xi % NXB]
                    xk = "xbuf%d" % (xi % NXB)
                    hbb = hb4[tt]
                    hbk = "hb4_%d" % tt
                    sc = ssb[:, (xi % 2) * 4:(xi % 2) * 4 + 1]
                    sck = "ss%d" % (xi % 2)
                    DMA(xb, xsrc[t0:t0 + 128, :], w=[xk], q=LQ)
                    ACT(junk, xb, AF.Square, accum=sc, r=[xk], w=["junk", sck])
                    ACT(sc, sc, AF.Ln, bias=eps_c, scale=1.0 / D, r=[sck, "epsc"], w=[sck])
                    ACT(sc, sc, AF.Exp, scale=-0.5, r=[sck], w=[sck])
                    TS_("dve", hbb, xb, sc, None, ALU.mult, r=[xk, sck], w=[hbk])

            def TRs(ji):
                st, b = jobs[ji]
                NB = st["NB"]
                hk_ = "hT%d" % (ji % 2)
                hTb = hT[ji % 2]
                for tt in range(NB // 128):
                    hbb = hb4[tt]
                    hbk = "hb4_%d" % tt
                    tb = bank(0, 2)
                    for dt_ in range(8):
                        TR(PBh[tb][:, dt_ * 128:(dt_ + 1) * 128], hbb[:, dt_ * 128:(dt_ + 1) * 128], ident_b,
                           r=[hbk, "ident_b"], w=[("ps", tb)])
                    CP("act" if tt % 2 else "dve", hTb[:, :, tt * 128:(tt + 1) * 128],
                       PBh[tb][:, 0:1024].rearrange("p (a b) -> p a b", a=8),
                       r=[("ps", tb)], w=[hk_])
                bsl = slice(b * NB, (b + 1) * NB)
                DMA(st["HT"].rearrange("(a p) t -> p a t", p=128)[:, :, bsl], hTb[:, :, 0:NB], r=[hk_])

            def projs(ji, part):
                st, b = jobs[ji]
                NB, nm = st["NB"], st["name"]
                ntt = NB // 128
                hk_ = "hT%d" % (ji % 2)
                hTb = hT[ji % 2]
                bsl = slice(b * NB, (b + 1) * NB)
                for grp, c0, M in (([("g", 1544 + j * 128, 128) for j in range(4)] +
                                    [("f", 1536, 8)] +
                                    [("q", j * 128, 128) for j in range(4)] +
                                    [("k", 512 + j * 128, 128) for j in range(4)]) if part == "fm" else []):
                    pb = bank(2, 5)
                    for dt_ in range(8):
                        MM(PB[pb][0:M, 0:NB], Wb[:, dt_, c0:c0 + M], hTb[:, dt_, 0:NB], dt_ == 0, dt_ == 7,
                           r=["W1", hk_], w=[("ps", pb)])
                    if grp == "f":
                        sg = sgt[0]
                        ACT(sg[0:8, 0:NB], PB[pb][0:8, 0:NB], AF.Exp, bias=nbfc[l][0:8, :], scale=-1.0,
                            r=[("ps", pb), "nbf%d" % l], w=["sgt0"])
                        ACT(sg[0:8, 0:NB], sg[0:8, 0:NB], AF.Ln, bias=one_c[0:8, :], r=["sgt0", "oc"], w=["sgt0"])
                        TS_("dve", lfT[nm][0:8, bsl], sg[0:8, 0:NB], -1.0, None, ALU.mult, r=["sgt0"], w=["lfT" + nm])
                        continue
                    ev = evb[cn["e"] % 4]
                    ek = "evb%d" % (cn["e"] % 4)
                    cn["e"] += 1
                    j = (c0 % 512) // 128 if grp != "g" else (c0 - 1544) // 128
                    if grp == "q":
                        TS_("dve", ev[:, 0:NB], PB[pb][:, 0:NB], 0.125, None, ALU.mult, r=[("ps", pb)], w=[ek])
                        DMA(st["QT"][2 * j, 0:64, bsl], ev[0:64, 0:NB], r=[ek])
                        DMA(st["QT"][2 * j + 1, 0:64, bsl], ev[64:128, 0:NB], r=[ek])
                    elif grp == "k":
                        CP("act", ev[:, 0:NB], PB[pb][:, 0:NB], r=[("ps", pb)], w=[ek])
                        DMA(st["KT"][2 * j, :, bsl], ev[0:64, 0:NB], r=[ek])
                        DMA(st["KT"][2 * j + 1, :, bsl], ev[64:128, 0:NB], r=[ek])
                    else:
                        gi = cn["s"] % 2
                        cn["s"] += 1
                        sg = sgz[gi]
                        zc = zcb[gi]
                        sgk, zck = "sgz%d" % gi, "zcb%d" % gi
                        CP("dve", zc[:, 0:NB], PB[pb][:, 0:NB], r=[("ps", pb)], w=[zck])
                        ACT(sg[:, 0:NB], zc[:, 0:NB], AF.Exp, scale=-1.0, r=[zck], w=[sgk])
                        ACT(sg[:, 0:NB], sg[:, 0:NB], AF.Ln, bias=one_c, r=[sgk, "oc"], w=[sgk])
                        ACT(sg[:, 0:NB], sg[:, 0:NB], AF.Exp, scale=-1.0, r=[sgk], w=[sgk])
                        TT("dve", ev[:, 0:NB], zc[:, 0:NB], sg[:, 0:NB], ALU.mult, r=[zck, sgk], w=[ek])
                        DMA(st["GT"][j * 128:(j + 1) * 128, bsl], ev[:, 0:NB], r=[ek])
                for tt in (range(ntt) if part == "tm" else []):
                    t0 = b * NB + tt * 128
                    for which, c0 in (("k", 512), ("v", 1024)):
                        pb = bank(5, 8)
                        for dt_ in range(8):
                            MM(PB[pb][:, :], hTb[:, dt_, tt * 128:(tt + 1) * 128], Wb[:, dt_, c0:c0 + 512],
                               dt_ == 0, dt_ == 7, r=["W1", hk_], w=[("ps", pb)])
                        ef = evf[cn["f"] % 4]
                        efk = "evf%d" % (cn["f"] % 4)
                        cn["f"] += 1
                        if which == "k":
                            CP("dve", ef, PB[pb][:, :], r=[("ps", pb)], w=[efk])
                            DMA(st["kout"][l, t0:t0 + 128, :], ef, r=[efk])
                        else:
                            CP("act", ef, PB[pb][:, :], r=[("ps", pb)], w=[efk])
                            DMA(st["vout"][l, t0:t0 + 128, :], ef, r=[efk])
                            ev = evb[cn["e"] % 4]
                            ek = "evb%d" % (cn["e"] % 4)
                            cn["e"] += 1
                            CP("dve", ev, ef, r=[efk], w=[ek])
                            DMA(st["Vb"][t0:t0 + 128, :], ev, r=[ek])

            norm_a(0)
            TRs(0)
            for ji in range(len(jobs)):
                if ji + 1 < len(jobs):
                    norm_a(ji + 1)
                projs(ji, "fm")
                if ji + 1 < len(jobs):
                    TRs(ji + 1)
                projs(ji, "tm")
            A.release()
            return lfT

        def phase_B(l, lfT, negc):
            A.mark()
            lfulls = [A.alloc([PAST + TS], F32) for _ in range(NS)]
            for s_ in range(NS):
                DMA(lfulls[s_][0:8, 0:PAST], clf[l, s_].rearrange("t h -> h t"), w=["lfull%d" % s_], slow=True)
            for st in STREAMS:
                nm, T, past = st["name"], st["T"], st["past"]
                nseq = 1 if past == 0 else NS
                Tq = T // nseq
                L = past + Tq
                lfull = None
                cT = A.alloc([L], F32)
                CH = min(2048, Tq)
                hi = A.alloc([CH], BF16)
                hif = A.alloc([CH], F32)
                lo = A.alloc([CH], BF16)
                ntile = (L + 127) // 128
                tok = tokp if past == 0 else A.alloc([ntile * 8], F32)
                for s in range(nseq):
                    if past:
                        lfull = lfulls[s]
                        CP("act", lfull[0:8, past:L], lfT[nm][0:8, s * Tq:(s + 1) * Tq], r=["lfT" + nm],
                           w=["lfull%d" % s])
                        src, srck = lfull, "lfull%d" % s
                    else:
                        src, srck = lfT[nm], "lfT" + nm
                    SCAN(cT[0:8, :], src[0:8, 0:L], zero_c[0:8, :].to_broadcast([8, L]), 0.0, ALU.add, ALU.add,
                         r=[srck, "zc"], w=["cT"])
                    for c0 in range(0, Tq, CH):
                        sl = slice(past + c0, past + c0 + CH)
                        CP("dve", hi[0:8, :], cT[0:8, sl], r=["cT"], w=["chi"])
                        CP("dve", hif[0:8, :], hi[0:8, :], r=["chi"], w=["chif"])
                        TT("dve", lo[0:8, :], cT[0:8, sl], hif[0:8, :], ALU.subtract, r=["cT", "chif"], w=["clo"])
                        dsl = slice(s * Tq + c0, s * Tq + c0 + CH)
                        DMA(st["QT"][:, 64, dsl], hi[0:8, :], r=["chi"])
                        DMA(st["QT"][:, 65, dsl], lo[0:8, :], r=["clo"])
                    pb = bank(0, 2)
                    for tI in range(ntile):
                        n = min(128, L - tI * 128)
                        TR(PB[pb][0:n, tI * 8:(tI + 1) * 8], cT[0:8, tI * 128:tI * 128 + n], ident_f[0:8, 0:8],
                           r=["cT", "ident_f"], w=[("ps", pb)])
                    ng = negc[nm][s]
                    TS_("dve", ng[:, 0:ntile * 8], PB[pb][:, 0:ntile * 8], -1.0, None, ALU.mult,
                        r=[("ps", pb)], w=["negc%s%d" % (nm, s)])
                    pb = bank(0, 2)
                    ntq = (Tq + 127) // 128
                    for tI in range(ntq):
                        n = min(128, Tq - tI * 128)
                        TR(PB[pb][0:n, tI * 8:(tI + 1) * 8], src[0:8, past + tI * 128:past + tI * 128 + n],
                           ident_f[0:8, 0:8], r=[srck, "ident_f"], w=[("ps", pb)])
                    CP("act", tok[:, 0:ntq * 8], PB[pb][:, 0:ntq * 8], r=[("ps", pb)], w=["tok"])
                    if past == 0:
                        DMA(st["lfout"][l].rearrange("(a p) h -> p a h", p=128),
                            tok[:, 0:ntq * 8].rearrange("p (a h) -> p a h", h=8), r=["tok"], slow=True, lazy=True)
                    else:
                        DMA(st["lfout"][l, s * Tq:(s + 1) * Tq, :], tok[0:Tq, 0:8], r=["tok"], slow=True)
            A.release()

        def phase_A2(l):
            A.mark()
            C0 = 2056
            NC2 = 2048
            Wb = A.alloc([8, NC2], BF16)
            load_w(l, w_in[l], C0, C0 + NC2, Wb, "W2")
            hT = [A.alloc([8, 512], BF16) for _ in range(2)]
            hib = [A.alloc([4, 512], BF16) for _ in range(2)]
            Sst2 = [A.alloc([G, 8, GD], F32) for _ in range(2)]
            Sbf = [A.alloc([8, GD], BF16) for _ in range(2)]
            stl = A.alloc([G, GD], F32)
            ez = [A.alloc([512], F32) for _ in range(3)]
            zq = A.alloc([512], F32)
            zg = A.alloc([512], F32)
            sig = A.alloc([512], F32)
            lf = A.alloc([512], F32)
            bcs = A.alloc([512], F32)
            hkk = A.alloc([512], F32)
            eb = A.alloc([512], F32)
            enb = A.alloc([512], F32)
            t1 = A.alloc([512], F32)
            qt = [A.alloc([512], BF16) for _ in range(2)]
            kt = [A.alloc([512], BF16) for _ in range(2)]
            kh = [A.alloc([512], BF16) for _ in range(2)]
            dch = [A.alloc([8], F32) for _ in range(2)]
            gs = [A.alloc([512], F32) for _ in range(2)]
            khT = A.alloc([4, 128], BF16)
            Am = A.alloc([4, 128], BF16)
            sq = A.alloc([512], BF16)
            rst = A.alloc([512], F32)
            t2 = A.alloc([512], F32)
            yb = [A.alloc([512], BF16) for _ in range(2)]
            cmask = A.alloc([512], F32)
            MSET("pool", cmask, 1.0, w=["cmask"])
            MSET("pool", cmask.rearrange("p (c t) -> p c t", t=64)[:, :, 0:1], 0.0, w=["cmask"])
            cn = {"y": 0}
            lbk = VK(l)
            for st in STREAMS:
                T, NB, nm, past = st["T"], st["NB"], st["name"], st["past"]
                ntt = NB // 128
                nch = NB // 64
                nblk = st["nblk"]
                if past == 0:
                    MSET("pool", Sst2[0][:, :, 0, :], 0.0, w=["S%d_0_0" % g for g in range(G)])
                HTv = st["HT"].rearrange("(a p) t -> p a t", p=128)
                DMA(hT[0][:, :, 0:NB], HTv[:, :, 0:NB], w=["hT0"])

                def pre_block(b):
                    hk_ = "hT%d" % (b % 2)
                    hTb = hT[b % 2]
                    hibb = hib[b % 2]
                    hibk = "hib%d" % (b % 2)
                    if b + 1 < nblk:
                        DMA(hT[(b + 1) % 2][:, :, 0:NB], HTv[:, :, (b + 1) * NB:(b + 2) * NB], w=["hT%d" % ((b + 1) % 2)])
                    for tt in range(ntt):
                        pb = bank(0, 2)
                        for dt_ in range(8):
                            MM(PB[pb][:, :], hTb[:, dt_, tt * 128:(tt + 1) * 128], Wb[:, dt_, 1024:1536],
                               dt_ == 0, dt_ == 7, r=["W2", hk_], w=[("ps", pb)])
                        CP("act", hibb[:, tt, :], PB[pb][:, :], r=[("ps", pb)], w=[hibk])

                def sigm(pb_, e, ek):
                    ACT(e[:, 0:NB], PB[pb_][:, 0:NB], AF.Exp, scale=-1.0, r=[("ps", pb_)], w=[ek])
                    ACT(e[:, 0:NB], e[:, 0:NB], AF.Ln, bias=one_c, r=[ek, "oc"], w=[ek])
                    ACT(e[:, 0:NB], e[:, 0:NB], AF.Exp, scale=-1.0, r=[ek], w=[ek])

                def sigm_sb(z, zk, e, ek):
                    ACT(e[:, 0:NB], z[:, 0:NB], AF.Exp, scale=-1.0, r=[zk], w=[ek])
                    ACT(e[:, 0:NB], e[:, 0:NB], AF.Ln, bias=one_c, r=[ek, "oc"], w=[ek])
                    ACT(e[:, 0:NB], e[:, 0:NB], AF.Exp, scale=-1.0, r=[ek], w=[ek])

                def stageA(b, g):
                    i = g % 2
                    hk_ = "hT%d" % (b % 2)
                    hTb = hT[b % 2]
                    pbank = {"n": 0}

                    def proj(c0):
                        pb_ = (2, 3, 5)[(g * 3 + pbank["n"]) % 3]
                        pbank["n"] += 1
                        for dt_ in range(8):
                            MM(PB[pb_][:, 0:NB], Wb[:, dt_, c0:c0 + 128], hTb[:, dt_, 0:NB], dt_ == 0, dt_ == 7,
                               r=["W2", hk_], w=[("ps", pb_)])
                        return pb_
                    pf = proj(512 + g * 128)
                    pq = proj(0 + g * 128)
                    pg = proj(1536 + g * 128)
                    CP("dve", zq[:, 0:NB], PB[pq][:, 0:NB], r=[("ps", pq)], w=["zq"])
                    CP("dve", zg[:, 0:NB], PB[pg][:, 0:NB], r=[("ps", pg)], w=["zg"])
                    ACT(ez[0][:, 0:NB], PB[pf][:, 0:NB], AF.Exp, scale=-1.0, r=[("ps", pf)], w=["ez0"])
                    ACT(ez[1][:, 0:NB], zq[:, 0:NB], AF.Exp, scale=-1.0, r=["zq"], w=["ez1"])
                    ACT(ez[2][:, 0:NB], zg[:, 0:NB], AF.Exp, scale=-1.0, r=["zg"], w=["ez2"])
                    for j_ in range(3):
                        ACT(ez[j_][:, 0:NB], ez[j_][:, 0:NB], AF.Ln, bias=one_c, r=["ez%d" % j_, "oc"], w=["ez%d" % j_])
                    for j_ in range(3):
                        ACT(ez[j_][:, 0:NB], ez[j_][:, 0:NB], AF.Exp, scale=-1.0, r=["ez%d" % j_], w=["ez%d" % j_])
                    ACT(lf[:, 0:NB], ez[0][:, 0:NB], AF.Ln, bias=lbv[l][:, g:g + 1], scale=oml[l][:, g:g + 1],
                        r=["ez0"] + lbk, w=["lf"])
                    TT("dve", gs[i][:, 0:NB], zg[:, 0:NB], ez[2][:, 0:NB], ALU.mult, r=["zg", "ez2"],
                       w=["gs%d" % i])
                    TS_("dve", hkk[:, 0:NB], ez[0][:, 0:NB], noml[l][:, g:g + 1], oml[l][:, g:g + 1],
                        ALU.mult, ALU.add, r=["ez0"] + lbk, w=["hkk"])
                    SCAN(bcs[:, 0:NB], cmask[:, 0:NB], lf[:, 0:NB], 0.0, ALU.mult, ALU.add,
                         r=["lf", "cmask"], w=["bcs"])
                    ACT(eb[:, 0:NB], bcs[:, 0:NB], AF.Exp, r=["bcs"], w=["eb"])
                    ACT(enb[:, 0:NB], bcs[:, 0:NB], AF.Exp, scale=-1.0, r=["bcs"], w=["enb"])
                    blast = bcs[:, 0:NB].rearrange("p (c t) -> p c t", t=64)[:, :, 63]
                    ACT(dch[i][:, 0:nch], blast, AF.Exp, r=["bcs"], w=["dch%d" % i])
                    TT("dve", ez[1][:, 0:NB], ez[1][:, 0:NB], eb[:, 0:NB], ALU.mult, r=["ez1", "eb"], w=["ez1"])
                    TT("dve", qt[i][:, 0:NB], zq[:, 0:NB], ez[1][:, 0:NB], ALU.mult, r=["zq", "ez1"],
                       w=["qt%d" % i])
                    TT("dve", t1[:, 0:NB], hkk[:, 0:NB], enb[:, 0:NB], ALU.mult, r=["hkk", "enb"], w=["t1"])
                    CP("act", kt[i][:, 0:NB], t1[:, 0:NB], r=["t1"], w=["kt%d" % i])
                    TT("dve", kh[i][:, 0:NB].rearrange("p (c t) -> p c t", t=64),
                       t1[:, 0:NB].rearrange("p (c t) -> p c t", t=64),
                       dch[i][:, 0:nch].unsqueeze(2).to_broadcast([128, nch, 64]), ALU.mult,
                       r=["t1", "dch%d" % i], w=["kh%d" % i])

                def stageB(b, g):
                    i = g % 2
                    hibb = hib[b % 2]
                    hibk = "hib%d" % (b % 2)
                    bsl = slice(b * NB, (b + 1) * NB)
                    Sst = Sst2[b % 2]
                    Snx = Sst2[(b + 1) % 2]

                    def skey(par, slot):
                        return "S%d_%d_%d" % (g, par, slot)
                    qtk, ktk, khk, dk, gk_ = "qt%d" % i, "kt%d" % i, "kh%d" % i, "dch%d" % i, "gs%d" % i
                    pa = bank(0, 2)
                    for tt in range(ntt):
                        tsl = slice(tt * 128, (tt + 1) * 128)
                        MM(PB[pa][:, tsl], kt[i][:, tsl], qt[i][:, tsl], True, True, r=[ktk, qtk], w=[("ps", pa)],
                           skip=True)
                    TT("dve", Am[:, 0:ntt, :], PB[pa][:, 0:NB].rearrange("p (a b) -> p a b", b=128),
                       hmask.unsqueeze(1).to_broadcast([128, ntt, 128]), ALU.mult,
                       r=[("ps", pa), "hmask"], w=["Am"])
                    pt = bank(0, 2)
                    for tt in range(ntt):
                        tsl = slice(tt * 128, (tt + 1) * 128)
                        TR(PBh[pt][:, tsl], kh[i][:, tsl], ident_b, r=[khk, "ident_b"], w=[("ps", pt)])
                    CP("act", khT[:, 0:ntt, :], PBh[pt][:, 0:NB].rearrange("p (a b) -> p a b", b=128),
                       r=[("ps", pt)], w=["khT"])
                    for c in range(nch):
                        tt, half = c // 2, c % 2
                        rows = slice(half * 64, half * 64 + 64)
                        pd = 6 + half
                        MM(PB[pd][:, tt * 128:(tt + 1) * 128], khT[rows, tt, :], hibb[rows, tt, g * 128:(g + 1) * 128],
                           True, True, r=["khT", hibk], w=[("ps", pd)], skip=True)
                    for c in range(nch):
                        kin = skey(b % 2, c)
                        if past:
                            DMA(stl[:, :, :], stin[l, c].rearrange("g k v -> k g v"), w=["stl"])
                            CP("pool", Sst[:, g, c, :], stl[:, g, :], r=["stl"], w=[kin])
                        tt, half = c // 2, c % 2
                        pd = 6 + half
                        CP("act", Sbf[i][:, c, :], Sst[:, g, c, :], r=[kin], w=["Sbf%d_%d" % (i, c)])
                        if c < nch - 1:
                            Sout, kout = Sst[:, g, c + 1, :], skey(b % 2, c + 1)
                        else:
                            Sout, kout = Snx[:, g, 0, :], skey((b + 1) % 2, 0)
                        STT(Sout, Sst[:, g, c, :], dch[i][:, c:c + 1], PB[pd][:, tt * 128:(tt + 1) * 128],
                            ALU.mult, ALU.add, r=[kin, dk, ("ps", pd)], w=[kout])
                        if past:
                            DMA(sso[l, c, g], Sout, r=[kout])
                    if past == 0 and b == nblk - 1:
                        DMA(spo[l, g], Snx[:, g, 0, :], r=[skey((b + 1) % 2, 0)])
                    po = 4
                    for c in range(nch):
                        csl = slice(c * 64, (c + 1) * 64)
                        MM(PB[po][:, csl], Sbf[i][:, c, :], qt[i][:, csl], c == 0, False,
                           r=["Sbf%d_%d" % (i, c), qtk], w=[("ps", po)], skip=True)
                    for tt in range(ntt):
                        tsl = slice(tt * 128, (tt + 1) * 128)
                        MM(PB[po][:, tsl], hibb[:, tt, g * 128:(g + 1) * 128], Am[:, tt, :], False, tt == ntt - 1,
                           r=[hibk, "Am"], w=[("ps", po)], skip=True)
                    ACT(sq[:, 0:NB], PB[po][:, 0:NB], AF.Square, r=[("ps", po)], w=["sq"])
                    pn = bank(0, 2)
                    MM(PB[pn][:, 0:NB], ones_b, sq[:, 0:NB], True, True, r=["ones_b", "sq"], w=[("ps", pn)])
                    ACT(rst[:, 0:NB], PB[pn][:, 0:NB], AF.Ln, bias=eps_c, scale=1.0 / GD, r=[("ps", pn), "epsc"],
                        w=["rst"])
                    ACT(rst[:, 0:NB], rst[:, 0:NB], AF.Exp, scale=-0.5, r=["rst"], w=["rst"])
                    TT("dve", t2[:, 0:NB], PB[po][:, 0:NB], rst[:, 0:NB], ALU.mult, r=[("ps", po), "rst"], w=["t2"])
                    y = yb[cn["y"] % 2]
                    yk = "yb%d" % (cn["y"] % 2)
                    cn["y"] += 1
                    STT(y[:, 0:NB], t2[:, 0:NB], gnv[l][:, g:g + 1], gs[i][:, 0:NB], ALU.mult, ALU.mult,
                        r=["t2", gk_] + lbk, w=[yk])
                    DMA(st["YT"][512 + g * 128:512 + (g + 1) * 128, bsl], y[:, 0:NB], r=[yk])

                jobs = [(b, g) for b in range(nblk) for g in range(G)]

                def emitA(k):
                    b, g = jobs[k]
                    if g == 0:
                        pre_block(b)
                    stageA(b, g)
                emitA(0)
                emitA(1)
                for k in range(len(jobs)):
                    stageB(*jobs[k])
                    if k + 2 < len(jobs):
                        emitA(k + 2)
            A.release()

        def attend_epilogue(po, Nq, Gt, gk, ydst, bufs):
            rc, rch, rcf, rcl, tn, yo, yk = bufs
            RECIP(rc[64:65, 0:Nq], PB[po][64:65, 0:Nq], r=[("ps", po)], w=["rc"])
            CP("dve", rch[64:65, 0:Nq], rc[64:65, 0:Nq], r=["rc"], w=["rch"])
            CP("dve", rcf[64:65, 0:Nq], rch[64:65, 0:Nq], r=["rch"], w=["rcf"])
            TT("dve", rcl[64:65, 0:Nq], rc[64:65, 0:Nq], rcf[64:65, 0:Nq], ALU.subtract, r=["rc", "rcf"], w=["rcl"])
            pbc = 7
            MM(PB[pbc][0:64, 0:Nq], ones_b[64:65, 0:64], rch[64:65, 0:Nq], True, False, r=["ones_b", "rch"],
               w=[("ps", pbc)])
            MM(PB[pbc][0:64, 0:Nq], ones_b[64:65, 0:64], rcl[64:65, 0:Nq], False, True, r=["ones_b", "rcl"],
               w=[("ps", pbc)])
            TT("dve", tn[0:64, 0:Nq], PB[po][0:64, 0:Nq], Gt, ALU.mult, r=[("ps", po), gk], w=["tn"])
            TT("dve", yo[0:64, 0:Nq], tn[0:64, 0:Nq], PB[pbc][0:64, 0:Nq], ALU.mult, r=["tn", ("ps", pbc)], w=[yk])
            DMA(ydst, yo[0:64, 0:Nq], r=[yk], w=["YTdram"])

        def phase_C_prompt(l, negc):
            st = P
            T = st["T"]
            NT = T // 128
            SBQ = 1024
            NSB = T // SBQ
            NDUM = cfg.get("ndum", 0)
            A.mark()
            Ka = [A.alloc([T], BF16) for _ in range(2)]
            Qa = [A.alloc([T], BF16) for _ in range(2)]
            Ga = [A.alloc([T], BF16) for _ in range(2)]
            Va = [A.alloc([NT, 128], BF16) for _ in range(2)]
            pT = [A.alloc([SBQ], BF16) for _ in range(4)]
            rc = A.alloc([SBQ], F32)
            rch = A.alloc([SBQ], BF16)
            rcf = A.alloc([SBQ], F32)
            rcl = A.alloc([SBQ], BF16)
            tn = A.alloc([SBQ], F32)
            yo = [A.alloc([SBQ], BF16) for _ in range(2)]
            for i in range(2):
                MSET("dve", Ka[i][64:128, :], 0.0, w=["Ka%d" % i])
                MSET("dve", Ka[i][64:66, :], 1.0, w=["Ka%d" % i])
                MSET("dve", Qa[i][64:128, :], 0.0, w=["Qa%d" % i])
                MSET("dve", Va[i][:, :, 64:128], 0.0, w=["Va%d" % i])
                MSET("dve", Va[i][:, :, 64:65], 1.0, w=["Va%d" % i])
            ng = negc["p"][0]
            cnt = {"s": 0, "p": 0, "o": 0, "y": 0}
            pend = []

            def loads(h):
                i = h % 2
                DMA(Ka[i][0:64, :], st["KT"][h], w=["Ka%d" % i], q=LQ)
                DMA(Qa[i][0:66, :], st["QT"][h], w=["Qa%d" % i], q=LQ)
                DMA(Ga[i][0:64, :], st["GT"][h * 64:(h + 1) * 64, :], w=["Ga%d" % i], q=LQ)
                DMA(Va[i][:, :, 0:64], st["Vb"].rearrange("(a p) e -> p a e", p=128)[:, :, h * 64:(h + 1) * 64],
                    w=["Va%d" % i], q="sp")

            loads(0)
            for h in range(H):
                i = h % 2
                if h + 1 < H:
                    loads(h + 1)
                K_, Q_, G_, V_ = Ka[i], Qa[i], Ga[i], Va[i]
                kk, qk, gk, vk = "Ka%d" % i, "Qa%d" % i, "Ga%d" % i, "Va%d" % i
                for I2 in range(NSB):
                    oi = 2 + cnt["o"] % 2
                    cnt["o"] += 1
                    O = PB2[oi]
                    ok = ("ps2", oi)
                    nJ = 8 * I2 + 8
                    q0 = I2 * SBQ

                    def qk_step(J):
                        n0 = max(0, J - 8 * I2) * 128
                        diag = J >= 8 * I2
                        si = cnt["s"] % 2
                        cnt["s"] += 1
                        S_ = PB2[si]
                        sk = ("ps2", si)
                        Kt = K_[:, J * 128:(J + 1) * 128]
                        for d in range(NDUM):
                            MM(S_[:, 0:512], Kt, K_[:, 0:512], True, True, r=[kk], w=[sk])
                        if n0 < 512:
                            MM(S_[:, n0:512], Kt, Q_[:, q0 + n0:q0 + 512], True, not diag, r=[kk, qk], w=[sk])
                            if diag:
                                MM(S_[:, n0:n0 + 128], ident_b, maskneg, False, True, r=["ident_b", "maskneg"], w=[sk])
                            MM(S_[:, 512:1024], Kt, Q_[:, q0 + 512:q0 + 1024], True, True, r=[kk, qk], w=[sk])
                        else:
                            MM(S_[:, n0:1024], Kt, Q_[:, q0 + n0:q0 + 1024], True, False, r=[kk, qk], w=[sk])
                            MM(S_[:, n0:n0 + 128], ident_b, maskneg, False, True, r=["ident_b", "maskneg"], w=[sk])
                        pi = cnt["p"] % 4
                        cnt["p"] += 1
                        pt_ = pT[pi]
                        ptk = "pT%d" % pi
                        ACT(pt_[:, n0:1024], S_[:, n0:1024], AF.Exp, bias=ng[:, J * 8 + h:J * 8 + h + 1],
                            r=[sk, "negcp0"], w=[ptk])
                        return (J, n0, pt_, ptk)

                    def pv_step(item):
                        J, n0, pt_, ptk = item
                        last = (J == nJ - 1)
                        if n0 < 512:
                            MM(O[:, n0:512], V_[:, J, :], pt_[:, n0:512], J == 0, last, r=[vk, ptk], w=[ok], skip=True)
                            MM(O[:, 512:1024], V_[:, J, :], pt_[:, 512:1024], J == 0, last, r=[vk, ptk], w=[ok],
                               skip=True)
                        else:
                            MM(O[:, n0:1024], V_[:, J, :], pt_[:, n0:1024], False, last, r=[vk, ptk], w=[ok],
                               skip=True)

                    items = []
                    for J in range(nJ):
                        items.append(qk_step(J))
                        if J >= 2:
                            pv_step(items[J - 2])
                        if J == 5 and pend:
                            pend.pop(0)()
                    for it in items[max(0, nJ - 2):]:
                        pv_step(it)
                    RECIP(rc[64:65, :], O[64:65, :], r=[ok], w=["rc"])
                    TT("dve", tn[0:64, :], O[0:64, :], G_[0:64, q0:q0 + SBQ], ALU.mult, r=[ok, gk], w=["tn"])

                    def stage2(h=h, q0=q0):
                        si = cnt["s"] % 2
                        cnt["s"] += 1
                        bc = PB2[si]
                        bk = ("ps2", si)
                        for hf in range(2):
                            hs = slice(hf * 512, (hf + 1) * 512)
                            MM(bc[0:64, hs], ones_f[64:65, 0:64], rc[64:65, hs], True, True, r=["ones_f", "rc"], w=[bk])
                        yb_ = yo[cnt["y"] % 2]
                        yk = "yo%d" % (cnt["y"] % 2)
                        cnt["y"] += 1
                        TT("dve", yb_[0:64, :], tn[0:64, :], bc[0:64, :], ALU.mult, r=["tn", bk], w=[yk])
                        DMA(st["YT"][h * 64:(h + 1) * 64, q0:q0 + SBQ], yb_[0:64, :], r=[yk])
                    pend.append(stage2)
            while pend:
                pend.pop(0)()
            A.release()

        def phase_C_sample(l, negc):
            st = Sm
            NTc = PAST // 128
            KcT = A.alloc([H, PAST], BF16)
            Vc = A.alloc([NTc, H, 65], BF16)
            CHT = 4
            stg = [A.alloc([CHT, 512], F32) for _ in range(2)]
            kbf = [A.alloc([CHT, 512], BF16) for _ in range(2)]
            Kn = A.alloc([H, 128], BF16)
            Qn = A.alloc([H, 128], BF16)
            Gn = A.alloc([H, 128], BF16)
            Vn = A.alloc([H, 65], BF16)
            pT = [A.alloc([64], BF16) for _ in range(3)]
            rc = A.alloc([512], F32)
            rch = A.alloc([512], BF16)
            rcf = A.alloc([512], F32)
            rcl = A.alloc([512], BF16)
            tn = A.alloc([512], F32)
            yo = [A.alloc([512], BF16) for _ in range(2)]
            MSET("pool", KcT[64:66, :, :], 1.0, w=["KcT"])
            MSET("pool", Vc[:, :, :, 64:65], 1.0, w=["Vc"])
            MSET("pool", Kn[64:66, :, :], 1.0, w=["Kn"])
            MSET("pool", Vn[:, :, 64:65], 1.0, w=["Vn"])
            for h in range(H):
                DMA(Kn[0:64, h, :], st["KT"][h], w=["Kn"])
                DMA(Qn[0:66, h, :], st["QT"][h], w=["Qn"])
                DMA(Gn[0:64, h, :], st["GT"][h * 64:(h + 1) * 64, :], w=["Gn"])
            yield
            pn = 0
            yn = 0
            sn = 0
            for s in range(NS):
                ng = negc["s"][s]
                ngk = "negcs%d" % s
                DMA(Vn[0:64, :, 0:64], st["Vb"][s * 64:(s + 1) * 64, :].rearrange("t (h e) -> t h e", h=H), w=["Vn"])
                for c in range(NTc // CHT):
                    for src, kind in ((ck, "k"), (cv, "v")):
                        sb = stg[sn % 2]
                        sk = "cstg%d" % (sn % 2)
                        kb = kbf[sn % 2]
                        kbk = "kbf%d" % (sn % 2)
                        sn += 1
                        DMA(sb, src[l, s, c * CHT * 128:(c + 1) * CHT * 128, :].rearrange("(a p) e -> p a e", p=128),
                            w=[sk])
                        if kind == "v":
                            CP("act" if c % 2 else "dve", Vc[:, c * CHT:(c + 1) * CHT, :, 0:64],
                               sb.rearrange("p a (h e) -> p a h e", h=H), r=[sk], w=["Vc"])
                            continue
                        CP("dve", kb, sb, r=[sk], w=[kbk])
                        for h in range(H):
                            pb = bank(0, 2)
                            for a in range(CHT):
                                TR(PBh[pb][0:64, a * 128:(a + 1) * 128], kb[:, a, h * 64:(h + 1) * 64], ident_b,
                                   r=[kbk, "ident_b"], w=[("ps", pb)])
                            CP("act" if h % 2 else "dve", KcT[0:64, h, c * CHT * 128:(c + 1) * CHT * 128],
                               PBh[pb][0:64, 0:CHT * 128], r=[("ps", pb)], w=["KcT"])
                yield
                for h in range(H):
                    po = 3 + (h % 2)
                    nJ = NTc + 1
                    qsl = slice(s * 64, (s + 1) * 64)

                    def qk_step(J):
                        nonlocal pn
                        ps_ = pn % 3
                        pn += 1
                        pt_ = pT[ps_]
                        ptk = "pTs%d" % ps_
                        if J < NTc:
                            MM(PB[ps_][:, 0:64], KcT[0:66, h, J * 128:(J + 1) * 128], Qn[0:66, h, qsl], True, True,
                               r=["KcT", "Qn"], w=[("ps", ps_)])
                            ACT(pt_[:, 0:64], PB[ps_][:, 0:64], AF.Exp, bias=ng[:, J * 8 + h:J * 8 + h + 1],
                                r=[("ps", ps_), ngk], w=[ptk])
                        else:
                            MM(PB[ps_][0:64, 0:64], Kn[0:66, h, qsl], Qn[0:66, h, qsl], True, False,
                               r=["Kn", "Qn"], w=[("ps", ps_)])
                            MM(PB[ps_][0:64, 0:64], ident_b[0:64, 0:64], maskneg[0:64, 0:64], False, True,
                               r=["ident_b", "maskneg"], w=[("ps", ps_)])
                            ACT(pt_[0:64, 0:64], PB[ps_][0:64, 0:64], AF.Exp, bias=ng[0:64, J * 8 + h:J * 8 + h + 1],
                                r=[("ps", ps_), ngk], w=[ptk])
                        return (J, pt_, ptk)

                    def pv_step(item):
                        J, pt_, ptk = item
                        if J < NTc:
                            MM(PB[po][0:65, 0:64], Vc[:, J, h, :], pt_[:, 0:64], J == 0, False, r=["Vc", ptk],
                               w=[("ps", po)], skip=True)
                        else:
                            MM(PB[po][0:65, 0:64], Vn[0:64, h, :], pt_[0:64, 0:64], False, True, r=["Vn", ptk],
                               w=[("ps", po)], skip=True)

                    prev = None
                    for J in range(nJ):
                        cur = qk_step(J)
                        if prev is not None:
                            pv_step(prev)
                        prev = cur
                        yield
                    pv_step(prev)
                    attend_epilogue(po, 64, Gn[0:64, h, qsl], "Gn", st["YT"][h * 64:(h + 1) * 64, qsl],
                                    (rc, rch, rcf, rcl, tn, yo[yn % 2], "yos%d" % (yn % 2)))
                    yn += 1
                    yield

        def phase_D_setup(l):
            ctx = {}
            ctx["Wo"] = A.alloc([8, D], BF16)
            load_w(l, w_out[l], 0, D, ctx["Wo"], "wout")
            ctx["yT"] = [A.alloc([8, 512], BF16) for _ in range(2)]
            ctx["xbuf"] = [A.alloc([D], F32) for _ in range(3)]
            ctx["xo"] = [A.alloc([D], F32) for _ in range(2)]
            ctx["junk"] = A.alloc([D], F32)
            ctx["ssb"] = A.alloc([8], F32)
            ctx["xi"] = 0
            ctx["yi"] = 0
            return ctx

        def phase_D_run(l, ctx, st):
            Wo, yT, xbuf, xo, junk, ssb = ctx["Wo"], ctx["yT"], ctx["xbuf"], ctx["xo"], ctx["junk"], ctx["ssb"]
            last = (l == 1)
            T, NB = st["T"], st["NB"]
            ntt = NB // 128
            xsrc = st["x0"] if l == 0 else st["x1"]
            YTv = st["YT"].rearrange("(a p) t -> p a t", p=128)
            ydep = ["YTdram"] if st is Sm else []
            y0 = ctx["yi"]
            DMA(yT[y0 % 2][:, :, 0:NB], YTv[:, :, 0:NB], r=ydep, w=["yT%d" % (y0 % 2)])
            for b in range(st["nblk"]):
                yi = ctx["yi"]
                ctx["yi"] += 1
                yk = "yT%d" % (yi % 2)
                yTb = yT[yi % 2]
                if b + 1 < st["nblk"]:
                    DMA(yT[(yi + 1) % 2][:, :, 0:NB], YTv[:, :, (b + 1) * NB:(b + 2) * NB], r=ydep,
                        w=["yT%d" % ((yi + 1) % 2)])
                for tt in range(ntt):
                    t0 = b * NB + tt * 128
                    xi = ctx["xi"]
                    ctx["xi"] += 1
                    xb = xbuf[xi % 3]
                    xk = "xbuf%d" % (xi % 3)
                    xob = xo[xi % 2]
                    xok = "xo%d" % (xi % 2)
                    sc = ssb[:, (xi % 2) * 4:(xi % 2) * 4 + 1]
                    sck = "ss%d" % (xi % 2)
                    DMA(xb, xsrc[t0:t0 + 128, :], w=[xk], q=LQ)
                    for half in range(2):
                        pb = bank(5, 7)
                        for et in range(8):
                            MM(PB[pb][:, :], yTb[:, et, tt * 128:(tt + 1) * 128],
                               Wo[:, et, half * 512:(half + 1) * 512], et == 0, et == 7,
                               r=[yk, "wout"], w=[("ps", pb)])
                        TT("dve", xob[:, half * 512:(half + 1) * 512], PB[pb][:, :],
                           xb[:, half * 512:(half + 1) * 512], ALU.add, r=[("ps", pb), xk], w=[xok])
                        yield
                    if not last:
                        DMA(st["x1"][t0:t0 + 128, :], xob, r=[xok])
                    else:
                        ACT(junk, xob, AF.Square, accum=sc, r=[xok], w=["junk", sck])
                        ACT(sc, sc, AF.Ln, bias=eps_c, scale=1.0 / D, r=[sck, "epsc"], w=[sck])
                        ACT(sc, sc, AF.Exp, scale=-0.5, r=[sck], w=[sck])
                        STT(xb, xob, sc, fgb, ALU.mult, ALU.mult, r=[xok, sck, "fgb"], w=[xk])
                        DMA(st["y"][t0:t0 + 128, :], xb, r=[xk])

        tokp = A.alloc([(SEQ // 128) * 8], F32)
        negc = {
            "p": [A.alloc([(SEQ // 128) * 8], F32)],
            "s": [A.alloc([((PAST + TS + 127) // 128) * 8], F32) for _ in range(NS)],
        }
        for l in range(2):
            lfT = phase_A1(l)
            S.barrier()
            phase_B(l, lfT, negc)
            A.release()
            S.barrier()
            phase_A2(l)
            S.barrier()
            phase_C_prompt(l, negc)
            S.barrier()
            A.mark()
            gC = phase_C_sample(l, negc)
            next(gC)
            dctx = phase_D_setup(l)
            gD = phase_D_run(l, dctx, P)
            c_alive, d_alive = True, True
            while c_alive or d_alive:
                for _ in range(2):
                    if c_alive:
                        try:
                            next(gC)
                        except StopIteration:
                            c_alive = False
                if d_alive:
                    try:
                        next(gD)
                    except StopIteration:
                        d_alive = False
            for _ in phase_D_run(l, dctx, Sm):
                pass
            A.release()
            S.barrier()
        S.emit()
    return nc


CFG_FULL = dict(SEQ=8192, PAST=2048)
_CONSTS = None


def _consts():
    ident = np.eye(128, dtype=np.float32)
    k = np.arange(128)[:, None]
    q = np.arange(128)[None, :]
    maskneg = np.where(k <= q, 0.0, NEG).astype(np.float32)
    hmask = ((k <= q) & ((k // 64) == (q // 64))).astype(np.float32)
    return ident, maskneg, hmask


def run(cfg, x_prompt, x_sample, cache_k, cache_v, cache_logf, state_hgrn,
        norm_g, w_in, fox_b_f, hg_lower, hg_norm_g, w_out, final_g, n_cores=8):
    SEQ, PAST = cfg["SEQ"], cfg["PAST"]
    nc = build(cfg)
    ident, maskneg, hmask = _consts()
    f = lambda a: np.ascontiguousarray(np.asarray(a, dtype=np.float32))
    in_maps = []
    for c in range(n_cores):
        sl = slice(2 * c, 2 * c + 2)
        in_maps.append({
            "xp": f(x_prompt[c]),
            "xs": f(x_sample[sl]).reshape(128, D),
            "ck": f(cache_k[:, sl]).reshape(2, 2, PAST, H * HD),
            "cv": f(cache_v[:, sl]).reshape(2, 2, PAST, H * HD),
            "clf": f(cache_logf[:, sl]),
            "st": f(state_hgrn[:, sl]),
            "norm_g": f(norm_g), "w_in": f(w_in), "fox_b_f": f(fox_b_f), "hg_lower": f(hg_lower),
            "hg_norm_g": f(hg_norm_g), "w_out": f(w_out), "final_g": f(final_g),
            "c_ident": ident, "c_maskneg": maskneg, "c_hmask": hmask,
        })
    res = run_bass_kernel_spmd(nc, in_maps, core_ids=list(range(n_cores)))
    R = res.results
    B = n_cores
    if cfg.get("debug"):
        global DEBUG_R
        DEBUG_R = R
    y_prompt = np.stack([R[c]["yp"] for c in range(B)])
    y_sample = np.concatenate([R[c]["ys"].reshape(2, 64, D) for c in range(B)])
    k_prompt = np.stack([R[c]["kp"].reshape(2, SEQ, H, HD) for c in range(B)], axis=1)
    v_prompt = np.stack([R[c]["vp"].reshape(2, SEQ, H, HD) for c in range(B)], axis=1)
    logf_prompt = np.stack([R[c]["lfp"] for c in range(B)], axis=1)
    hgrn_prompt = np.stack([R[c]["sp"] for c in range(B)], axis=1)
    k_sample = np.concatenate([R[c]["ks"].reshape(2, 2, 64, H, HD) for c in range(B)], axis=1)
    v_sample = np.concatenate([R[c]["vs"].reshape(2, 2, 64, H, HD) for c in range(B)], axis=1)
    logf_sample = np.concatenate([R[c]["lfs"].reshape(2, 2, 64, H) for c in range(B)], axis=1)
    hgrn_sample = np.concatenate([R[c]["ss"] for c in range(B)], axis=1)
    outs = (y_prompt, y_sample, k_prompt, v_prompt, logf_prompt, hgrn_prompt,
            k_sample, v_sample, logf_sample, hgrn_sample)
    return tuple(np.ascontiguousarray(o, dtype=np.float32) for o in outs)


def kernel(x_prompt, x_sample, cache_k, cache_v, cache_logf, state_hgrn,
           norm_g, w_in, fox_b_f, hg_lower, hg_norm_g, w_out, final_g):
    return run(CFG_FULL, x_prompt, x_sample, cache_k, cache_v, cache_logf, state_hgrn,
               norm_g, w_in, fox_b_f, hg_lower, hg_norm_g, w_out, final_g)
```
